# Optimizing a Trainium2 kernel written in Bass

```python
import math
import jax, jax.numpy as jnp
from jax import lax
import numpy as np

D_MODEL = 1024
BATCH = 8
SEQ = 4096
DEPTH = 2

N_A_LAYERS = DEPTH // 2
N_B_LAYERS = DEPTH - N_A_LAYERS
A_HEADS = 8
A_HEAD_DIM = 128
A_WIDTH = A_HEADS * A_HEAD_DIM
A_CONV = 4
A_CHUNK = 64
B_HEADS = 16
B_GROUPS = 2
B_HPG = B_HEADS // B_GROUPS
B_HEAD_DIM = 64
B_WIDTH = B_HEADS * B_HEAD_DIM
N_BRANCH = 3
L_CMP = 32
CMP_STRIDE = 16
CMP_HIDDEN = 256
L_SLC = 64
N_SEL = 16
WINDOW = 512
Q_BLOCK = 64
NUM_BUCKETS = 32
MAX_DISTANCE = 128

EPS = 1e-6
NEG_INF = -1e30
SEL_BOOST = 1e9

kernel_name = "yoco_deltanet_nsa_hybrid"


def rmsnorm(x, g):
    xf = x.astype(jnp.float32)
    y = xf * lax.rsqrt(jnp.mean(xf * xf, axis=-1, keepdims=True) + EPS)
    return (y * g.astype(jnp.float32)).astype(x.dtype)


def l2norm(x):
    xf = x.astype(jnp.float32)
    return xf * lax.rsqrt(jnp.sum(xf * xf, axis=-1, keepdims=True) + EPS)


def masked_softmax(logits, mask):
    logits = jnp.where(mask, logits.astype(jnp.float32), NEG_INF)
    m = jnp.max(logits, axis=-1, keepdims=True)
    e = jnp.exp(logits - m) * mask
    return e / jnp.maximum(jnp.sum(e, axis=-1, keepdims=True), 1e-30)


def ada_modulation(c, w, b):
    mod = jax.nn.silu(c) @ w + b
    shift, scale, gate = jnp.split(mod, 3, axis=-1)
    return shift[:, None], scale[:, None], gate[:, None]


def t5_bucket(dist):
    n = jnp.maximum(dist, 0)
    max_exact = NUM_BUCKETS // 2
    nf = jnp.maximum(n, 1).astype(jnp.float32)
    large = max_exact + (jnp.log(nf / max_exact) / math.log(MAX_DISTANCE / max_exact)
                         * (NUM_BUCKETS - max_exact)).astype(jnp.int32)
    large = jnp.minimum(large, NUM_BUCKETS - 1)
    return jnp.where(n < max_exact, n, large)


def causal_depthwise_conv(x, w):
    K = w.shape[0]
    T = x.shape[1]
    xp = jnp.pad(x, ((0, 0), (K - 1, 0), (0, 0)))
    y = xp[:, 0:T] * w[0]
    for k in range(1, K):
        y = y + xp[:, k:k + T] * w[k]
    return y


def chunk_gated_delta_rule(q, k, v, g, beta):
    Bn, T, H, dk = q.shape
    dv = v.shape[-1]
    C = A_CHUNK
    N = T // C

    def chunks(t):
        return t.reshape(Bn, N, C, H, -1).transpose(0, 3, 1, 2, 4)

    q, k, v = chunks(q), chunks(k), chunks(v)
    g = g.reshape(Bn, N, C, H).transpose(0, 3, 1, 2)
    beta = beta.reshape(Bn, N, C, H).transpose(0, 3, 1, 2)
    gc = jnp.cumsum(g, axis=-1)
    incl = np.tril(np.ones((C, C), dtype=bool))
    strict = np.tril(np.ones((C, C), dtype=bool), -1)
    diff = gc[..., :, None] - gc[..., None, :]
    decay = jnp.where(incl, jnp.exp(jnp.where(incl, diff, 0.0)), 0.0)
    kb = k * beta[..., None]
    m = jnp.where(strict, jnp.einsum('bhnid,bhnjd->bhnij', kb, k) * decay, 0.0)
    a_mat = m + jnp.eye(C, dtype=m.dtype)
    rhs = jnp.concatenate([v * beta[..., None], kb * jnp.exp(gc)[..., None]], axis=-1)
    uw = lax.linalg.triangular_solve(a_mat, rhs, left_side=True, lower=True, unit_diagonal=True)
    u, w = uw[..., :dv], uw[..., dv:]
    attn = jnp.einsum('bhnid,bhnjd->bhnij', q, k) * decay
    q_dec = q * jnp.exp(gc)[..., None]
    k_dec = k * jnp.exp(gc[..., -1:] - gc)[..., None]
    g_last = jnp.exp(gc[..., -1])

    def step(S, xs):
        q_n, u_n, w_n, a_n, kd_n, gl_n = xs
        v_new = u_n - jnp.einsum('bhck,bhkv->bhcv', w_n, S)
        o_n = jnp.einsum('bhck,bhkv->bhcv', q_n, S) + jnp.einsum('bhij,bhjv->bhiv', a_n, v_new)
        S = S * gl_n[..., None, None] + jnp.einsum('bhck,bhcv->bhkv', kd_n, v_new)
        return S, o_n

    xs = tuple(jnp.moveaxis(t, 2, 0) for t in (q_dec, u, w, attn, k_dec, g_last))
    S0 = jnp.zeros((Bn, H, dk, dv), jnp.float32)
    _, o = lax.scan(step, S0, xs)
    return o.transpose(1, 0, 3, 2, 4).reshape(Bn, T, H, dv)


def gated_deltanet_layer(h, in_w, conv_w, A_log, dt_bias, onorm_g, out_w):
    Bn, T, _ = h.shape
    proj = h @ in_w
    qkv = jax.nn.silu(causal_depthwise_conv(proj[..., :3 * A_WIDTH], conv_w))
    z = proj[..., 3 * A_WIDTH:4 * A_WIDTH].reshape(Bn, T, A_HEADS, A_HEAD_DIM)
    b = proj[..., 4 * A_WIDTH:4 * A_WIDTH + A_HEADS]
    a = proj[..., 4 * A_WIDTH + A_HEADS:]
    q, k, v = [t.reshape(Bn, T, A_HEADS, A_HEAD_DIM) for t in jnp.split(qkv, 3, axis=-1)]
    q = l2norm(q) * (A_HEAD_DIM ** -0.5)
    k = l2norm(k)
    beta = jax.nn.sigmoid(b.astype(jnp.float32))
    g = -jnp.exp(A_log.astype(jnp.float32)) * jax.nn.softplus(a.astype(jnp.float32) + dt_bias.astype(jnp.float32))
    o = chunk_gated_delta_rule(q, k, v.astype(jnp.float32), g, beta)
    o = rmsnorm(o, onorm_g) * jax.nn.silu(z.astype(jnp.float32))
    return o.astype(h.dtype).reshape(Bn, T, A_WIDTH) @ out_w


def nsa_shared_kv(stream, kv_norm_g, kv_w, cmp_pos_k, cmp_pos_v, cmp_k_w1, cmp_k_w2, cmp_v_w1, cmp_v_w2):
    Bn, T, _ = stream.shape
    s = rmsnorm(stream, kv_norm_g)
    kv = (s @ kv_w).reshape(Bn, T, 6, B_GROUPS, B_HEAD_DIM).transpose(2, 0, 3, 1, 4)
    n_cmp = (T - L_CMP) // CMP_STRIDE + 1
    idx = np.arange(n_cmp)[:, None] * CMP_STRIDE + np.arange(L_CMP)[None, :]

    def compress(tok, pos, w1, w2):
        blk = tok[:, :, idx] + pos
        return jax.nn.silu(blk.reshape(Bn, B_GROUPS, n_cmp, L_CMP * B_HEAD_DIM) @ w1) @ w2

    k_cmp = compress(kv[0], cmp_pos_k, cmp_k_w1, cmp_k_w2)
    v_cmp = compress(kv[1], cmp_pos_v, cmp_v_w1, cmp_v_w2)
    n_slc = T // L_SLC
    k_slc = kv[2].reshape(Bn, B_GROUPS, n_slc, L_SLC, B_HEAD_DIM)
    v_slc = kv[3].reshape(Bn, B_GROUPS, n_slc, L_SLC, B_HEAD_DIM)
    return k_cmp, v_cmp, k_slc, v_slc, kv[4], kv[5]


def nsa_layer(h, shared, rel_bias, in_w, out_w):
    k_cmp, v_cmp, k_slc, v_slc, k_win, v_win = shared
    Bn, T, _ = h.shape
    proj = h @ in_w
    z = proj[..., B_WIDTH:4 * B_WIDTH].reshape(Bn, T, N_BRANCH, B_HEADS, B_HEAD_DIM)
    gates = jax.nn.sigmoid(proj[..., 4 * B_WIDTH:].astype(jnp.float32)).reshape(Bn, T, N_BRANCH, B_HEADS)
    n_q = T // Q_BLOCK
    q = (proj[..., :B_WIDTH] * (B_HEAD_DIM ** -0.5)).reshape(
        Bn, n_q, Q_BLOCK, B_GROUPS, B_HPG, B_HEAD_DIM).transpose(1, 0, 3, 4, 2, 5)
    n_cmp = k_cmp.shape[2]
    n_slc = k_slc.shape[2]
    n_sel = min(N_SEL, n_slc)
    cmp_end = jnp.arange(n_cmp) * CMP_STRIDE + (L_CMP - 1)
    cells = np.arange(n_cmp)[:, None] + np.arange(L_CMP // CMP_STRIDE)[None, :]
    overlap = jnp.asarray((cells[:, None, :] // (L_SLC // CMP_STRIDE) == np.arange(n_slc)[None, :, None])
                          .sum(-1).astype(np.float32))
    rb_group = rel_bias.reshape(NUM_BUCKETS, B_GROUPS, B_HPG)
    k_win_p = jnp.pad(k_win, ((0, 0), (0, 0), (WINDOW, 0), (0, 0)))
    v_win_p = jnp.pad(v_win, ((0, 0), (0, 0), (WINDOW, 0), (0, 0)))
    b_idx = jnp.arange(Bn)[:, None, None, None]
    g_idx = jnp.arange(B_GROUPS)[None, :, None, None]
    blk_ids = jnp.arange(n_slc)

    def dense_bias(dist):
        return rel_bias[t5_bucket(dist)].transpose(2, 0, 1).reshape(B_GROUPS, B_HPG, *dist.shape)

    def attend_block(args):
        qb, qi = args
        t = qi * Q_BLOCK + jnp.arange(Q_BLOCK)
        d_c = t[:, None] - cmp_end[None, :]
        s_c = jnp.einsum('bghqd,bgcd->bghqc', qb, k_cmp).astype(jnp.float32) + dense_bias(d_c)
        p_c = masked_softmax(s_c, d_c >= 0)
        o_c = jnp.einsum('bghqc,bgcd->bghqd', p_c.astype(v_cmp.dtype), v_cmp)
        imp = jnp.einsum('bghqc,cs->bgqs', p_c, overlap)
        cur = (t // L_SLC)[:, None]
        forced = (blk_ids == 0) | (blk_ids == cur) | (blk_ids == cur - 1)
        imp = jnp.where(forced, SEL_BOOST, jnp.where(blk_ids > cur, -SEL_BOOST, imp))
        _, sel = lax.top_k(imp, n_sel)
        k_sel = k_slc[b_idx, g_idx, sel]
        v_sel = v_slc[b_idx, g_idx, sel]
        d_s = t[:, None, None] - (sel[..., None] * L_SLC + jnp.arange(L_SLC))
        bias_s = jnp.moveaxis(rb_group[t5_bucket(d_s), g_idx[..., None]], -1, 2)
        s_s = jnp.einsum('bghqd,bgqnld->bghqnl', qb, k_sel).astype(jnp.float32) + bias_s
        p_s = masked_softmax(s_s.reshape(Bn, B_GROUPS, B_HPG, Q_BLOCK, n_sel * L_SLC),
                             (d_s >= 0).reshape(Bn, B_GROUPS, 1, Q_BLOCK, n_sel * L_SLC))
        o_s = jnp.einsum('bghqk,bgqkd->bghqd', p_s.astype(v_sel.dtype),
                         v_sel.reshape(Bn, B_GROUPS, Q_BLOCK, n_sel * L_SLC, B_HEAD_DIM))
        q0 = qi * Q_BLOCK
        k_w = lax.dynamic_slice_in_dim(k_win_p, q0, WINDOW + Q_BLOCK, axis=2)
        v_w = lax.dynamic_slice_in_dim(v_win_p, q0, WINDOW + Q_BLOCK, axis=2)
        kpos = q0 - WINDOW + jnp.arange(WINDOW + Q_BLOCK)
        d_w = t[:, None] - kpos[None, :]
        mask_w = (d_w >= 0) & (d_w < WINDOW) & (kpos[None, :] >= 0)
        s_w = jnp.einsum('bghqd,bgkd->bghqk', qb, k_w).astype(jnp.float32) + dense_bias(d_w)
        p_w = masked_softmax(s_w, mask_w)
        o_w = jnp.einsum('bghqk,bgkd->bghqd', p_w.astype(v_w.dtype), v_w)
        return jnp.stack([o_c, o_s, o_w], axis=0)

    o = lax.map(attend_block, (q, jnp.arange(n_q)))
    o = o.transpose(2, 0, 5, 1, 3, 4, 6).reshape(Bn, T, N_BRANCH, B_HEADS, B_HEAD_DIM)
    y = jnp.sum(gates[..., None].astype(o.dtype) * o * jax.nn.silu(z), axis=2)
    return y.reshape(Bn, T, B_WIDTH) @ out_w


def setup_inputs(seed: int = 0) -> dict:
    key = jax.random.key(seed)
    ks = jax.random.split(key, 24)
    f32 = jnp.float32

    def nrm(k, shape, scale):
        return jax.random.normal(k, shape, f32) * scale

    a_cols = 4 * A_WIDTH + 2 * A_HEADS
    b_cols = 4 * B_WIDTH + N_BRANCH * B_HEADS
    dt = jnp.exp(jax.random.uniform(ks[9], (N_A_LAYERS, A_HEADS), f32, math.log(1e-3), math.log(1e-1)))
    return {
        "x": nrm(ks[0], (BATCH, SEQ, D_MODEL), 1.0),
        "c": nrm(ks[1], (BATCH, D_MODEL), 1.0),
        "rel_bias": nrm(ks[2], (NUM_BUCKETS, B_HEADS), 0.3),
        "ada_w": nrm(ks[3], (DEPTH, D_MODEL, 3 * D_MODEL), 0.5 * D_MODEL ** -0.5),
        "ada_b": nrm(ks[4], (DEPTH, 3 * D_MODEL), 0.02),
        "norm_g": 1.0 + nrm(ks[5], (DEPTH, D_MODEL), 0.02),
        "a_in_w": nrm(ks[6], (N_A_LAYERS, D_MODEL, a_cols), D_MODEL ** -0.5),
        "a_conv_w": nrm(ks[7], (N_A_LAYERS, A_CONV, 3 * A_WIDTH), A_CONV ** -0.5),
        "a_A_log": jnp.log(jax.random.uniform(ks[8], (N_A_LAYERS, A_HEADS), f32, 1.0, 16.0)),
        "a_dt_bias": dt + jnp.log(-jnp.expm1(-dt)),
        "a_onorm_g": 1.0 + nrm(ks[10], (N_A_LAYERS, A_HEAD_DIM), 0.02),
        "a_out_w": nrm(ks[11], (N_A_LAYERS, A_WIDTH, D_MODEL), A_WIDTH ** -0.5),
        "kv_norm_g": 1.0 + nrm(ks[12], (D_MODEL,), 0.02),
        "kv_w": nrm(ks[13], (D_MODEL, 6 * B_GROUPS * B_HEAD_DIM), D_MODEL ** -0.5),
        "cmp_pos_k": nrm(ks[14], (L_CMP, B_HEAD_DIM), 0.1),
        "cmp_pos_v": nrm(ks[15], (L_CMP, B_HEAD_DIM), 0.1),
        "cmp_k_w1": nrm(ks[16], (L_CMP * B_HEAD_DIM, CMP_HIDDEN), (L_CMP * B_HEAD_DIM) ** -0.5),
        "cmp_k_w2": nrm(ks[17], (CMP_HIDDEN, B_HEAD_DIM), CMP_HIDDEN ** -0.5),
        "cmp_v_w1": nrm(ks[18], (L_CMP * B_HEAD_DIM, CMP_HIDDEN), (L_CMP * B_HEAD_DIM) ** -0.5),
        "cmp_v_w2": nrm(ks[19], (CMP_HIDDEN, B_HEAD_DIM), CMP_HIDDEN ** -0.5),
        "b_in_w": nrm(ks[20], (N_B_LAYERS, D_MODEL, b_cols), D_MODEL ** -0.5),
        "b_out_w": nrm(ks[21], (N_B_LAYERS, B_WIDTH, D_MODEL), B_WIDTH ** -0.5),
        "final_g": 1.0 + nrm(ks[22], (D_MODEL,), 0.02),
    }


def reference(x, c, rel_bias, ada_w, ada_b, norm_g, a_in_w, a_conv_w, a_A_log, a_dt_bias, a_onorm_g, a_out_w,
              kv_norm_g, kv_w, cmp_pos_k, cmp_pos_v, cmp_k_w1, cmp_k_w2, cmp_v_w1, cmp_v_w2,
              b_in_w, b_out_w, final_g):
    shared = None
    for l in range(DEPTH):
        shift, scale, gate = ada_modulation(c, ada_w[l], ada_b[l])
        h = rmsnorm(x, norm_g[l]) * (1.0 + scale) + shift
        if l < N_A_LAYERS:
            out = gated_deltanet_layer(h, a_in_w[l], a_conv_w[l], a_A_log[l], a_dt_bias[l], a_onorm_g[l], a_out_w[l])
        else:
            j = l - N_A_LAYERS
            out = nsa_layer(h, shared, rel_bias, b_in_w[j], b_out_w[j])
        x = x + gate * out
        if l == N_A_LAYERS - 1:
            shared = nsa_shared_kv(x, kv_norm_g, kv_w, cmp_pos_k, cmp_pos_v, cmp_k_w1, cmp_k_w2, cmp_v_w1, cmp_v_w2)
    return rmsnorm(x, final_g)
```

```python
import math
from contextlib import ExitStack
import numpy as np
import concourse.bass as bass
import concourse.mybir as mybir
from concourse.bass_utils import run_bass_kernel_spmd

F32 = mybir.dt.float32
BF16 = mybir.dt.bfloat16
ALU = mybir.AluOpType
AF = mybir.ActivationFunctionType
AX = mybir.AxisListType

ENGS = ("sync", "scalar", "vector", "gpsimd", "tensor")
T = 4096
D = 1024
NT = 32
BIG = 30000.0
import os
NH = int(os.environ.get('MK_NH', '8'))
NTB = int(os.environ.get('MK_NTB', '32'))
STOP = int(os.environ.get('MK_STOP', '99'))
KIT = int(os.environ.get('MK_KIT', '5'))
VV = int(os.environ.get('MK_V', '3'))
NQT = int(os.environ.get('MK_NQT', '8'))
NSTOP = int(os.environ.get('MK_NSTOP', '99'))
NSB = int(os.environ.get('MK_NSB', '3'))
NBR = tuple(int(x) for x in os.environ.get('MK_NBR', '1,2').split(','))


class Tok:
    __slots__ = ("w", "r")

    def __init__(self):
        self.w = None
        self.r = []


class MK:
    NDMA = 20

    def __init__(self, nc):
        self.nc = nc
        self.q = {e: [] for e in ENGS}
        self.seen = {e: {} for e in ENGS}
        self.slots = {e: [0] * self.NDMA for e in ("sync", "scalar", "gpsimd")}
        self.rr = {e: 0 for e in ("sync", "scalar", "gpsimd")}
        self.signal = {e: set() for e in ENGS}

    def op(self, eng, fn, reads=(), writes=(), dma=False):
        q = self.q[eng]
        idx = len(q)
        waits = {}

        def need(ev):
            if ev is None:
                return
            key, val = ev
            if key[0] == "c" and key[1] == "tensor" and eng == "tensor":
                return
            if waits.get(key, -1) < val:
                waits[key] = val

        for t in reads:
            need(t.w)
        for t in writes:
            need(t.w)
            for ev in t.r:
                need(ev)
        if dma:
            rr = self.rr[eng]
            self.rr[eng] = (rr + 1) % self.NDMA
            prev = self.slots[eng][rr]
            if prev > 0:
                need((("d", eng, rr), prev))
            self.slots[eng][rr] = prev + 1
            ev = (("d", eng, rr), prev + 1)
        else:
            ev = (("c", eng), idx)
        seen = self.seen[eng]
        final = []
        for key, val in waits.items():
            if seen.get(key, -1) >= val:
                continue
            seen[key] = val
            final.append((key, val))
            if key[0] == "c":
                self.signal[key[1]].add(val)
        q.append((fn, final, ev, dma))
        for t in reads:
            t.r.append(ev)
            if len(t.r) > 64:
                t.r = t.r[-64:] if False else t.r
        for t in writes:
            t.w = ev
            t.r = []
        return ev

    def barrier(self):
        last = {}
        for e in ENGS:
            for i in range(len(self.q[e]) - 1, -1, -1):
                fn, _, ev, dma = self.q[e][i]
                if fn is not None and not dma:
                    last[e] = i
                    break
        for e in ENGS:
            waits = []
            seen = self.seen[e]
            for e2, ix in last.items():
                if e2 == e:
                    continue
                key = ("c", e2)
                if seen.get(key, -1) < ix:
                    seen[key] = ix
                    waits.append((key, ix))
                    self.signal[e2].add(ix)
            for e2 in ("sync", "scalar", "gpsimd"):
                for rr, cnt in enumerate(self.slots[e2]):
                    key = ("d", e2, rr)
                    if cnt > 0 and seen.get(key, -1) < cnt:
                        seen[key] = cnt
                        waits.append((key, cnt))
            self.q[e].append((None, waits, None, False))

    def finish(self, eng="sync"):
        waits = []
        for e in ("sync", "scalar", "gpsimd"):
            for rr, cnt in enumerate(self.slots[e]):
                if cnt > 0:
                    waits.append((("d", e, rr), cnt))
        self.q[eng].append((None, waits, None, False))

    def emit(self):
        nc = self.nc
        csem = {e: nc.alloc_semaphore(f"c_{e}") for e in ENGS}
        dsem = {e: [nc.alloc_semaphore(f"d_{e}_{i}") for i in range(self.NDMA)]
                for e in ("sync", "scalar", "gpsimd")}
        cval = {}
        for e in ENGS:
            s = sorted(self.signal[e])
            cval[e] = {ix: n + 1 for n, ix in enumerate(s)}

        def replay(e, engobj):
            sig = self.signal[e]
            for i, (fn, waits, ev, dma) in enumerate(self.q[e]):
                for key, val in waits:
                    if key[0] == "c":
                        engobj.wait_ge(csem[key[1]], cval[key[1]][val])
                    else:
                        engobj.wait_ge(dsem[key[1]][key[2]], 16 * val)
                if fn is None:
                    continue
                ins = fn(engobj)
                if dma:
                    ins.then_inc(dsem[e][ev[0][2]], 16)
                elif i in sig:
                    ins.then_inc(csem[e], 1)

        with nc.Block() as block:
            @block.sync
            def _(eng):
                replay("sync", eng)

            @block.scalar
            def _(eng):
                replay("scalar", eng)

            @block.vector
            def _(eng):
                replay("vector", eng)

            @block.gpsimd
            def _(eng):
                replay("gpsimd", eng)

            @block.tensor
            def _(eng):
                replay("tensor", eng)


class Buf:
    def __init__(self, t):
        self.t = t
        self.k = Tok()

    def __getitem__(self, key):
        return self.t[key]


class Rot:
    def __init__(self, bufs):
        self.bufs = bufs
        self.i = 0

    def next(self):
        b = self.bufs[self.i % len(self.bufs)]
        self.i += 1
        return b


class Bld:
    def __init__(self, dbg=()):
        self.nc = bass.Bass("TRN2", target_bir_lowering=False)
        self.mk = MK(self.nc)
        self.dbg = set(dbg)
        self.n = 0
        self.scopes = [ExitStack()]

    def sb(self, shape, dt=F32, name=None):
        self.n += 1
        return Buf(self.scopes[-1].enter_context(self.nc.sbuf_tensor(name or f"sb{self.n}", list(shape), dt)))

    def push(self):
        self.scopes.append(ExitStack())

    def pop(self):
        self.mk.barrier()
        self.scopes.pop().close()

    def rot(self, n, shape, dt=F32):
        return Rot([self.sb(shape, dt) for _ in range(n)])

    def din(self, name, shape, dt=F32):
        return Buf(self.nc.dram_tensor(name, list(shape), dt, kind="ExternalInput"))

    def dscr(self, name, shape, dt=F32, out=False):
        kind = "ExternalOutput" if (out or name in self.dbg) else "Internal"
        if ("REFIN:" + name) in self.dbg:
            kind = "ExternalInput"
        return Buf(self.nc.dram_tensor(name, list(shape), dt, kind=kind))

    def dma(self, out, in_, reads, writes, eng=None, slow=False):
        if eng is None:
            eng = "sync"
        if slow:
            fn = lambda e: e.dma_start(out=out, in_=in_, allow_slow_non_contiguous=True)
        else:
            fn = lambda e: e.dma_start(out=out, in_=in_)
        self.mk.op(eng, fn, reads=[b.k for b in reads], writes=[b.k for b in writes], dma=True)

    def dmac(self, out, in_, reads, writes):
        self.dma(out, in_, reads, writes, eng="gpsimd")

    def mm(self, out, lhsT, rhs, start, stop, reads, writes):
        self.mk.op("tensor", lambda e: e.matmul(out, lhsT=lhsT, rhs=rhs, start=start, stop=stop),
                   reads=[b.k for b in reads], writes=[b.k for b in writes])

    def tr(self, out, in_, ident, reads, writes):
        self.mk.op("tensor", lambda e: e.transpose(out, in_, ident),
                   reads=[b.k for b in reads], writes=[b.k for b in writes])

    def act(self, out, in_, func, reads, writes, bias=None, scale=None, accum=None):
        kw = {}
        if bias is not None:
            kw["bias"] = bias
        if scale is not None:
            kw["scale"] = scale
        if accum is not None:
            kw["accum_out"] = accum
        self.mk.op("scalar", lambda e: e.activation(out=out, in_=in_, func=func, **kw),
                   reads=[b.k for b in reads], writes=[b.k for b in writes])

    def ve(self, fn, reads, writes, eng="vector"):
        self.mk.op(eng, fn, reads=[b.k for b in reads], writes=[b.k for b in writes])

    def copy(self, out, in_, reads, writes, eng="vector"):
        self.ve(lambda e: e.tensor_copy(out=out, in_=in_), reads, writes, eng)

    def tt(self, out, a, b_, op, reads, writes, eng="vector"):
        self.ve(lambda e: e.tensor_tensor(out=out, in0=a, in1=b_, op=op), reads, writes, eng)

    def ts(self, out, a, s1, s2, op0, op1, reads, writes, eng="vector"):
        if op1 is None:
            self.ve(lambda e: e.tensor_scalar(out=out, in0=a, scalar1=s1, scalar2=None, op0=op0), reads, writes, eng)
        else:
            self.ve(lambda e: e.tensor_scalar(out=out, in0=a, scalar1=s1, scalar2=s2, op0=op0, op1=op1), reads, writes, eng)

    def stt(self, out, a, s, b_, op0, op1, reads, writes, eng="vector"):
        self.ve(lambda e: e.scalar_tensor_tensor(out=out, in0=a, scalar=s, in1=b_, op0=op0, op1=op1), reads, writes, eng)

    def memset(self, ap, val, writes, eng="gpsimd"):
        self.ve(lambda e: e.memset(ap, val), [], writes, eng)


def build(dbg=(), phases=("mod", "A", "B", "C", "KV", "Z", "NSA", "FIN")):
    b = Bld(dbg)
    nc = b.nc
    P = {}
    x_d = b.din("x", [T, D])
    cl_d = b.din("c_l", [128, 8])
    relb_d = b.din("rel_bias", [32, 16])
    adaw_d = b.din("ada_w", [2, D, 3 * D])
    adab_d = b.din("ada_b", [2, 3 * D])
    ng_d = b.din("norm_g_l", [128, 2, 8])
    ainw_d = b.din("a_in_w", [D, 4112])
    convw_d = b.din("conv_w_l", [128, 24, 4])
    alog_d = b.din("a_A_log", [1, 8])
    dtb_d = b.din("a_dt_bias", [1, 8])
    aong_d = b.din("a_onorm_g", [1, 128])
    aoutw_d = b.din("a_out_w", [D, D])
    cst_d = b.din("consts", [128, 6 * 128])
    out_d = b.dscr("out", [T, D], F32, out=True)

    ident = b.sb([128, 128]); identb = b.sb([128, 128], BF16)
    ones = b.sb([128, 128])
    cst = b.sb([128, 6 * 128])
    b.memset(ident[:], 1.0, [ident])
    b.ve(lambda e: e.affine_select(out=ident[:], in_=ident[:], pattern=[[-1, 128]], compare_op=ALU.is_equal,
                                   fill=0.0, base=0, channel_multiplier=1), [ident], [ident], "gpsimd")
    b.copy(identb[:], ident[:], [ident], [identb])
    b.memset(ones[:], 1.0, [ones])
    b.dma(cst[:], cst_d.t.ap(), [cst_d], [cst])
    U2 = cst[:, 0:128]; BONES = cst[:, 128:256]; SELA = cst[:, 256:384]; SELB = cst[:, 384:512]
    NMS = cst[:, 512:640]; NMT = cst[:, 640:768]

    pb = [Buf(nc.alloc_psum_tensor(f"pb{i}", [128, 512], F32)) for i in range(8)]

    gateB = [b.sb([128, D]) for _ in range(2)]
    modcol = [b.sb([128, 24]) for _ in range(2)]
    Acol = [b.sb([128, 8]) for _ in range(2)]
    ng = b.sb([128, 2, 8])
    if "mod" in phases:
        b.push()
        modB = b.sb([128, 3 * D])
        cs = b.sb([128, 8]); csb = b.sb([128, 8, 128])
        adabB = b.sb([128, 3 * D])
        awr = b.rot(2, [128, 3 * D])
        b.dma(cs[:], cl_d.t.ap(), [cl_d], [cs])
        b.dma(ng[:], ng_d.t.ap(), [ng_d], [ng])
        b.act(cs[:], cs[:], AF.Silu, [cs], [cs])
        for kc in range(8):
            b.copy(csb[:, kc, :], cs[:, kc:kc + 1].to_broadcast([128, 128]), [cs], [csb])
        for l in range(2):
            b.dma(adabB[:], adab_d.t.ap()[l:l + 1, :].to_broadcast([128, 3 * D]), [adab_d], [adabB])
            for kc in range(8):
                aw = awr.next()
                b.dma(aw[:], adaw_d.t.ap()[l, kc * 128:(kc + 1) * 128, :], [adaw_d], [aw])
                for n in range(6):
                    b.mm(pb[n][:], csb[:, kc, :], aw[:, n * 512:(n + 1) * 512], kc == 0, kc == 7, [csb, aw], [pb[n]])
            for n in range(6):
                b.tt(modB[:, n * 512:(n + 1) * 512], pb[n][:], adabB[:, n * 512:(n + 1) * 512], ALU.add,
                     [pb[n], adabB], [modB])
            for j in range(24):
                b.mm(pb[6][:, j:j + 1], modB[0:1, j * 128:(j + 1) * 128], ones[0:1, 0:1], True, True,
                     [modB, ones], [pb[6]])
            b.copy(modcol[l][:], pb[6][:, 0:24], [pb[6]], [modcol[l]])
            b.copy(gateB[l][:], modB[:, 2 * D:3 * D], [modB], [gateB[l]])
            b.stt(Acol[l][:], modcol[l][:, 8:16], 1.0, ng[:, l, :], ALU.add, ALU.mult, [modcol[l], ng], [Acol[l]])
            if l == 0 and "modB0" in b.dbg:
                dd = b.dscr("modB0", [128, 3 * D])
                b.dma(dd.t.ap(), modB[:], [modB], [dd])
                dd2 = b.dscr("modcol0", [128, 24])
                b.dma(dd2.t.ap(), modcol[0][:], [modcol[0]], [dd2])
        b.pop()

    QT = b.dscr("QT", [8, 128, T], BF16)
    KT = b.dscr("KT", [8, 128, T], BF16)
    VT = b.dscr("VT", [8, 128, T], BF16)
    ZS = b.dscr("ZS", [T, D], F32)
    GB = b.dscr("GB", [128, NT, 16], F32)

    g_all = b.sb([128, NT, 8]); beta_all = b.sb([128, NT, 8])

    def silu_from(dst, src, e_buf, reads, eng2="gpsimd"):
        b.act(e_buf[:], src, AF.Exp, reads, [e_buf], scale=-1.0)
        b.ts(e_buf[:], e_buf[:], 1.0, None, ALU.add, None, [e_buf], [e_buf])
        b.ve(lambda e: e.reciprocal(out=e_buf[:], in_=e_buf[:]), [e_buf], [e_buf])
        return e_buf

    if "A" in phases:
        b.push()
        Win = b.sb([128, 8, 4112], BF16)
        for kc in range(8):
            b.dmac(Win[:, kc, :], ainw_d.t.ap()[kc * 128:(kc + 1) * 128, :], [ainw_d], [Win])
        convw = b.sb([128, 24, 4])
        b.dma(convw[:], convw_d.t.ap(), [convw_d], [convw])
        alogB = b.sb([128, 8]); dtbB = b.sb([128, 8]); negA = b.sb([128, 8])
        b.dma(alogB[:], alog_d.t.ap().to_broadcast([128, 8]), [alog_d], [alogB])
        b.dma(dtbB[:], dtb_d.t.ap().to_broadcast([128, 8]), [dtb_d], [dtbB])
        b.act(negA[:], alogB[:], AF.Exp, [alogB], [negA])
        b.ts(negA[:], negA[:], -1.0, None, ALU.mult, None, [negA], [negA])
        pre = b.sb([128, 24, 515])
        b.memset(pre[:], 0.0, [pre])
        hT = b.rot(2, [128, 8, 512], BF16)
        xr = b.rot(2, [128, D]); xnr = b.rot(2, [128, D], BF16)
        st = b.rot(4, [128, 4])
        accr = b.rot(2, [128, 512]); er = b.rot(2, [128, 512]); sr = b.rot(2, [128, 512])
        sqr = b.rot(2, [128, 512]); rir = b.rot(2, [128, 512])
        obr = b.rot(3, [128, 512], BF16)
        zr = b.rot(2, [128, D]); ezr = b.rot(2, [128, 512]); bar = b.rot(2, [128, 16])
        for blk in range(8):
            h = hT.next()
            for tt_ in range(4):
                ti = blk * 4 + tt_
                xt = xr.next(); xn = xnr.next(); s = st.next()
                b.dma(xt[:], x_d.t.ap()[ti * 128:(ti + 1) * 128, :], [x_d], [xt])
                b.act(xn[:], xt[:], AF.Square, [xt], [xn, s], scale=1.0 / 32.0, accum=s[:, 0:1])
                b.act(s[:, 1:2], s[:, 0:1], AF.Ln, [s], [s], bias=1e-6)
                b.act(s[:, 2:3], s[:, 1:2], AF.Exp, [s], [s], scale=-0.5)
                b.ts(xn[:], xt[:], s[:, 2:3], None, ALU.mult, None, [xt, s], [xn])
                for kc in range(8):
                    pbt = pb[6 + (kc % 2)]
                    pv = pbt[:, 0:64].bitcast(BF16)
                    b.tr(pv, xn[:, kc * 128:(kc + 1) * 128], identb[:], [xn, identb], [pbt])
                    b.act(h[:, kc, tt_ * 128:(tt_ + 1) * 128], pv, AF.Identity, [pbt, Acol[0], modcol[0]], [h],
                          bias=modcol[0][:, kc:kc + 1], scale=Acol[0][:, kc:kc + 1])
            for oc in range(24):
                hh = oc % 8
                pbt = pb[oc % 4]
                for kc in range(8):
                    b.mm(pbt[:], Win[:, kc, oc * 128:(oc + 1) * 128], h[:, kc, :], kc == 0, kc == 7, [Win, h], [pbt])
                b.act(pre[:, oc, 3:515], pbt[:], AF.Copy, [pbt], [pre])
                acc = accr.next()
                b.ts(acc[:], pre[:, oc, 0:512], convw[:, oc, 0:1], None, ALU.mult, None, [pre, convw], [acc])
                for k in range(1, 4):
                    b.stt(acc[:], pre[:, oc, k:k + 512], convw[:, oc, k:k + 1], acc[:], ALU.mult, ALU.add,
                          [pre, convw, acc], [acc])
                b.copy(pre[:, oc, 0:3], pre[:, oc, 512:515], [pre], [pre], "gpsimd")
                e_ = silu_from(None, acc[:], er.next(), [acc])
                sv = sr.next()
                b.tt(sv[:], acc[:], e_[:], ALU.mult, [acc, e_], [sv], "gpsimd")
                ob = obr.next()
                if oc < 16:
                    sq = sqr.next(); ri = rir.next()
                    b.tt(sq[:], sv[:], sv[:], ALU.mult, [sv], [sq], "gpsimd")
                    pb2 = pb[4 + (oc % 2)]
                    b.mm(pb2[:], ones[:], sq[:], True, True, [ones, sq], [pb2])
                    b.act(ri[:], pb2[:], AF.Ln, [pb2], [ri], bias=1e-6)
                    b.act(ri[:], ri[:], AF.Exp, [ri], [ri], scale=-0.5)
                    if oc < 8:
                        b.stt(ob[:], sv[:], 128.0 ** -0.5, ri[:], ALU.mult, ALU.mult, [sv, ri], [ob])
                    else:
                        b.tt(ob[:], sv[:], ri[:], ALU.mult, [sv, ri], [ob])
                    dst = QT if oc < 8 else KT
                else:
                    b.copy(ob[:], sv[:], [sv], [ob], "gpsimd")
                    dst = VT
                b.dma(dst.t.ap()[hh, :, blk * 512:(blk + 1) * 512], ob[:], [ob], [dst], "gpsimd")
            for tt_ in range(4):
                ti = blk * 4 + tt_
                z = zr.next(); ba = bar.next()
                for n in range(2):
                    pbt = pb[4 + n]
                    for kc in range(8):
                        b.mm(pbt[:], h[:, kc, tt_ * 128:(tt_ + 1) * 128], Win[:, kc, 3072 + n * 512:3072 + (n + 1) * 512],
                             kc == 0, kc == 7, [h, Win], [pbt])
                    e_ = silu_from(None, pbt[:], ezr.next(), [pbt])
                    b.tt(z[:, n * 512:(n + 1) * 512], pbt[:], e_[:], ALU.mult, [pbt, e_], [z])
                pbt = pb[6]
                for kc in range(8):
                    b.mm(pbt[:, 0:16], h[:, kc, tt_ * 128:(tt_ + 1) * 128], Win[:, kc, 4096:4112], kc == 0, kc == 7, [h, Win], [pbt])
                b.copy(ba[:], pbt[:, 0:16], [pbt], [ba])
                b.dma(ZS.t.ap()[ti * 128:(ti + 1) * 128, :], z[:], [z], [ZS], "gpsimd")
                b.copy(beta_all[:, ti, :], ba[:, 0:8], [ba], [beta_all])
                b.tt(g_all[:, ti, :], ba[:, 8:16], dtbB[:], ALU.add, [ba, dtbB], [g_all])
        bf_ = beta_all[:].rearrange("p t h -> p (t h)")
        b.act(bf_, bf_, AF.Exp, [beta_all], [beta_all], scale=-1.0)
        b.ts(bf_, bf_, 1.0, None, ALU.add, None, [beta_all], [beta_all])
        b.ve(lambda e: e.reciprocal(out=bf_, in_=bf_), [beta_all], [beta_all])
        gf = g_all[:].rearrange("p t h -> p (t h)")
        b.act(gf, gf, AF.Exp, [g_all], [g_all])
        b.act(gf, gf, AF.Ln, [g_all], [g_all], bias=1.0)
        b.tt(g_all[:], g_all[:], negA[:].unsqueeze(1).to_broadcast([128, NT, 8]), ALU.mult, [g_all, negA], [g_all])
        if "GB" in b.dbg:
            b.dma(GB.t.ap()[:, :, 0:8], g_all[:], [g_all], [GB])
            b.dma(GB.t.ap()[:, :, 8:16], beta_all[:], [beta_all], [GB])
        b.pop()

    if "A" not in phases:
        b.push()
        b.memset(g_all[:], -0.05, [g_all]); b.memset(beta_all[:], 0.5, [beta_all])
        zt = b.sb([128, T], BF16); zt2 = b.sb([128, D])
        b.memset(zt[:], 0.01, [zt]); b.memset(zt2[:], 0.5, [zt2])
        for h_ in range(8):
            for dd_ in (QT, KT, VT):
                b.dma(dd_.t.ap()[h_], zt[:], [zt], [dd_])
        for ti in range(NT):
            b.dma(ZS.t.ap()[ti * 128:(ti + 1) * 128, :], zt2[:], [zt2], [ZS])
        b.pop()

    YT = b.dscr("YT", [8, 128, T], BF16)
    OD = b.dscr("OD", [8, T, 128], F32)
    if "B" in phases:
        b.push()
        sc = {n: b.sb([128, NT * 8]) for n in ("gc", "ngc", "gam", "bG", "kap", "nbeta", "GlA", "GlB")}
        gflat = g_all[:].rearrange("p t h -> p (t h)")
        bflat = beta_all[:].rearrange("p t h -> p (t h)")
        for i_, (lh, nm) in enumerate(((U2, "gc"), (BONES, "kap"), (SELA, "GlA"), (SELB, "GlB"))):
            b.mm(pb[i_][:, 0:256], lh, gflat, True, True, [cst, g_all], [pb[i_]])
            b.copy(sc[nm][:], pb[i_][:, 0:256], [pb[i_]], [sc[nm]])
        b.tt(sc["kap"][:], sc["kap"][:], sc["gc"][:], ALU.subtract, [sc["kap"], sc["gc"]], [sc["kap"]])
        b.act(sc["kap"][:], sc["kap"][:], AF.Exp, [sc["kap"]], [sc["kap"]])
        b.act(sc["GlA"][:], sc["GlA"][:], AF.Exp, [sc["GlA"]], [sc["GlA"]])
        b.act(sc["GlB"][:], sc["GlB"][:], AF.Exp, [sc["GlB"]], [sc["GlB"]])
        b.act(sc["gam"][:], sc["gc"][:], AF.Exp, [sc["gc"]], [sc["gam"]])
        b.ts(sc["ngc"][:], sc["gc"][:], -1.0, None, ALU.mult, None, [sc["gc"]], [sc["ngc"]])
        b.ts(sc["nbeta"][:], bflat, -1.0, None, ALU.mult, None, [beta_all], [sc["nbeta"]])
        b.tt(sc["bG"][:], bflat, sc["gam"][:], ALU.mult, [beta_all, sc["gam"]], [sc["bG"]])
        ongB = b.sb([128, 128])
        b.dma(ongB[:], aong_d.t.ap().to_broadcast([128, 128]), [aong_d], [ongB])
        class Reg:
            def __init__(self, bank, lo, hi, bf=False, shared=False):
                self.ap = bank.t[:, lo:hi].bitcast(BF16) if bf else bank.t[:, lo:hi]
                self.k = bank.k if shared else Tok()

        def mkctx(hp):
            bb = 4 * hp
            c = {}
            X1_, X2_, Y1_, Y2_ = pb[bb], pb[bb + 1], pb[bb + 2], pb[bb + 3]
            c["RD"] = Reg(X1_, 0, 128, False, True); c["RT"] = Reg(X1_, 128, 256, False, True)
            c["Nt"] = Reg(X1_, 256, 384, False, True); c["N2"] = Reg(X1_, 384, 512, False, True)
            c["Nt2"] = Reg(X2_, 0, 128, False, True); c["kt"] = Reg(X2_, 128, 192, True, True); c["vt"] = Reg(X2_, 192, 256, True, True)
            c["yt"] = Reg(X2_, 256, 320, True, True); c["U"] = Reg(X2_, 320, 448, False, True)
            c["KK"] = Reg(Y1_, 0, 128); c["QK"] = Reg(Y1_, 128, 256); c["A"] = Reg(Y1_, 256, 384); c["P1"] = Reg(Y1_, 384, 512)
            c["O1"] = Reg(Y2_, 0, 128); c["O2"] = Reg(Y2_, 128, 256); c["S"] = Reg(Y2_, 256, 384); c["W"] = Reg(Y2_, 384, 512)
            c["qT"] = b.sb([128, T], BF16); c["kT"] = b.sb([128, T], BF16); c["vT"] = b.sb([128, T], BF16)
            c["zs"] = b.sb([128, NT, 128]); c["yT"] = b.sb([128, T], BF16)
            c["Sf"] = b.sb([128, 128]); c["Sb"] = b.sb([128, 128], BF16)
            for nm in ("dg", "D", "DT", "u", "tmp", "o", "gz", "jk"):
                c[nm] = b.rot(2, [128, 128])
            for nm in ("N", "Ntb", "Acc"):
                c[nm] = b.rot(3, [128, 128])
            for nm in ("TT", "att", "kbg", "kde", "vb", "wT", "vn", "y"):
                c[nm] = b.rot(2, [128, 128], BF16)
            c["s4"] = b.rot(4, [128, 4])
            return c

        def head_gen(c, h):
            qT, kT, vT, zs, yT, S, Sb = c["qT"], c["kT"], c["vT"], c["zs"], c["yT"], c["Sf"], c["Sb"]
            b.dma(qT[:], QT.t.ap()[h], [QT], [qT]); b.dma(kT[:], KT.t.ap()[h], [KT], [kT]); b.dma(vT[:], VT.t.ap()[h], [VT], [vT])
            for q4 in range(4):
                b.dma(zs[:, q4 * 8:(q4 + 1) * 8, :],
                      ZS.t.ap()[q4 * 1024:(q4 + 1) * 1024, h * 128:(h + 1) * 128].rearrange("(t p) v -> p t v", p=128), [ZS], [zs])
            b.memset(S[:], 0.0, [S]); b.memset(Sb[:], 0.0, [Sb])
            yield
            for t in range(NTB):
                cols = slice(t * 128, (t + 1) * 128)
                th = slice(t * 8 + h, t * 8 + h + 1)
                KK, QK, RD, RT = c["KK"], c["QK"], c["RD"], c["RT"]
                b.mm(KK.ap, kT[:, cols], kT[:, cols], True, True, [kT], [KK])
                b.mm(QK.ap, kT[:, cols], qT[:, cols], True, True, [kT, qT], [QK])
                dg = c["dg"].next()
                b.ts(dg[:], ident[:], sc["gc"][:, th], None, ALU.mult, None, [ident, sc["gc"]], [dg])
                yield
                b.mm(RD.ap, ones[:], dg[:], True, False, [ones, dg], [RD])
                b.mm(RD.ap, ident[:], NMS, False, True, [ident, cst], [RD])
                b.mm(RT.ap, ones[:], dg[:], True, False, [ones, dg], [RT])
                b.mm(RT.ap, ident[:], NMT, False, True, [ident, cst], [RT])
                yield
                Dm = c["D"].next(); DTm = c["DT"].next()
                b.act(Dm[:], RD.ap, AF.Exp, [RD, sc["gc"]], [Dm], bias=sc["gc"][:, th], scale=-1.0)
                b.act(DTm[:], RT.ap, AF.Exp, [RT, sc["ngc"]], [DTm], bias=sc["ngc"][:, th], scale=1.0)
                yield
                N = c["N"].next()
                b.stt(N[:], KK.ap, sc["nbeta"][:, th], Dm[:], ALU.mult, ALU.mult, [KK, sc["nbeta"], Dm], [N])
                att = c["att"].next()
                b.tt(att[:], QK.ap, DTm[:], ALU.mult, [QK, DTm], [att])
                yield
                pNt, pA, pN2, pNt2 = c["Nt"], c["A"], c["N2"], c["Nt2"]
                b.mm(pNt.ap, N[:], ident[:], True, True, [N, ident], [pNt])
                Nt = c["Ntb"].next(); Acc = c["Acc"].next()
                yield
                b.act(Nt[:], pNt.ap, AF.Copy, [pNt], [Nt])
                yield
                b.tt(Acc[:], Nt[:], ident[:], ALU.add, [Nt, ident], [Acc])
                TTb = c["TT"].next()
                for k in range(1, 6):
                    N2 = c["N"].next()
                    b.mm(pN2.ap, Nt[:], N[:], True, True, [Nt, N], [pN2])
                    if k < 5:
                        Nt2 = c["Ntb"].next()
                        b.mm(pNt2.ap, N[:], Nt[:], True, True, [N, Nt], [pNt2])
                    yield
                    b.act(N2[:], pN2.ap, AF.Copy, [pN2], [N2])
                    if k < 5:
                        b.act(Nt2[:], pNt2.ap, AF.Copy, [pNt2], [Nt2])
                    yield
                    b.mm(pA.ap, N2[:], Acc[:], True, True, [N2, Acc], [pA])
                    yield
                    if k < 5:
                        Acc2 = c["Acc"].next()
                        b.tt(Acc2[:], pA.ap, Acc[:], ALU.add, [pA, Acc], [Acc2])
                        Acc = Acc2; Nt = Nt2
                    else:
                        b.tt(TTb[:], pA.ap, Acc[:], ALU.add, [pA, Acc], [TTb])
                    N = N2
                    yield
                pkt, pvt, pyt = c["kt"], c["vt"], c["yt"]
                b.tr(pkt.ap, kT[:, cols], identb[:], [kT, identb], [pkt])
                b.tr(pvt.ap, vT[:, cols], identb[:], [vT, identb], [pvt])
                yield
                kbg = c["kbg"].next(); kde = c["kde"].next(); vb = c["vb"].next()
                b.act(kbg[:], pkt.ap, AF.Identity, [pkt, sc["bG"]], [kbg], scale=sc["bG"][:, th])
                b.act(kde[:], pkt.ap, AF.Identity, [pkt, sc["kap"]], [kde], scale=sc["kap"][:, th])
                b.act(vb[:], pvt.ap, AF.Identity, [pvt, beta_all], [vb], scale=bflat[:, th])
                yield
                pU, pW = c["U"], c["W"]
                u = c["u"].next(); wT = c["wT"].next()
                b.mm(pU.ap, TTb[:], vb[:], True, True, [TTb, vb], [pU])
                b.mm(pW.ap, kbg[:], TTb[:], True, True, [kbg, TTb], [pW])
                yield
                b.act(u[:], pU.ap, AF.Copy, [pU], [u])
                b.copy(wT[:], pW.ap, [pW], [wT])
                yield
                vn = c["vn"].next(); o = c["o"].next()
                pP1, pO1, pO2, pS = c["P1"], c["O1"], c["O2"], c["S"]
                for hf in range(2):
                    rows = slice(hf * 64, hf * 64 + 64)
                    b.mm(pP1.ap, wT[:], Sb[:], True, True, [wT, Sb], [pP1])
                    b.mm(pO1.ap, qT[:, cols], Sb[:], True, True, [qT, Sb], [pO1])
                    yield
                    b.tt(vn[rows, :], u[rows, :], pP1.ap[rows, :], ALU.subtract, [u, pP1], [vn])
                    yield
                    b.mm(pS.ap, kde[rows, :], vn[rows, :], True, True, [kde, vn], [pS])
                    b.mm(pO2.ap, att[rows, :], vn[rows, :], True, True, [att, vn], [pO2])
                    yield
                    gl = sc["GlA"] if hf == 0 else sc["GlB"]
                    b.stt(S[:], S[:], gl[:, th], pS.ap, ALU.mult, ALU.add, [S, gl, pS], [S])
                    yield
                    b.act(Sb[:], S[:], AF.Copy, [S], [Sb])
                    tmp = c["tmp"].next()
                    b.copy(tmp[rows, :], pO2.ap[rows, :], [pO2], [tmp])
                    b.stt(o[rows, :], pO1.ap[rows, :], sc["gam"][rows, th], tmp[rows, :], ALU.mult, ALU.add,
                          [pO1, sc["gam"], tmp], [o])
                    yield
                if "OD" in b.dbg:
                    b.dma(OD.t.ap()[h, t * 128:(t + 1) * 128, :], o[:], [o], [OD], "gpsimd")
                s = c["s4"].next(); jk = c["jk"].next(); gz = c["gz"].next(); y = c["y"].next()
                b.act(jk[:], o[:], AF.Square, [o], [jk, s], scale=128.0 ** -0.5, accum=s[:, 0:1])
                b.tt(gz[:], zs[:, t, :], ongB[:], ALU.mult, [zs, ongB], [gz], "gpsimd")
                yield
                b.act(s[:, 1:2], s[:, 0:1], AF.Ln, [s], [s], bias=1e-6)
                yield
                b.act(s[:, 2:3], s[:, 1:2], AF.Exp, [s], [s], scale=-0.5)
                yield
                b.stt(y[:], o[:], s[:, 2:3], gz[:], ALU.mult, ALU.mult, [o, s, gz], [y])
                yield
                b.tr(pyt.ap, y[:], identb[:], [y, identb], [pyt])
                yield
                b.act(yT[:, cols], pyt.ap, AF.Copy, [pyt], [yT])
                yield
            b.dma(YT.t.ap()[h], yT[:], [yT], [YT], "gpsimd")

        ctxs = [mkctx(0), mkctx(1)]
        b.mk.barrier()
        for hp in range(0, NH, 2):
            gens = [head_gen(ctxs[i], hp + i) for i in range(2) if hp + i < NH]
            while gens:
                for g_ in list(gens):
                    try:
                        next(g_)
                    except StopIteration:
                        gens.remove(g_)
        b.pop()

    X1 = b.dscr("X1", [T, D], F32)
    X2 = b.dscr("X2", [T, D], F32)
    ST = b.dscr("ST", [8, 128, T], BF16)
    H1T = b.dscr("H1T", [8, 128, T], BF16)
    kvg_d = b.din("kvg_l", [128, 8])
    if "C" in phases:
        b.push()
        Wo = b.sb([128, 8, D], BF16)
        for kc in range(8):
            b.dmac(Wo[:, kc, :], aoutw_d.t.ap()[kc * 128:(kc + 1) * 128, :], [aoutw_d], [Wo])
        kvg = b.sb([128, 8])
        b.dma(kvg[:], kvg_d.t.ap(), [kvg_d], [kvg])
        ytl = b.rot(2, [128, 8, 128], BF16); xr = b.rot(2, [128, D]); x1r = b.rot(2, [128, D])
        xnr = b.rot(2, [128, D], BF16); st = b.rot(4, [128, 4]); jr2 = b.rot(1, [128, D], BF16)
        sTt = b.rot(2, [128, 8, 128], BF16); hTt = b.rot(2, [128, 8, 128], BF16)
        for ti in range(NT):
            rows = slice(ti * 128, (ti + 1) * 128)
            yt = ytl.next(); xt = xr.next(); x1 = x1r.next(); xn = xnr.next(); s = st.next()
            b.dma(yt[:], YT.t.ap()[:, :, rows].rearrange("h v t -> v h t"), [YT], [yt])
            b.dma(xt[:], x_d.t.ap()[rows, :], [x_d], [xt])
            for n in range(2):
                for kc in range(8):
                    b.mm(pb[n][:], yt[:, kc, :], Wo[:, kc, n * 512:(n + 1) * 512], kc == 0, kc == 7, [yt, Wo], [pb[n]])
                b.tt(x1[:, n * 512:(n + 1) * 512], pb[n][:], gateB[0][:, n * 512:(n + 1) * 512], ALU.mult, [pb[n], gateB[0]], [x1])
            b.tt(x1[:], x1[:], xt[:], ALU.add, [x1, xt], [x1], "gpsimd")
            b.dma(X1.t.ap()[rows, :], x1[:], [x1], [X1], "gpsimd")
            jk = jr2.next()
            b.act(jk[:], x1[:], AF.Square, [x1], [jk, s], scale=1.0 / 32.0, accum=s[:, 0:1])
            b.act(s[:, 1:2], s[:, 0:1], AF.Ln, [s], [s], bias=1e-6)
            b.act(s[:, 2:3], s[:, 1:2], AF.Exp, [s], [s], scale=-0.5)
            b.ts(xn[:], x1[:], s[:, 2:3], None, ALU.mult, None, [x1, s], [xn])
            sT_ = sTt.next(); hT_ = hTt.next()
            for kc in range(8):
                pbt = pb[6 + (kc % 2)]
                pv = pbt[:, 0:64].bitcast(BF16)
                b.tr(pv, xn[:, kc * 128:(kc + 1) * 128], identb[:], [xn, identb], [pbt])
                b.act(hT_[:, kc, :], pv, AF.Identity, [pbt, Acol[1], modcol[1]], [hT_],
                      bias=modcol[1][:, kc:kc + 1], scale=Acol[1][:, kc:kc + 1])
                b.act(sT_[:, kc, :], pv, AF.Identity, [pbt, kvg], [sT_], scale=kvg[:, kc:kc + 1])
            b.dma(ST.t.ap()[:, :, rows].rearrange("k p t -> p k t"), sT_[:], [sT_], [ST], "gpsimd")
            b.dma(H1T.t.ap()[:, :, rows].rearrange("k p t -> p k t"), hT_[:], [hT_], [H1T], "gpsimd")
        b.pop()

    kvw_d = b.din("kv_w", [D, 768])
    posT_d = b.din("posT", [2, 64, 32])
    w1_d = [b.din("cmp_k_w1", [2048, 256]), b.din("cmp_v_w1", [2048, 256])]
    w2_d = [b.din("cmp_k_w2", [256, 64]), b.din("cmp_v_w2", [256, 64])]
    KVT = b.dscr("KVT", [6, 128, T], BF16)
    VTOK = b.dscr("VTOK", [2, T, 128], BF16)
    KCT = b.dscr("KCT", [2, 2, 64, 256], F32)
    VCT = b.dscr("VCT", [2, 2, 256, 64], F32)
    if "KV" in phases:
        b.push()
        kvw = b.sb([128, 8, 768], BF16)
        for kc in range(8):
            b.dmac(kvw[:, kc, :], kvw_d.t.ap()[kc * 128:(kc + 1) * 128, :], [kvw_d], [kvw])
        sTr = b.rot(2, [128, 8, 512], BF16); ocr = b.rot(3, [128, 512], BF16); otr = b.rot(3, [128, 128], BF16)
        for blk in range(8):
            cols = slice(blk * 512, (blk + 1) * 512)
            sTb = sTr.next()
            b.dma(sTb[:], ST.t.ap()[:, :, cols].rearrange("k p t -> p k t"), [ST], [sTb])
            for i in range(6):
                pbt = pb[i % 4]
                for kc in range(8):
                    b.mm(pbt[:], kvw[:, kc, i * 128:(i + 1) * 128], sTb[:, kc, :], kc == 0, kc == 7, [kvw, sTb], [pbt])
                oc_ = ocr.next()
                b.act(oc_[:], pbt[:], AF.Copy, [pbt], [oc_])
                b.dma(KVT.t.ap()[i, :, cols], oc_[:], [oc_], [KVT], "gpsimd")
            for j, i in enumerate((3, 5)):
                for tt_ in range(4):
                    pbt = pb[4 + (tt_ % 2)]
                    for kc in range(8):
                        b.mm(pbt[:, 0:128], sTb[:, kc, tt_ * 128:(tt_ + 1) * 128], kvw[:, kc, i * 128:(i + 1) * 128],
                             kc == 0, kc == 7, [sTb, kvw], [pbt])
                    ot_ = otr.next()
                    b.copy(ot_[:], pbt[:, 0:128], [pbt], [ot_])
                    r0 = blk * 512 + tt_ * 128
                    b.dma(VTOK.t.ap()[j, r0:r0 + 128, :], ot_[:], [ot_], [VTOK], "gpsimd")
        b.mk.barrier()
        w1 = b.sb([64, 32, 256], BF16); w2 = b.sb([128, 2, 64], BF16); posT = b.sb([64, 32], BF16)
        tokT = b.sb([64, T], BF16)
        pbias = b.sb([128, 4]); npbias = b.sb([128, 4])
        hid = [b.sb([128, 256], BF16) for _ in range(2)]
        er2 = b.rot(2, [128, 256]); zz2 = b.rot(2, [128, 256])
        kco = b.sb([64, 256]); vco = b.sb([128, 2, 64])
        for kind in range(2):
            for l4 in range(4):
                b.dmac(w1[:, l4 * 8:(l4 + 1) * 8, :],
                       w1_d[kind].t.ap()[l4 * 512:(l4 + 1) * 512, :].rearrange("(l d) h -> d l h", d=64), [w1_d[kind]], [w1])
            b.dmac(w2[:], w2_d[kind].t.ap().rearrange("(c p) d -> p c d", p=128), [w2_d[kind]], [w2])
            b.dmac(posT[:], posT_d.t.ap()[kind], [posT_d], [posT])
            for hc in range(2):
                for l in range(32):
                    b.mm(pb[6][:, hc:hc + 1], w1[:, l, hc * 128:(hc + 1) * 128], posT[:, l:l + 1], l == 0, l == 31, [w1, posT], [pb[6]])
            b.copy(pbias[:, 0:2], pb[6][:, 0:2], [pb[6]], [pbias])
            b.ts(npbias[:, 0:2], pbias[:, 0:2], -1.0, None, ALU.mult, None, [pbias], [npbias])
            for g in range(2):
                b.dma(tokT[:], KVT.t.ap()[kind, g * 64:(g + 1) * 64, :], [KVT], [tokT])
                b.memset(hid[0][:], 0.0, [hid[0]]); b.memset(hid[1][:], 0.0, [hid[1]])
                for hc in range(2):
                    pbt = pb[hc]
                    for l in range(32):
                        b.mm(pbt[:, 0:255], w1[:, l, hc * 128:(hc + 1) * 128], tokT[:, l:l + 16 * 254 + 1:16], l == 0, l == 31, [w1, tokT], [pbt])
                    e_ = er2.next(); zz = zz2.next()
                    b.act(e_[:, 0:255], pbt[:, 0:255], AF.Exp, [pbt, npbias], [e_], bias=npbias[:, hc:hc + 1], scale=-1.0)
                    b.act(zz[:, 0:255], pbt[:, 0:255], AF.Identity, [pbt, pbias], [zz], bias=pbias[:, hc:hc + 1])
                    b.ts(e_[:, 0:255], e_[:, 0:255], 1.0, None, ALU.add, None, [e_], [e_])
                    b.ve(lambda e, e_=e_: e.reciprocal(out=e_[:, 0:255], in_=e_[:, 0:255]), [e_], [e_])
                    b.tt(hid[hc][:, 0:255], zz[:, 0:255], e_[:, 0:255], ALU.mult, [zz, e_], [hid[hc]])
                for hc in range(2):
                    b.mm(pb[2][0:64, 0:256], w2[:, hc, :], hid[hc][:], hc == 0, hc == 1, [w2, hid[hc]], [pb[2]])
                b.copy(kco[:], pb[2][0:64, 0:256], [pb[2]], [kco])
                b.dma(KCT.t.ap()[kind, g], kco[:], [kco], [KCT], "gpsimd")
                for ct in range(2):
                    for hc in range(2):
                        b.mm(pb[3][:, ct * 64:(ct + 1) * 64], hid[hc][:, ct * 128:(ct + 1) * 128], w2[:, hc, :], hc == 0, hc == 1,
                             [hid[hc], w2], [pb[3]])
                b.copy(vco[:].rearrange("p c d -> p (c d)"), pb[3][:, 0:128], [pb[3]], [vco])
                b.dma(VCT.t.ap()[kind, g].rearrange("(c p) d -> p c d", p=128), vco[:], [vco], [VCT], "gpsimd")
        b.pop()

    binw_d = b.din("b_in_w", [D, 4144])
    boutw_d = b.din("b_out_w", [D, D])
    oh_d = b.din("oh_const", [33, 1536])
    ov_d = b.din("ov_const", [128, 2, 64])
    fmw_d = b.din("fmw_const", [128, 126])
    ZG = b.dscr("ZG", [T, 3072], F32)
    Y2 = b.dscr("Y2", [T, D], BF16)
    FD = b.dscr("FD", [16, 1536], F32)
    G1 = b.dscr("G1", [16 * 128 * 1537 + 4096], F32)
    G2 = b.dscr("G2", [16 * 24 * 1552 + 4096], F32)
    OCD = b.dscr("OCD", [3, T, D], F32)
    SELD = b.dscr("SELD", [2, T, 64], F32)
    IMPD = b.dscr("IMPD", [2, T, 64], F32)

    if "Z" in phases:
        b.push()
        Wz = b.sb([128, 8, 3120], BF16)
        for kc in range(8):
            b.dmac(Wz[:, kc, :], binw_d.t.ap()[kc * 128:(kc + 1) * 128, 1024:4144], [binw_d], [Wz])
        hbr = b.rot(2, [128, 8, 512], BF16)
        zgr = b.rot(2, [128, 3072]); er3 = b.rot(2, [128, 512]); zcr = b.rot(2, [128, 512]); sgr = b.rot(2, [128, 48])
        for blk in range(NQT):
            hb = hbr.next()
            b.dma(hb[:], H1T.t.ap()[:, :, blk * 512:(blk + 1) * 512].rearrange("k p t -> p k t"), [H1T], [hb])
            for tt_ in range(4):
                tsl = slice(tt_ * 128, (tt_ + 1) * 128)
                r0 = blk * 512 + tt_ * 128
                sg = sgr.next(); zg = zgr.next()
                for kc in range(8):
                    b.mm(pb[6][:, 0:48], hb[:, kc, tsl], Wz[:, kc, 3072:3120], kc == 0, kc == 7, [hb, Wz], [pb[6]])
                b.act(sg[:], pb[6][:, 0:48], AF.Exp, [pb[6]], [sg], scale=-1.0)
                b.ts(sg[:], sg[:], 1.0, None, ALU.add, None, [sg], [sg])
                b.ve(lambda e, sg=sg: e.reciprocal(out=sg[:], in_=sg[:]), [sg], [sg])
                for n in range(6):
                    pbt = pb[n % 4]
                    for kc in range(8):
                        b.mm(pbt[:], hb[:, kc, tsl], Wz[:, kc, n * 512:(n + 1) * 512], kc == 0, kc == 7, [hb, Wz], [pbt])
                    e_ = er3.next(); zc = zcr.next()
                    b.act(e_[:], pbt[:], AF.Exp, [pbt], [e_], scale=-1.0)
                    b.act(zc[:], pbt[:], AF.Copy, [pbt], [zc])
                    b.ts(e_[:], e_[:], 1.0, None, ALU.add, None, [e_], [e_])
                    b.ve(lambda e, e_=e_: e.reciprocal(out=e_[:], in_=e_[:]), [e_], [e_])
                    b.tt(zc[:], zc[:], e_[:], ALU.mult, [zc, e_], [zc], "gpsimd")
                    b.tt(zg[:, n * 512:(n + 1) * 512].rearrange("p (h d) -> p h d", d=64),
                         zc[:].rearrange("p (h d) -> p h d", d=64),
                         sg[:, n * 8:(n + 1) * 8].unsqueeze(2).to_broadcast([128, 8, 64]), ALU.mult, [zc, sg], [zg])
                b.dma(ZG.t.ap()[r0:r0 + 128, :], zg[:], [zg], [ZG], "gpsimd")
        b.pop()

    if "NSA" in phases:
        b.push()
        cfarB = b.sb([128, 16])
        b.push()
        rb = b.sb([33, 16]); ohs = b.sb([33, 1536]); fsb = b.sb([16, 1536])
        b.dma(rb[0:32, :], relb_d.t.ap(), [relb_d], [rb])
        b.memset(rb[32:33, :], -BIG, [rb])
        b.dma(ohs[:], oh_d.t.ap(), [oh_d], [ohs])
        b.dma(cfarB[:], relb_d.t.ap()[31:32, :].to_broadcast([128, 16]), [relb_d], [cfarB])
        for n in range(3):
            b.mm(pb[n][0:16, :], rb[:], ohs[:, n * 512:(n + 1) * 512], True, True, [rb, ohs], [pb[n]])
            b.copy(fsb[:, n * 512:(n + 1) * 512], pb[n][0:16, :], [pb[n]], [fsb])
        b.dma(FD.t.ap(), fsb[:], [fsb], [FD])
        b.pop()
        E64 = b.sb([128, T], BF16)
        b.memset(E64[:], 1.0, [E64])
        b.ve(lambda e: e.affine_select(out=E64[:], in_=E64[:], pattern=[[1, T]], compare_op=ALU.is_ge, fill=0.0,
                                       base=0, channel_multiplier=-64), [E64], [E64], "gpsimd")
        b.ve(lambda e: e.affine_select(out=E64[:], in_=E64[:], pattern=[[-1, T]], compare_op=ALU.is_ge, fill=0.0,
                                       base=63, channel_multiplier=64), [E64], [E64], "gpsimd")
        OVt = b.sb([128, 2, 64]); FMW = b.sb([128, 126]); zer = b.sb([128, 232])
        b.dma(OVt[:], ov_d.t.ap(), [ov_d], [OVt]); b.dma(FMW[:], fmw_d.t.ap(), [fmw_d], [FMW])
        b.memset(zer[:], 0.0, [zer])
        kslc = b.sb([128, T], BF16); kwin = b.sb([128, T], BF16)
        VS = b.sb([128, NT, 65], BF16); VW = b.sb([128, NT, 65], BF16)
        kcT = b.sb([128, 256], BF16); VC = b.sb([128, 2, 64], BF16)
        Wq = b.sb([128, 8, 512], BF16)
        EBS = b.sb([128, 8, 1024], BF16); EBW = b.sb([128, 8, 512], BF16); EBc = b.sb([128, 8, 504], BF16)
        hbr = b.rot(1, [128, 8, 512], BF16); qT = b.sb([128, 4, 512], BF16)
        Eer = b.rot(2, [128, 512]); Pbr = b.rot(3, [128, 512], BF16)
        OB = [b.sb([128, 4, 512]) for _ in range(3)]
        zgt = b.rot(2, [128, 3, 512]); yr2 = b.rot(2, [128, 512], BF16); t1r = b.rot(2, [128, 512]); t2r = b.rot(2, [128, 512])
        impP = b.rot(2, [128, 256]); Ecr = b.rot(2, [128, 256]); Pcr = b.rot(2, [128, 256]); Pnr = b.rot(2, [128, 256], BF16)
        PTr = b.rot(2, [128, 2, 128], BF16); rsr = b.rot(4, [128, 4]); impT = b.rot(2, [128, 2, 128])
        impf = b.rot(2, [128, 64]); m8r = b.rot(2, [128, 8]); m8br = b.rot(2, [128, 8]); tmpm = b.rot(2, [128, 64]); seln = b.rot(2, [128, 64])
        selT = b.sb([128, 512], BF16)
        b.memset(selT[:], 0.0, [selT])
        qTz = b.sb([128, 8, 512], BF16)
        b.memset(qTz[:], 0.0, [qTz])
        rs2 = b.rot(8, [128, 1])
        for g in range(2):
            for half in range(2):
                rws = slice(half * 64, half * 64 + 64)
                b.dma(kslc[rws, :], KVT.t.ap()[2, g * 64:(g + 1) * 64, :], [KVT], [kslc])
                b.dma(kwin[rws, :], KVT.t.ap()[4, g * 64:(g + 1) * 64, :], [KVT], [kwin])
                b.dmac(kcT[rws, :], KCT.t.ap()[0, g], [KCT], [kcT])
            for q4 in range(4):
                tsl = slice(q4 * 8, (q4 + 1) * 8)
                b.dma(VS[:, tsl, 0:64], VTOK.t.ap()[0, q4 * 1024:(q4 + 1) * 1024, g * 64:(g + 1) * 64].rearrange("(t p) d -> p t d", p=128), [VTOK], [VS])
                b.dma(VW[:, tsl, 0:64], VTOK.t.ap()[1, q4 * 1024:(q4 + 1) * 1024, g * 64:(g + 1) * 64].rearrange("(t p) d -> p t d", p=128), [VTOK], [VW])
            b.memset(VS[:, :, 64:65], 1.0, [VS]); b.memset(VW[:, :, 64:65], 1.0, [VW])
            b.dmac(VC[:], VCT.t.ap()[1, g].rearrange("(c p) d -> p c d", p=128), [VCT], [VC])
            for kc in range(8):
                b.dmac(Wq[:, kc, :], binw_d.t.ap()[kc * 128:(kc + 1) * 128, g * 512:(g + 1) * 512], [binw_d], [Wq])
            b.push()
            Frep = b.rot(1, [128, 1536]); tS = b.rot(1, [128, 1408]); tC = b.rot(2, [128, 128])
            b.memset(EBc[:], 0.0, [EBc])
            for hh in range(8):
                H = g * 8 + hh
                fr = Frep.next()
                b.dma(fr[:], FD.t.ap()[H:H + 1, :].to_broadcast([128, 1536]), [FD], [fr])
                b.dma(bass.AP(G1.t, H * 128 * 1537, [[1537, 128], [1, 1536]]), fr[:], [fr], [G1])
                b.dma(bass.AP(G2.t, H * 24 * 1552, [[1552, 24], [1, 1536]]), fr[0:24, :], [fr], [G2])
                ts_ = tS.next()
                b.dma(ts_[:], bass.AP(G1.t, H * 128 * 1537 + 128, [[1536, 128], [1, 1408]]), [G1], [ts_])
                b.act(EBS[:, hh, :], ts_[:, 0:1024], AF.Exp, [ts_], [EBS])
                b.act(EBW[:, hh, :], ts_[:, 896:1408], AF.Exp, [ts_], [EBW])
                b.ve(lambda e, hh=hh: e.affine_select(out=EBW[:, hh, :], in_=EBW[:, hh, :], pattern=[[-1, 512]], compare_op=ALU.is_ge,
                                                      fill=0.0, base=-1, channel_multiplier=1), [EBW], [EBW], "gpsimd")
                tc_ = tC.next()
                b.dma(tc_[0:24, :], bass.AP(G2.t, H * 24 * 1552 + 737, [[1536, 24], [1, 128]]), [G2], [tc_])
                b.mm(pb[6][:, 0:24], tc_[0:24, :], ident[0:24, 0:24], True, True, [tc_, ident], [pb[6]])
                b.act(EBc[:, hh, 232:256], pb[6][:, 0:24], AF.Exp, [pb[6]], [EBc])
                b.act(EBc[:, hh, 0:232], zer[:], AF.Exp, [zer, cfarB], [EBc], bias=cfarB[:, H:H + 1])
            b.pop()
            if "EBT" in b.dbg and g == 0:
                dd = b.dscr("EBT", [128, 8, 512], BF16); b.dma(dd.t.ap(), EBW[:], [EBW], [dd])
                dd = b.dscr("EBS_", [128, 8, 1024], BF16); b.dma(dd.t.ap(), EBS[:], [EBS], [dd])
                dd = b.dscr("EBC_", [128, 8, 504], BF16); b.dma(dd.t.ap(), EBc[:], [EBc], [dd])
            for QT in range(NQT if NSTOP > 1 else 0):
                hb = hbr.next()
                b.dma(hb[:], H1T.t.ap()[:, :, QT * 512:(QT + 1) * 512].rearrange("k p t -> p k t"), [H1T], [hb])
                for p in range(4):
                    pbq = pb[p % 2]
                    for kc in range(8):
                        b.mm(pbq[:], Wq[:, kc, p * 128:(p + 1) * 128], hb[:, kc, :], kc == 0, kc == 7, [Wq, hb], [pbq])
                    b.act(qT[:, p, :], pbq[:], AF.Copy, [pbq], [qT], scale=0.125)
                    b.act(qTz[0:64, 2 * p, :], pbq[0:64, :], AF.Copy, [pbq], [qTz], scale=0.125)
                    b.act(qTz[64:128, 2 * p + 1, :], pbq[64:128, :], AF.Copy, [pbq], [qTz], scale=0.125)
                for qs in range(4):
                    qt = QT * 4 + qs
                    qsl = slice(qs * 128, (qs + 1) * 128)
                    r0 = qt * 128
                    nct = 1 if 8 * qt + 7 < 128 else 2
                    ip = impP.next()
                    for hh in range(8):
                        p = hh // 2; rws = slice((hh % 2) * 64, (hh % 2) * 64 + 64)
                        pS = pb[6][:, 0:256]
                        b.mm(pS, qT[rws, p, qsl], kcT[rws, :], True, True, [qT, kcT], [pb[6]])
                        Ee = Ecr.next(); Pc = Pcr.next(); Pn = Pnr.next(); rs = rsr.next(); PT = PTr.next()
                        b.act(Ee[:], pS, AF.Exp, [pb[6]], [Ee])
                        j0 = 248 - 8 * qt
                        b.ve(lambda e, Pc=Pc, Ee=Ee, hh=hh, j0=j0, rs=rs: e.scalar_tensor_tensor(
                            out=Pc[:], in0=Ee[:], scalar=1.0, in1=EBc[:, hh, j0:j0 + 256], op0=ALU.mult, op1=ALU.mult,
                            accum_out=rs[:, 0:1]), [Ee, EBc], [Pc, rs])
                        b.ts(rs[:, 0:1], rs[:, 0:1], 1e-30, None, ALU.max, None, [rs], [rs])
                        b.ve(lambda e, rs=rs: e.reciprocal(out=rs[:, 1:2], in_=rs[:, 0:1]), [rs], [rs])
                        b.ts(Pn[:], Pc[:], rs[:, 1:2], None, ALU.mult, None, [Pc, rs], [Pn])
                        if hh == 0:
                            b.ts(ip[:], Pc[:], rs[:, 1:2], None, ALU.mult, None, [Pc, rs], [ip])
                        else:
                            b.stt(ip[:], Pc[:], rs[:, 1:2], ip[:], ALU.mult, ALU.add, [Pc, rs, ip], [ip])
                        pTv = pb[6][:, 256:384].bitcast(BF16)
                        for ct in range(nct):
                            b.tr(pTv[:, ct * 128:(ct + 1) * 128], Pn[:, ct * 128:(ct + 1) * 128], identb[:], [Pn, identb], [pb[6]])
                        b.act(PT[:, 0:nct, :].rearrange("p c q -> p (c q)"), pTv[:, 0:nct * 128], AF.Copy, [pb[6]], [PT])
                        for ct in range(nct):
                            b.mm(pb[7][:, 0:64], PT[:, ct, :], VC[:, ct, :], ct == 0, ct == nct - 1, [PT, VC], [pb[7]])
                        b.copy(OB[0][:, qs, hh * 64:(hh + 1) * 64], pb[7][:, 0:64], [pb[7]], [OB[0]])
                    if NSTOP <= 2:
                        continue
                    it = impT.next()
                    for ct in range(2):
                        b.mm(pb[7][:, 128 + ct * 128:256 + ct * 128], ip[:, ct * 128:(ct + 1) * 128], ident[:], True, True, [ip, ident], [pb[7]])
                    b.copy(it[:].rearrange("p c q -> p (c q)"), pb[7][:, 128:384], [pb[7]], [it])
                    for ct in range(2):
                        b.mm(pb[7][:, 64:128], it[:, ct, :], OVt[:, ct, :], ct == 0, ct == 1, [it, OVt], [pb[7]])
                    imf = impf.next(); m8 = m8r.next(); m8b = m8br.next(); tm = tmpm.next(); sn = seln.next()
                    if "IMPD" in b.dbg:
                        b.copy(tm[:], pb[7][:, 64:128], [pb[7]], [tm])
                        b.dma(IMPD.t.ap()[g, r0:r0 + 128, :], tm[:], [tm], [IMPD], "gpsimd")
                    b.tt(imf[:], pb[7][:, 64:128], FMW[:, 62 - 2 * qt:62 - 2 * qt + 64], ALU.add, [pb[7], FMW], [imf])
                    b.ts(imf[:, 0:1], imf[:, 0:1], 2.0e9, None, ALU.add, None, [imf], [imf])
                    b.ve(lambda e, m8=m8, imf=imf: e.max(out=m8[:], in_=imf[:]), [imf], [m8])
                    b.ve(lambda e, tm=tm, m8=m8, imf=imf: e.match_replace(out=tm[:], in_to_replace=m8[:], in_values=imf[:], imm_value=-3.0e9),
                         [m8, imf], [tm])
                    b.ve(lambda e, m8b=m8b, tm=tm: e.max(out=m8b[:], in_=tm[:]), [tm], [m8b])
                    b.ts(sn[:], imf[:], m8b[:, 7:8], -BIG, ALU.is_lt, ALU.mult, [imf, m8b], [sn])
                    if "SELD" in b.dbg:
                        b.dma(SELD.t.ap()[g, r0:r0 + 128, :], sn[:], [sn], [SELD], "gpsimd")
                    b.mm(pb[7][0:64, 384:512], sn[:], ident[:], True, True, [sn, ident], [pb[7]])
                    b.copy(selT[0:64, qsl], pb[7][0:64, 384:512], [pb[7]], [selT])
                items = []
                for hh in range(8 if NSTOP > 3 else 0):
                    for br in NBR:
                        kt_lo = 0 if br == 1 else max(0, 4 * QT - 4)
                        for kt in range(kt_lo, 4 * QT + 4):
                            items.append((hh, br, kt, kt == 4 * QT + 3))

                def issue_scores(i):
                    hh, br, kt, _ = items[i]
                    KT_ = kslc if br == 1 else kwin
                    pS = pb[i % 2]
                    ksl = slice(kt * 128, (kt + 1) * 128)
                    b.mm(pS[:], KT_[:, ksl], qTz[:, hh, :], True, br == 2, [KT_, qTz], [pS])
                    if br == 1:
                        b.mm(pS[:], E64[:, ksl], selT[:], False, True, [E64, selT], [pS])

                if items:
                    issue_scores(0)
                for i, (hh, br, kt, last) in enumerate(items):
                    H = g * 8 + hh
                    V_ = VS if br == 1 else VW
                    relq0 = 512 * QT - 128 * kt
                    pS = pb[i % 2]
                    Pb_ = Pbr.next()
                    if br == 1 and relq0 >= 256:
                        b.act(Pb_[:], pS[:], AF.Exp, [pS, cfarB], [Pb_], bias=cfarB[:, H:H + 1])
                    else:
                        Ee = Eer.next()
                        b.act(Ee[:], pS[:], AF.Exp, [pS], [Ee])
                        j0 = relq0 + 384
                        if br == 1 or j0 + 512 <= 896:
                            b.tt(Pb_[:], Ee[:], EBS[:, hh, j0:j0 + 512], ALU.mult, [Ee, EBS], [Pb_])
                        elif j0 == 896:
                            b.tt(Pb_[:], Ee[:], EBW[:, hh, :], ALU.mult, [Ee, EBW], [Pb_])
                        else:
                            n1 = 896 - j0
                            b.tt(Pb_[:, 0:n1], Ee[:, 0:n1], EBS[:, hh, j0:896], ALU.mult, [Ee, EBS], [Pb_])
                            b.tt(Pb_[:, n1:512], Ee[:, n1:512], EBW[:, hh, 0:512 - n1], ALU.mult, [Ee, EBW], [Pb_])
                    if i + 1 < len(items):
                        issue_scores(i + 1)
                    for qs in range(4 if NSB >= 2 else 0):
                        qt = 4 * QT + qs
                        lo = 0 if br == 1 else max(0, qt - 4)
                        if kt > qt or kt < lo:
                            continue
                        b.mm(pb[2 + qs][:, 0:65], Pb_[:, qs * 128:(qs + 1) * 128], V_[:, kt, :], kt == lo, kt == qt,
                             [Pb_, V_], [pb[2 + qs]])
                    if last:
                        for qs in range(4 if NSB >= 3 else 0):
                            r_ = rs2.next()
                            b.ve(lambda e, r_=r_, qs=qs: e.reciprocal(out=r_[:], in_=pb[2 + qs][:, 64:65]), [pb[2 + qs]], [r_])
                            b.ts(OB[br][:, qs, hh * 64:(hh + 1) * 64], pb[2 + qs][:, 0:64], r_[:, 0:1], None, ALU.mult, None,
                                 [pb[2 + qs], r_], [OB[br]])
                for qs in range(4 if NSTOP > 4 else 0):
                    r0 = (4 * QT + qs) * 128
                    zt = zgt.next(); t1 = t1r.next(); t2 = t2r.next(); y = yr2.next()
                    b.dma(zt[:], ZG.t.ap()[r0:r0 + 128, :].rearrange("p (br c) -> p br c", br=3)[:, :, g * 512:(g + 1) * 512], [ZG], [zt])
                    if "OCD" in b.dbg:
                        for br in range(3):
                            b.dma(OCD.t.ap()[br, r0:r0 + 128, g * 512:(g + 1) * 512], OB[br][:, qs, :], [OB[br]], [OCD], "gpsimd")
                    b.tt(t1[:], zt[:, 0, :], OB[0][:, qs, :], ALU.mult, [zt, OB[0]], [t1], "gpsimd")
                    b.tt(t2[:], zt[:, 1, :], OB[1][:, qs, :], ALU.mult, [zt, OB[1]], [t2])
                    b.tt(t1[:], t1[:], t2[:], ALU.add, [t1, t2], [t1], "gpsimd")
                    b.tt(t2[:], zt[:, 2, :], OB[2][:, qs, :], ALU.mult, [zt, OB[2]], [t2])
                    b.tt(y[:], t1[:], t2[:], ALU.add, [t1, t2], [y])
                    b.dma(Y2.t.ap()[r0:r0 + 128, g * 512:(g + 1) * 512], y[:], [y], [Y2], "gpsimd")
        b.pop()

    fg_d = b.din("final_g", [1, D])
    if "FIN2" in phases:
        b.push()
        Wo2 = b.sb([128, 8, D], BF16)
        for kc in range(8):
            b.dmac(Wo2[:, kc, :], boutw_d.t.ap()[kc * 128:(kc + 1) * 128, :], [boutw_d], [Wo2])
        fgB = b.sb([128, D])
        b.dma(fgB[:], fg_d.t.ap().to_broadcast([128, D]), [fg_d], [fgB])
        y2r = b.rot(2, [128, D], BF16); ytr2 = b.rot(2, [128, 8, 128], BF16); x1r2 = b.rot(2, [128, D]); x2r = b.rot(2, [128, D])
        otr2 = b.rot(2, [128, D]); st = b.rot(4, [128, 4]); jr4 = b.rot(1, [128, D], BF16)
        for ti in range(NQT * 4):
            rows = slice(ti * 128, (ti + 1) * 128)
            y2 = y2r.next(); yt = ytr2.next(); x1t = x1r2.next(); x2 = x2r.next(); ot = otr2.next(); s = st.next(); jk = jr4.next()
            b.dma(y2[:], Y2.t.ap()[rows, :], [Y2], [y2])
            b.dma(x1t[:], X1.t.ap()[rows, :], [X1], [x1t])
            for kc in range(8):
                pbt = pb[6 + (kc % 2)]
                pv = pbt[:, 0:64].bitcast(BF16)
                b.tr(pv, y2[:, kc * 128:(kc + 1) * 128], identb[:], [y2, identb], [pbt])
                b.act(yt[:, kc, :], pv, AF.Copy, [pbt], [yt])
            for n in range(2):
                for kc in range(8):
                    b.mm(pb[n][:], yt[:, kc, :], Wo2[:, kc, n * 512:(n + 1) * 512], kc == 0, kc == 7, [yt, Wo2], [pb[n]])
                b.tt(x2[:, n * 512:(n + 1) * 512], pb[n][:], gateB[1][:, n * 512:(n + 1) * 512], ALU.mult, [pb[n], gateB[1]], [x2])
            b.tt(x2[:], x2[:], x1t[:], ALU.add, [x2, x1t], [x2], "gpsimd")
            if "X2" in b.dbg:
                b.dma(X2.t.ap()[rows, :], x2[:], [x2], [X2], "gpsimd")
            b.act(jk[:], x2[:], AF.Square, [x2], [jk, s], scale=1.0 / 32.0, accum=s[:, 0:1])
            b.act(s[:, 1:2], s[:, 0:1], AF.Ln, [s], [s], bias=1e-6)
            b.act(s[:, 2:3], s[:, 1:2], AF.Exp, [s], [s], scale=-0.5)
            b.stt(ot[:], x2[:], s[:, 2:3], fgB[:], ALU.mult, ALU.mult, [x2, s, fgB], [ot])
            b.dma(out_d.t.ap()[rows, :], ot[:], [ot], [out_d], "gpsimd")
        b.pop()

    if "FIN" in phases:
        b.push()
        fgB = b.sb([128, D])
        b.dma(fgB[:], fg_d.t.ap().to_broadcast([128, D]), [fg_d], [fgB])
        src = X1 if "C" in phases else x_d
        xr = b.rot(2, [128, D]); orr2 = b.rot(2, [128, D]); st = b.rot(4, [128, 4]); jr3 = b.rot(1, [128, D], BF16)
        for ti in range(NT):
            rows = slice(ti * 128, (ti + 1) * 128)
            xt = xr.next(); ot = orr2.next(); s = st.next(); jk = jr3.next()
            b.dma(xt[:], src.t.ap()[rows, :], [src], [xt])
            b.act(jk[:], xt[:], AF.Square, [xt], [jk, s], scale=1.0 / 32.0, accum=s[:, 0:1])
            b.act(s[:, 1:2], s[:, 0:1], AF.Ln, [s], [s], bias=1e-6)
            b.act(s[:, 2:3], s[:, 1:2], AF.Exp, [s], [s], scale=-0.5)
            b.stt(ot[:], xt[:], s[:, 2:3], fgB[:], ALU.mult, ALU.mult, [xt, s, fgB], [ot])
            b.dma(out_d.t.ap()[rows, :], ot[:], [ot], [out_d], "gpsimd")
        b.pop()

    b.mk.finish("sync")
    b.mk.emit()
    return b


def t5_bucket_np(d):
    n = np.maximum(d, 0)
    nf = np.maximum(n, 1).astype(np.float32)
    large = 16 + (np.log(nf / np.float32(16.0)) / np.float32(math.log(8.0)) * np.float32(16.0)).astype(np.int32)
    large = np.minimum(large, 31)
    return np.where(n < 16, n, large)


def nsa_consts():
    dd = np.arange(1536) - 512
    bk = t5_bucket_np(dd)
    oh = np.zeros((33, 1536), np.float32)
    for m in range(1536):
        if dd[m] < 0:
            oh[32, m] = 1.0
        else:
            oh[bk[m], m] = 1.0
    n_cmp = 255
    cells = np.arange(n_cmp)[:, None] + np.arange(2)[None, :]
    ov = (cells[:, None, :] // 4 == np.arange(64)[None, :, None]).sum(-1).astype(np.float32)
    ovp = np.zeros((256, 64), np.float32); ovp[:255] = ov
    ovl = ovp.reshape(2, 128, 64).transpose(1, 0, 2).copy()
    qi = np.arange(128)[:, None]; j = np.arange(126)[None, :]
    sp = j - 62
    curp = (qi >= 64).astype(np.int64)
    fm = np.where((sp == curp) | (sp == curp - 1), 1.0e9, np.where(sp > curp, -1.0e9, 0.0)).astype(np.float32)
    return oh, ovl, fm


def make_consts():
    i = np.arange(128)
    same = (i[:, None] // 64) == (i[None, :] // 64)
    U2 = (same & (i[:, None] <= i[None, :])).astype(np.float32)
    BON = same.astype(np.float32)
    SELA = np.repeat((i < 64).astype(np.float32)[:, None], 128, 1)
    SELB = np.repeat((i >= 64).astype(np.float32)[:, None], 128, 1)
    NMS = np.where(same & (i[:, None] > i[None, :]), 0.0, BIG).astype(np.float32)
    NMT = np.where(same & (i[None, :] >= i[:, None]), 0.0, -BIG).astype(np.float32)
    return np.concatenate([U2, BON, SELA, SELB, NMS, NMT], axis=1)


def core_inputs(inp, bi):
    f = np.ascontiguousarray
    d = {
        "x": f(inp["x"][bi]),
        "c_l": f(inp["c"][bi].reshape(8, 128).T),
        "rel_bias": f(inp["rel_bias"]),
        "ada_w": f(inp["ada_w"]),
        "ada_b": f(inp["ada_b"]),
        "norm_g_l": f(inp["norm_g"].reshape(2, 8, 128).transpose(2, 0, 1)),
        "a_in_w": f(inp["a_in_w"][0]),
        "conv_w_l": f(inp["a_conv_w"][0].reshape(4, 24, 128).transpose(2, 1, 0)),
        "a_A_log": f(inp["a_A_log"]),
        "a_dt_bias": f(inp["a_dt_bias"]),
        "a_onorm_g": f(inp["a_onorm_g"]),
        "a_out_w": f(inp["a_out_w"][0]),
        "consts": make_consts(),
        "kvg_l": f(inp["kv_norm_g"].reshape(8, 128).T),
        "final_g": f(inp["final_g"].reshape(1, D)),
        "kv_w": f(inp["kv_w"]),
        "posT": f(np.stack([inp["cmp_pos_k"].T, inp["cmp_pos_v"].T], 0)),
        "cmp_k_w1": f(inp["cmp_k_w1"]), "cmp_v_w1": f(inp["cmp_v_w1"]),
        "cmp_k_w2": f(inp["cmp_k_w2"]), "cmp_v_w2": f(inp["cmp_v_w2"]),
        "b_in_w": f(inp["b_in_w"][0]), "b_out_w": f(inp["b_out_w"][0]),
    }
    oh, ovl, fm = nsa_consts()
    d["oh_const"] = oh; d["ov_const"] = ovl; d["fmw_const"] = fm
    return d


_CACHE = {}
PHASES = ("mod", "A", "B", "C", "KV", "Z", "NSA", "FIN2")


def kernel(**inputs):
    inp = {k: np.asarray(v) for k, v in inputs.items()}
    if "nc" not in _CACHE:
        _CACHE["nc"] = build(phases=PHASES).nc
    nc = _CACHE["nc"]
    in_maps = [core_inputs(inp, bi) for bi in range(8)]
    res = run_bass_kernel_spmd(nc, in_maps, core_ids=list(range(8)))
    return np.stack([np.asarray(r["out"]).reshape(T, D) for r in res.results], 0).astype(np.float32)
```

```python
import math
from contextlib import ExitStack
import numpy as np
import concourse.bass as bass
import concourse.mybir as mybir
from concourse.bass_utils import run_bass_kernel_spmd

F32 = mybir.dt.float32
BF16 = mybir.dt.bfloat16
ALU = mybir.AluOpType
AF = mybir.ActivationFunctionType
AX = mybir.AxisListType

ENGS = ("sync", "scalar", "vector", "gpsimd", "tensor")
T = 4096
D = 1024
NT = 32
BIG = 30000.0
import os
NH = int(os.environ.get('MK_NH', '8'))
NTB = int(os.environ.get('MK_NTB', '32'))
STOP = int(os.environ.get('MK_STOP', '99'))
KIT = int(os.environ.get('MK_KIT', '5'))
VV = int(os.environ.get('MK_V', '3'))
NQT = int(os.environ.get('MK_NQT', '8'))
ANNOT = bool(os.environ.get('MK_ANNOT'))
NSTOP = int(os.environ.get('MK_NSTOP', '99'))
NSB = int(os.environ.get('MK_NSB', '3'))
NBR = tuple(int(x) for x in os.environ.get('MK_NBR', '1,2').split(','))


class Tok:
    __slots__ = ("w", "r")

    def __init__(self):
        self.w = None
        self.r = []


class MK:
    NDMA = 20

    def __init__(self, nc):
        self.nc = nc
        self.q = {e: [] for e in ENGS}
        self.seen = {e: {} for e in ENGS}
        self.slots = {e: [0] * self.NDMA for e in ("sync", "scalar", "gpsimd")}
        self.rr = {e: 0 for e in ("sync", "scalar", "gpsimd")}
        self.signal = {e: set() for e in ENGS}
        self.label = ""

    def op(self, eng, fn, reads=(), writes=(), dma=False):
        q = self.q[eng]
        idx = len(q)
        waits = {}

        def need(ev):
            if ev is None:
                return
            key, val = ev
            if key[0] == "c" and key[1] == "tensor" and eng == "tensor":
                return
            if waits.get(key, -1) < val:
                waits[key] = val

        for t in reads:
            need(t.w)
        for t in writes:
            need(t.w)
            for ev in t.r:
                need(ev)
        if dma:
            rr = self.rr[eng]
            self.rr[eng] = (rr + 1) % self.NDMA
            prev = self.slots[eng][rr]
            if prev > 0:
                need((("d", eng, rr), prev))
            self.slots[eng][rr] = prev + 1
            ev = (("d", eng, rr), prev + 1)
        else:
            ev = (("c", eng), idx)
        seen = self.seen[eng]
        final = []
        for key, val in waits.items():
            if seen.get(key, -1) >= val:
                continue
            seen[key] = val
            final.append((key, val))
            if key[0] == "c":
                self.signal[key[1]].add(val)
        q.append((fn, final, ev, dma, self.label))
        for t in reads:
            t.r.append(ev)
            if len(t.r) > 64:
                t.r = t.r[-64:] if False else t.r
        for t in writes:
            t.w = ev
            t.r = []
        return ev

    def barrier(self):
        last = {}
        for e in ENGS:
            for i in range(len(self.q[e]) - 1, -1, -1):
                fn, _, ev, dma = self.q[e][i][:4]
                if fn is not None and not dma:
                    last[e] = i
                    break
        for e in ENGS:
            waits = []
            seen = self.seen[e]
            for e2, ix in last.items():
                if e2 == e:
                    continue
                key = ("c", e2)
                if seen.get(key, -1) < ix:
                    seen[key] = ix
                    waits.append((key, ix))
                    self.signal[e2].add(ix)
            for e2 in ("sync", "scalar", "gpsimd"):
                for rr, cnt in enumerate(self.slots[e2]):
                    key = ("d", e2, rr)
                    if cnt > 0 and seen.get(key, -1) < cnt:
                        seen[key] = cnt
                        waits.append((key, cnt))
            self.q[e].append((None, waits, None, False, ""))

    def finish(self, eng="sync"):
        waits = []
        for e in ("sync", "scalar", "gpsimd"):
            for rr, cnt in enumerate(self.slots[e]):
                if cnt > 0:
                    waits.append((("d", e, rr), cnt))
        self.q[eng].append((None, waits, None, False, ""))

    def emit(self):
        nc = self.nc
        csem = {e: nc.alloc_semaphore(f"c_{e}") for e in ENGS}
        dsem = {e: [nc.alloc_semaphore(f"d_{e}_{i}") for i in range(self.NDMA)]
                for e in ("sync", "scalar", "gpsimd")}
        cval = {}
        for e in ENGS:
            s = sorted(self.signal[e])
            cval[e] = {ix: n + 1 for n, ix in enumerate(s)}

        def replay(e, engobj):
            sig = self.signal[e]
            for i, (fn, waits, ev, dma, lab) in enumerate(self.q[e]):
                for key, val in waits:
                    if key[0] == "c":
                        engobj.wait_ge(csem[key[1]], cval[key[1]][val])
                    else:
                        engobj.wait_ge(dsem[key[1]][key[2]], 16 * val)
                if fn is None:
                    continue
                ins = fn(engobj)
                if ANNOT and lab:
                    ins.annotate(lab)
                if dma:
                    ins.then_inc(dsem[e][ev[0][2]], 16)
                elif i in sig:
                    ins.then_inc(csem[e], 1)

        with nc.Block() as block:
            @block.sync
            def _(eng):
                replay("sync", eng)

            @block.scalar
            def _(eng):
                replay("scalar", eng)

            @block.vector
            def _(eng):
                replay("vector", eng)

            @block.gpsimd
            def _(eng):
                replay("gpsimd", eng)

            @block.tensor
            def _(eng):
                replay("tensor", eng)


class Buf:
    def __init__(self, t):
        self.t = t
        self.k = Tok()

    def __getitem__(self, key):
        return self.t[key]


class Rot:
    def __init__(self, bufs):
        self.bufs = bufs
        self.i = 0

    def next(self):
        b = self.bufs[self.i % len(self.bufs)]
        self.i += 1
        return b


class Bld:
    def __init__(self, dbg=()):
        self.nc = bass.Bass("TRN2", target_bir_lowering=False)
        self.mk = MK(self.nc)
        self.dbg = set(dbg)
        self.n = 0
        self.scopes = [ExitStack()]

    def sb(self, shape, dt=F32, name=None):
        self.n += 1
        return Buf(self.scopes[-1].enter_context(self.nc.sbuf_tensor(name or f"sb{self.n}", list(shape), dt)))

    def push(self):
        self.scopes.append(ExitStack())

    def pop(self):
        self.mk.barrier()
        self.scopes.pop().close()

    def rot(self, n, shape, dt=F32):
        return Rot([self.sb(shape, dt) for _ in range(n)])

    def din(self, name, shape, dt=F32):
        return Buf(self.nc.dram_tensor(name, list(shape), dt, kind="ExternalInput"))

    def dscr(self, name, shape, dt=F32, out=False):
        kind = "ExternalOutput" if (out or name in self.dbg) else "Internal"
        if ("REFIN:" + name) in self.dbg:
            kind = "ExternalInput"
        return Buf(self.nc.dram_tensor(name, list(shape), dt, kind=kind))

    def dma(self, out, in_, reads, writes, eng=None, slow=False):
        if eng is None:
            eng = "sync"
        if slow:
            fn = lambda e: e.dma_start(out=out, in_=in_, allow_slow_non_contiguous=True)
        else:
            fn = lambda e: e.dma_start(out=out, in_=in_)
        self.mk.op(eng, fn, reads=[b.k for b in reads], writes=[b.k for b in writes], dma=True)

    def dmac(self, out, in_, reads, writes):
        self.dma(out, in_, reads, writes, eng="gpsimd")

    def mm(self, out, lhsT, rhs, start, stop, reads, writes):
        self.mk.op("tensor", lambda e: e.matmul(out, lhsT=lhsT, rhs=rhs, start=start, stop=stop),
                   reads=[b.k for b in reads], writes=[b.k for b in writes])

    def tr(self, out, in_, ident, reads, writes):
        self.mk.op("tensor", lambda e: e.transpose(out, in_, ident),
                   reads=[b.k for b in reads], writes=[b.k for b in writes])

    def act(self, out, in_, func, reads, writes, bias=None, scale=None, accum=None):
        kw = {}
        if bias is not None:
            kw["bias"] = bias
        if scale is not None:
            kw["scale"] = scale
        if accum is not None:
            kw["accum_out"] = accum
        self.mk.op("scalar", lambda e: e.activation(out=out, in_=in_, func=func, **kw),
                   reads=[b.k for b in reads], writes=[b.k for b in writes])

    def ve(self, fn, reads, writes, eng="vector"):
        self.mk.op(eng, fn, reads=[b.k for b in reads], writes=[b.k for b in writes])

    def copy(self, out, in_, reads, writes, eng="vector"):
        self.ve(lambda e: e.tensor_copy(out=out, in_=in_), reads, writes, eng)

    def tt(self, out, a, b_, op, reads, writes, eng="vector"):
        self.ve(lambda e: e.tensor_tensor(out=out, in0=a, in1=b_, op=op), reads, writes, eng)

    def ts(self, out, a, s1, s2, op0, op1, reads, writes, eng="vector"):
        if op1 is None:
            self.ve(lambda e: e.tensor_scalar(out=out, in0=a, scalar1=s1, scalar2=None, op0=op0), reads, writes, eng)
        else:
            self.ve(lambda e: e.tensor_scalar(out=out, in0=a, scalar1=s1, scalar2=s2, op0=op0, op1=op1), reads, writes, eng)

    def stt(self, out, a, s, b_, op0, op1, reads, writes, eng="vector"):
        self.ve(lambda e: e.scalar_tensor_tensor(out=out, in0=a, scalar=s, in1=b_, op0=op0, op1=op1), reads, writes, eng)

    def memset(self, ap, val, writes, eng="gpsimd"):
        self.ve(lambda e: e.memset(ap, val), [], writes, eng)


def build(dbg=(), phases=("mod", "A", "B", "C", "KV", "Z", "NSA", "FIN")):
    b = Bld(dbg)
    nc = b.nc
    P = {}
    x_d = b.din("x", [T, D])
    cl_d = b.din("c_l", [128, 8])
    relb_d = b.din("rel_bias", [32, 16])
    adaw_d = b.din("ada_w", [2, D, 3 * D])
    adab_d = b.din("ada_b", [2, 3 * D])
    ng_d = b.din("norm_g_l", [128, 2, 8])
    ainw_d = b.din("a_in_w", [D, 4112])
    convw_d = b.din("conv_w_l", [128, 24, 4])
    alog_d = b.din("a_A_log", [1, 8])
    dtb_d = b.din("a_dt_bias", [1, 8])
    aong_d = b.din("a_onorm_g", [1, 128])
    aoutw_d = b.din("a_out_w", [D, D])
    cst_d = b.din("consts", [128, 6 * 128])
    out_d = b.dscr("out", [T, D], F32, out=True)

    ident = b.sb([128, 128]); identb = b.sb([128, 128], BF16)
    ones = b.sb([128, 128])
    cst = b.sb([128, 6 * 128])
    b.memset(ident[:], 1.0, [ident])
    b.ve(lambda e: e.affine_select(out=ident[:], in_=ident[:], pattern=[[-1, 128]], compare_op=ALU.is_equal,
                                   fill=0.0, base=0, channel_multiplier=1), [ident], [ident], "gpsimd")
    b.copy(identb[:], ident[:], [ident], [identb])
    b.memset(ones[:], 1.0, [ones])
    b.dma(cst[:], cst_d.t.ap(), [cst_d], [cst])
    U2 = cst[:, 0:128]; BONES = cst[:, 128:256]; SELA = cst[:, 256:384]; SELB = cst[:, 384:512]
    NMS = cst[:, 512:640]; NMT = cst[:, 640:768]

    pb = [Buf(nc.alloc_psum_tensor(f"pb{i}", [128, 512], F32)) for i in range(8)]

    gateB = [b.sb([128, D]) for _ in range(2)]
    modcol = [b.sb([128, 24]) for _ in range(2)]
    Acol = [b.sb([128, 8]) for _ in range(2)]
    ng = b.sb([128, 2, 8])
    if "mod" in phases:
        b.mk.label = "mod"
        b.push()
        modB = b.sb([128, 3 * D])
        cs = b.sb([128, 8]); csb = b.sb([128, 8, 128])
        adabB = b.sb([128, 3 * D])
        awr = b.rot(2, [128, 3 * D])
        b.dma(cs[:], cl_d.t.ap(), [cl_d], [cs])
        b.dma(ng[:], ng_d.t.ap(), [ng_d], [ng])
        b.act(cs[:], cs[:], AF.Silu, [cs], [cs])
        for kc in range(8):
            b.copy(csb[:, kc, :], cs[:, kc:kc + 1].to_broadcast([128, 128]), [cs], [csb])
        for l in range(2):
            b.dma(adabB[:], adab_d.t.ap()[l:l + 1, :].to_broadcast([128, 3 * D]), [adab_d], [adabB])
            for kc in range(8):
                aw = awr.next()
                b.dma(aw[:], adaw_d.t.ap()[l, kc * 128:(kc + 1) * 128, :], [adaw_d], [aw])
                for n in range(6):
                    b.mm(pb[n][:], csb[:, kc, :], aw[:, n * 512:(n + 1) * 512], kc == 0, kc == 7, [csb, aw], [pb[n]])
            for n in range(6):
                b.tt(modB[:, n * 512:(n + 1) * 512], pb[n][:], adabB[:, n * 512:(n + 1) * 512], ALU.add,
                     [pb[n], adabB], [modB])
            for j in range(24):
                b.mm(pb[6][:, j:j + 1], modB[0:1, j * 128:(j + 1) * 128], ones[0:1, 0:1], True, True,
                     [modB, ones], [pb[6]])
            b.copy(modcol[l][:], pb[6][:, 0:24], [pb[6]], [modcol[l]])
            b.copy(gateB[l][:], modB[:, 2 * D:3 * D], [modB], [gateB[l]])
            b.stt(Acol[l][:], modcol[l][:, 8:16], 1.0, ng[:, l, :], ALU.add, ALU.mult, [modcol[l], ng], [Acol[l]])
            if l == 0 and "modB0" in b.dbg:
                dd = b.dscr("modB0", [128, 3 * D])
                b.dma(dd.t.ap(), modB[:], [modB], [dd])
                dd2 = b.dscr("modcol0", [128, 24])
                b.dma(dd2.t.ap(), modcol[0][:], [modcol[0]], [dd2])
        b.pop()

    QT = b.dscr("QT", [8, 128, T], BF16)
    KT = b.dscr("KT", [8, 128, T], BF16)
    VT = b.dscr("VT", [8, 128, T], BF16)
    ZS = b.dscr("ZS", [T, D], F32)
    GB = b.dscr("GB", [128, NT, 16], F32)

    g_all = b.sb([128, NT, 8]); beta_all = b.sb([128, NT, 8])

    def silu_from(dst, src, e_buf, reads, eng2="gpsimd"):
        b.act(e_buf[:], src, AF.Exp, reads, [e_buf], scale=-1.0)
        b.ts(e_buf[:], e_buf[:], 1.0, None, ALU.add, None, [e_buf], [e_buf])
        b.ve(lambda e: e.reciprocal(out=e_buf[:], in_=e_buf[:]), [e_buf], [e_buf])
        return e_buf

    if "A" in phases:
        b.mk.label = "A"
        b.push()
        Win = b.sb([128, 8, 4112], BF16)
        for kc in range(8):
            b.dmac(Win[:, kc, :], ainw_d.t.ap()[kc * 128:(kc + 1) * 128, :], [ainw_d], [Win])
        convw = b.sb([128, 24, 4])
        b.dma(convw[:], convw_d.t.ap(), [convw_d], [convw])
        alogB = b.sb([128, 8]); dtbB = b.sb([128, 8]); negA = b.sb([128, 8])
        b.dma(alogB[:], alog_d.t.ap().to_broadcast([128, 8]), [alog_d], [alogB])
        b.dma(dtbB[:], dtb_d.t.ap().to_broadcast([128, 8]), [dtb_d], [dtbB])
        b.act(negA[:], alogB[:], AF.Exp, [alogB], [negA])
        b.ts(negA[:], negA[:], -1.0, None, ALU.mult, None, [negA], [negA])
        pre = b.sb([128, 24, 515])
        b.memset(pre[:], 0.0, [pre])
        hT = b.rot(2, [128, 8, 512], BF16)
        xr = b.rot(2, [128, D]); xnr = b.rot(2, [128, D], BF16)
        st = b.rot(4, [128, 4])
        accr = b.rot(2, [128, 512]); er = b.rot(2, [128, 512]); sr = b.rot(2, [128, 512])
        sqr = b.rot(2, [128, 512]); rir = b.rot(2, [128, 512])
        obr = b.rot(3, [128, 512], BF16)
        zr = b.rot(2, [128, D]); ezr = b.rot(2, [128, 512]); bar = b.rot(2, [128, 16])
        for blk in range(8):
            h = hT.next()
            for tt_ in range(4):
                ti = blk * 4 + tt_
                xt = xr.next(); xn = xnr.next(); s = st.next()
                b.dma(xt[:], x_d.t.ap()[ti * 128:(ti + 1) * 128, :], [x_d], [xt])
                b.act(xn[:], xt[:], AF.Square, [xt], [xn, s], scale=1.0 / 32.0, accum=s[:, 0:1])
                b.act(s[:, 1:2], s[:, 0:1], AF.Ln, [s], [s], bias=1e-6)
                b.act(s[:, 2:3], s[:, 1:2], AF.Exp, [s], [s], scale=-0.5)
                b.ts(xn[:], xt[:], s[:, 2:3], None, ALU.mult, None, [xt, s], [xn])
                for kc in range(8):
                    pbt = pb[6 + (kc % 2)]
                    pv = pbt[:, 0:64].bitcast(BF16)
                    b.tr(pv, xn[:, kc * 128:(kc + 1) * 128], identb[:], [xn, identb], [pbt])
                    b.act(h[:, kc, tt_ * 128:(tt_ + 1) * 128], pv, AF.Identity, [pbt, Acol[0], modcol[0]], [h],
                          bias=modcol[0][:, kc:kc + 1], scale=Acol[0][:, kc:kc + 1])
            for oc in range(24):
                hh = oc % 8
                pbt = pb[oc % 4]
                for kc in range(8):
                    b.mm(pbt[:], Win[:, kc, oc * 128:(oc + 1) * 128], h[:, kc, :], kc == 0, kc == 7, [Win, h], [pbt])
                b.act(pre[:, oc, 3:515], pbt[:], AF.Copy, [pbt], [pre])
                acc = accr.next()
                b.ts(acc[:], pre[:, oc, 0:512], convw[:, oc, 0:1], None, ALU.mult, None, [pre, convw], [acc])
                for k in range(1, 4):
                    b.stt(acc[:], pre[:, oc, k:k + 512], convw[:, oc, k:k + 1], acc[:], ALU.mult, ALU.add,
                          [pre, convw, acc], [acc])
                b.copy(pre[:, oc, 0:3], pre[:, oc, 512:515], [pre], [pre], "gpsimd")
                e_ = silu_from(None, acc[:], er.next(), [acc])
                sv = sr.next()
                b.tt(sv[:], acc[:], e_[:], ALU.mult, [acc, e_], [sv], "gpsimd")
                ob = obr.next()
                if oc < 16:
                    sq = sqr.next(); ri = rir.next()
                    b.tt(sq[:], sv[:], sv[:], ALU.mult, [sv], [sq], "gpsimd")
                    pb2 = pb[4 + (oc % 2)]
                    b.mm(pb2[:], ones[:], sq[:], True, True, [ones, sq], [pb2])
                    b.act(ri[:], pb2[:], AF.Ln, [pb2], [ri], bias=1e-6)
                    b.act(ri[:], ri[:], AF.Exp, [ri], [ri], scale=-0.5)
                    if oc < 8:
                        b.stt(ob[:], sv[:], 128.0 ** -0.5, ri[:], ALU.mult, ALU.mult, [sv, ri], [ob])
                    else:
                        b.tt(ob[:], sv[:], ri[:], ALU.mult, [sv, ri], [ob])
                    dst = QT if oc < 8 else KT
                else:
                    b.copy(ob[:], sv[:], [sv], [ob], "gpsimd")
                    dst = VT
                b.dma(dst.t.ap()[hh, :, blk * 512:(blk + 1) * 512], ob[:], [ob], [dst], "gpsimd")
            for tt_ in range(4):
                ti = blk * 4 + tt_
                z = zr.next(); ba = bar.next()
                for n in range(2):
                    pbt = pb[4 + n]
                    for kc in range(8):
                        b.mm(pbt[:], h[:, kc, tt_ * 128:(tt_ + 1) * 128], Win[:, kc, 3072 + n * 512:3072 + (n + 1) * 512],
                             kc == 0, kc == 7, [h, Win], [pbt])
                    e_ = silu_from(None, pbt[:], ezr.next(), [pbt])
                    b.tt(z[:, n * 512:(n + 1) * 512], pbt[:], e_[:], ALU.mult, [pbt, e_], [z])
                pbt = pb[6]
                for kc in range(8):
                    b.mm(pbt[:, 0:16], h[:, kc, tt_ * 128:(tt_ + 1) * 128], Win[:, kc, 4096:4112], kc == 0, kc == 7, [h, Win], [pbt])
                b.copy(ba[:], pbt[:, 0:16], [pbt], [ba])
                b.dma(ZS.t.ap()[ti * 128:(ti + 1) * 128, :], z[:], [z], [ZS], "gpsimd")
                b.copy(beta_all[:, ti, :], ba[:, 0:8], [ba], [beta_all])
                b.tt(g_all[:, ti, :], ba[:, 8:16], dtbB[:], ALU.add, [ba, dtbB], [g_all])
        bf_ = beta_all[:].rearrange("p t h -> p (t h)")
        b.act(bf_, bf_, AF.Exp, [beta_all], [beta_all], scale=-1.0)
        b.ts(bf_, bf_, 1.0, None, ALU.add, None, [beta_all], [beta_all])
        b.ve(lambda e: e.reciprocal(out=bf_, in_=bf_), [beta_all], [beta_all])
        gf = g_all[:].rearrange("p t h -> p (t h)")
        b.act(gf, gf, AF.Exp, [g_all], [g_all])
        b.act(gf, gf, AF.Ln, [g_all], [g_all], bias=1.0)
        b.tt(g_all[:], g_all[:], negA[:].unsqueeze(1).to_broadcast([128, NT, 8]), ALU.mult, [g_all, negA], [g_all])
        if "GB" in b.dbg:
            b.dma(GB.t.ap()[:, :, 0:8], g_all[:], [g_all], [GB])
            b.dma(GB.t.ap()[:, :, 8:16], beta_all[:], [beta_all], [GB])
        b.pop()

    if "A" not in phases:
        b.push()
        b.memset(g_all[:], -0.05, [g_all]); b.memset(beta_all[:], 0.5, [beta_all])
        zt = b.sb([128, T], BF16); zt2 = b.sb([128, D])
        b.memset(zt[:], 0.01, [zt]); b.memset(zt2[:], 0.5, [zt2])
        for h_ in range(8):
            for dd_ in (QT, KT, VT):
                b.dma(dd_.t.ap()[h_], zt[:], [zt], [dd_])
        for ti in range(NT):
            b.dma(ZS.t.ap()[ti * 128:(ti + 1) * 128, :], zt2[:], [zt2], [ZS])
        b.pop()

    YT = b.dscr("YT", [8, 128, T], BF16)
    OD = b.dscr("OD", [8, T, 128], F32)
    if "B" in phases:
        b.mk.label = "B"
        b.push()
        sc = {n: b.sb([128, NT * 8]) for n in ("gc", "ngc", "gam", "bG", "kap", "nbeta", "GlA", "GlB")}
        gflat = g_all[:].rearrange("p t h -> p (t h)")
        bflat = beta_all[:].rearrange("p t h -> p (t h)")
        for i_, (lh, nm) in enumerate(((U2, "gc"), (BONES, "kap"), (SELA, "GlA"), (SELB, "GlB"))):
            b.mm(pb[i_][:, 0:256], lh, gflat, True, True, [cst, g_all], [pb[i_]])
            b.copy(sc[nm][:], pb[i_][:, 0:256], [pb[i_]], [sc[nm]])
        b.tt(sc["kap"][:], sc["kap"][:], sc["gc"][:], ALU.subtract, [sc["kap"], sc["gc"]], [sc["kap"]])
        b.act(sc["kap"][:], sc["kap"][:], AF.Exp, [sc["kap"]], [sc["kap"]])
        b.act(sc["GlA"][:], sc["GlA"][:], AF.Exp, [sc["GlA"]], [sc["GlA"]])
        b.act(sc["GlB"][:], sc["GlB"][:], AF.Exp, [sc["GlB"]], [sc["GlB"]])
        b.act(sc["gam"][:], sc["gc"][:], AF.Exp, [sc["gc"]], [sc["gam"]])
        b.ts(sc["ngc"][:], sc["gc"][:], -1.0, None, ALU.mult, None, [sc["gc"]], [sc["ngc"]])
        b.ts(sc["nbeta"][:], bflat, -1.0, None, ALU.mult, None, [beta_all], [sc["nbeta"]])
        b.tt(sc["bG"][:], bflat, sc["gam"][:], ALU.mult, [beta_all, sc["gam"]], [sc["bG"]])
        ongB = b.sb([128, 128])
        b.dma(ongB[:], aong_d.t.ap().to_broadcast([128, 128]), [aong_d], [ongB])
        class Reg:
            def __init__(self, bank, lo, hi, bf=False, shared=False):
                self.ap = bank.t[:, lo:hi].bitcast(BF16) if bf else bank.t[:, lo:hi]
                self.k = bank.k if shared else Tok()

        def mkctx(hp):
            bb = 4 * hp
            c = {}
            X1_, X2_, Y1_, Y2_ = pb[bb], pb[bb + 1], pb[bb + 2], pb[bb + 3]
            c["RD"] = Reg(X1_, 0, 128, False, True); c["RT"] = Reg(X1_, 128, 256, False, True)
            c["Nt"] = Reg(X1_, 256, 384, False, True); c["N2"] = Reg(X1_, 384, 512, False, True)
            c["Nt2"] = Reg(X2_, 0, 128, False, True); c["kt"] = Reg(X2_, 128, 192, True, True); c["vt"] = Reg(X2_, 192, 256, True, True)
            c["yt"] = Reg(X2_, 256, 320, True, True); c["U"] = Reg(X2_, 320, 448, False, True)
            c["KK"] = Reg(Y1_, 0, 128); c["QK"] = Reg(Y1_, 128, 256); c["A"] = Reg(Y1_, 256, 384); c["P1"] = Reg(Y1_, 384, 512)
            c["O1"] = Reg(Y2_, 0, 128); c["O2"] = Reg(Y2_, 128, 256); c["S"] = Reg(Y2_, 256, 384); c["W"] = Reg(Y2_, 384, 512)
            c["qT"] = b.sb([128, T], BF16); c["kT"] = b.sb([128, T], BF16); c["vT"] = b.sb([128, T], BF16)
            c["zs"] = b.sb([128, NT, 128]); c["yT"] = b.sb([128, T], BF16)
            c["Sf"] = b.sb([128, 128]); c["Sb"] = b.sb([128, 128], BF16)
            for nm in ("dg", "D", "DT", "u", "tmp", "o", "gz", "jk"):
                c[nm] = b.rot(2, [128, 128])
            for nm in ("N", "Ntb", "Acc"):
                c[nm] = b.rot(3, [128, 128])
            for nm in ("TT", "att", "kbg", "kde", "vb", "wT", "vn", "y"):
                c[nm] = b.rot(2, [128, 128], BF16)
            c["s4"] = b.rot(4, [128, 4])
            return c

        def head_gen(c, h):
            qT, kT, vT, zs, yT, S, Sb = c["qT"], c["kT"], c["vT"], c["zs"], c["yT"], c["Sf"], c["Sb"]
            b.dma(qT[:], QT.t.ap()[h], [QT], [qT]); b.dma(kT[:], KT.t.ap()[h], [KT], [kT]); b.dma(vT[:], VT.t.ap()[h], [VT], [vT])
            for q4 in range(4):
                b.dma(zs[:, q4 * 8:(q4 + 1) * 8, :],
                      ZS.t.ap()[q4 * 1024:(q4 + 1) * 1024, h * 128:(h + 1) * 128].rearrange("(t p) v -> p t v", p=128), [ZS], [zs])
            b.memset(S[:], 0.0, [S]); b.memset(Sb[:], 0.0, [Sb])
            yield
            for t in range(NTB):
                cols = slice(t * 128, (t + 1) * 128)
                th = slice(t * 8 + h, t * 8 + h + 1)
                KK, QK, RD, RT = c["KK"], c["QK"], c["RD"], c["RT"]
                b.mm(KK.ap, kT[:, cols], kT[:, cols], True, True, [kT], [KK])
                b.mm(QK.ap, kT[:, cols], qT[:, cols], True, True, [kT, qT], [QK])
                dg = c["dg"].next()
                b.ts(dg[:], ident[:], sc["gc"][:, th], None, ALU.mult, None, [ident, sc["gc"]], [dg])
                yield
                b.mm(RD.ap, ones[:], dg[:], True, False, [ones, dg], [RD])
                b.mm(RD.ap, ident[:], NMS, False, True, [ident, cst], [RD])
                b.mm(RT.ap, ones[:], dg[:], True, False, [ones, dg], [RT])
                b.mm(RT.ap, ident[:], NMT, False, True, [ident, cst], [RT])
                yield
                Dm = c["D"].next(); DTm = c["DT"].next()
                b.act(Dm[:], RD.ap, AF.Exp, [RD, sc["gc"]], [Dm], bias=sc["gc"][:, th], scale=-1.0)
                b.act(DTm[:], RT.ap, AF.Exp, [RT, sc["ngc"]], [DTm], bias=sc["ngc"][:, th], scale=1.0)
                yield
                N = c["N"].next()
                b.stt(N[:], KK.ap, sc["nbeta"][:, th], Dm[:], ALU.mult, ALU.mult, [KK, sc["nbeta"], Dm], [N])
                att = c["att"].next()
                b.tt(att[:], QK.ap, DTm[:], ALU.mult, [QK, DTm], [att])
                yield
                pNt, pA, pN2, pNt2 = c["Nt"], c["A"], c["N2"], c["Nt2"]
                b.mm(pNt.ap, N[:], ident[:], True, True, [N, ident], [pNt])
                Nt = c["Ntb"].next(); Acc = c["Acc"].next()
                yield
                b.act(Nt[:], pNt.ap, AF.Copy, [pNt], [Nt])
                yield
                b.tt(Acc[:], Nt[:], ident[:], ALU.add, [Nt, ident], [Acc])
                TTb = c["TT"].next()
                for k in range(1, 6):
                    N2 = c["N"].next()
                    b.mm(pN2.ap, Nt[:], N[:], True, True, [Nt, N], [pN2])
                    if k < 5:
                        Nt2 = c["Ntb"].next()
                        b.mm(pNt2.ap, N[:], Nt[:], True, True, [N, Nt], [pNt2])
                    yield
                    b.act(N2[:], pN2.ap, AF.Copy, [pN2], [N2])
                    if k < 5:
                        b.act(Nt2[:], pNt2.ap, AF.Copy, [pNt2], [Nt2])
                    yield
                    b.mm(pA.ap, N2[:], Acc[:], True, True, [N2, Acc], [pA])
                    yield
                    if k < 5:
                        Acc2 = c["Acc"].next()
                        b.tt(Acc2[:], pA.ap, Acc[:], ALU.add, [pA, Acc], [Acc2])
                        Acc = Acc2; Nt = Nt2
                    else:
                        b.tt(TTb[:], pA.ap, Acc[:], ALU.add, [pA, Acc], [TTb])
                    N = N2
                    yield
                pkt, pvt, pyt = c["kt"], c["vt"], c["yt"]
                b.tr(pkt.ap, kT[:, cols], identb[:], [kT, identb], [pkt])
                b.tr(pvt.ap, vT[:, cols], identb[:], [vT, identb], [pvt])
                yield
                kbg = c["kbg"].next(); kde = c["kde"].next(); vb = c["vb"].next()
                b.act(kbg[:], pkt.ap, AF.Identity, [pkt, sc["bG"]], [kbg], scale=sc["bG"][:, th])
                b.act(kde[:], pkt.ap, AF.Identity, [pkt, sc["kap"]], [kde], scale=sc["kap"][:, th])
                b.act(vb[:], pvt.ap, AF.Identity, [pvt, beta_all], [vb], scale=bflat[:, th])
                yield
                pU, pW = c["U"], c["W"]
                u = c["u"].next(); wT = c["wT"].next()
                b.mm(pU.ap, TTb[:], vb[:], True, True, [TTb, vb], [pU])
                b.mm(pW.ap, kbg[:], TTb[:], True, True, [kbg, TTb], [pW])
                yield
                b.act(u[:], pU.ap, AF.Copy, [pU], [u])
                b.copy(wT[:], pW.ap, [pW], [wT])
                yield
                vn = c["vn"].next(); o = c["o"].next()
                pP1, pO1, pO2, pS = c["P1"], c["O1"], c["O2"], c["S"]
                for hf in range(2):
                    rows = slice(hf * 64, hf * 64 + 64)
                    b.mm(pP1.ap, wT[:], Sb[:], True, True, [wT, Sb], [pP1])
                    b.mm(pO1.ap, qT[:, cols], Sb[:], True, True, [qT, Sb], [pO1])
                    yield
                    b.tt(vn[rows, :], u[rows, :], pP1.ap[rows, :], ALU.subtract, [u, pP1], [vn])
                    yield
                    b.mm(pS.ap, kde[rows, :], vn[rows, :], True, True, [kde, vn], [pS])
                    b.mm(pO2.ap, att[rows, :], vn[rows, :], True, True, [att, vn], [pO2])
                    yield
                    gl = sc["GlA"] if hf == 0 else sc["GlB"]
                    b.stt(S[:], S[:], gl[:, th], pS.ap, ALU.mult, ALU.add, [S, gl, pS], [S])
                    yield
                    b.act(Sb[:], S[:], AF.Copy, [S], [Sb])
                    tmp = c["tmp"].next()
                    b.copy(tmp[rows, :], pO2.ap[rows, :], [pO2], [tmp])
                    b.stt(o[rows, :], pO1.ap[rows, :], sc["gam"][rows, th], tmp[rows, :], ALU.mult, ALU.add,
                          [pO1, sc["gam"], tmp], [o])
                    yield
                if "OD" in b.dbg:
                    b.dma(OD.t.ap()[h, t * 128:(t + 1) * 128, :], o[:], [o], [OD], "gpsimd")
                s = c["s4"].next(); jk = c["jk"].next(); gz = c["gz"].next(); y = c["y"].next()
                b.act(jk[:], o[:], AF.Square, [o], [jk, s], scale=128.0 ** -0.5, accum=s[:, 0:1])
                b.tt(gz[:], zs[:, t, :], ongB[:], ALU.mult, [zs, ongB], [gz], "gpsimd")
                yield
                b.act(s[:, 1:2], s[:, 0:1], AF.Ln, [s], [s], bias=1e-6)
                yield
                b.act(s[:, 2:3], s[:, 1:2], AF.Exp, [s], [s], scale=-0.5)
                yield
                b.stt(y[:], o[:], s[:, 2:3], gz[:], ALU.mult, ALU.mult, [o, s, gz], [y])
                yield
                b.tr(pyt.ap, y[:], identb[:], [y, identb], [pyt])
                yield
                b.act(yT[:, cols], pyt.ap, AF.Copy, [pyt], [yT])
                yield
            b.dma(YT.t.ap()[h], yT[:], [yT], [YT], "gpsimd")

        ctxs = [mkctx(0), mkctx(1)]
        b.mk.barrier()
        for hp in range(0, NH, 2):
            gens = [head_gen(ctxs[i], hp + i) for i in range(2) if hp + i < NH]
            while gens:
                for g_ in list(gens):
                    try:
                        next(g_)
                    except StopIteration:
                        gens.remove(g_)
        b.pop()

    X1 = b.dscr("X1", [T, D], F32)
    X2 = b.dscr("X2", [T, D], F32)
    ST = b.dscr("ST", [8, 128, T], BF16)
    H1T = b.dscr("H1T", [8, 128, T], BF16)
    kvg_d = b.din("kvg_l", [128, 8])
    if "C" in phases:
        b.mk.label = "C"
        b.push()
        Wo = b.sb([128, 8, D], BF16)
        for kc in range(8):
            b.dmac(Wo[:, kc, :], aoutw_d.t.ap()[kc * 128:(kc + 1) * 128, :], [aoutw_d], [Wo])
        kvg = b.sb([128, 8])
        b.dma(kvg[:], kvg_d.t.ap(), [kvg_d], [kvg])
        ytl = b.rot(2, [128, 8, 128], BF16); xr = b.rot(2, [128, D]); x1r = b.rot(2, [128, D])
        xnr = b.rot(2, [128, D], BF16); st = b.rot(4, [128, 4]); jr2 = b.rot(1, [128, D], BF16)
        sTt = b.rot(2, [128, 8, 128], BF16); hTt = b.rot(2, [128, 8, 128], BF16)
        for ti in range(NT):
            rows = slice(ti * 128, (ti + 1) * 128)
            yt = ytl.next(); xt = xr.next(); x1 = x1r.next(); xn = xnr.next(); s = st.next()
            b.dma(yt[:], YT.t.ap()[:, :, rows].rearrange("h v t -> v h t"), [YT], [yt])
            b.dma(xt[:], x_d.t.ap()[rows, :], [x_d], [xt])
            for n in range(2):
                for kc in range(8):
                    b.mm(pb[n][:], yt[:, kc, :], Wo[:, kc, n * 512:(n + 1) * 512], kc == 0, kc == 7, [yt, Wo], [pb[n]])
                b.tt(x1[:, n * 512:(n + 1) * 512], pb[n][:], gateB[0][:, n * 512:(n + 1) * 512], ALU.mult, [pb[n], gateB[0]], [x1])
            b.tt(x1[:], x1[:], xt[:], ALU.add, [x1, xt], [x1], "gpsimd")
            b.dma(X1.t.ap()[rows, :], x1[:], [x1], [X1], "gpsimd")
            jk = jr2.next()
            b.act(jk[:], x1[:], AF.Square, [x1], [jk, s], scale=1.0 / 32.0, accum=s[:, 0:1])
            b.act(s[:, 1:2], s[:, 0:1], AF.Ln, [s], [s], bias=1e-6)
            b.act(s[:, 2:3], s[:, 1:2], AF.Exp, [s], [s], scale=-0.5)
            b.ts(xn[:], x1[:], s[:, 2:3], None, ALU.mult, None, [x1, s], [xn])
            sT_ = sTt.next(); hT_ = hTt.next()
            for kc in range(8):
                pbt = pb[6 + (kc % 2)]
                pv = pbt[:, 0:64].bitcast(BF16)
                b.tr(pv, xn[:, kc * 128:(kc + 1) * 128], identb[:], [xn, identb], [pbt])
                b.act(hT_[:, kc, :], pv, AF.Identity, [pbt, Acol[1], modcol[1]], [hT_],
                      bias=modcol[1][:, kc:kc + 1], scale=Acol[1][:, kc:kc + 1])
                b.act(sT_[:, kc, :], pv, AF.Identity, [pbt, kvg], [sT_], scale=kvg[:, kc:kc + 1])
            b.dma(ST.t.ap()[:, :, rows].rearrange("k p t -> p k t"), sT_[:], [sT_], [ST], "gpsimd")
            b.dma(H1T.t.ap()[:, :, rows].rearrange("k p t -> p k t"), hT_[:], [hT_], [H1T], "gpsimd")
        b.pop()

    kvw_d = b.din("kv_w", [D, 768])
    posT_d = b.din("posT", [2, 64, 32])
    w1_d = [b.din("cmp_k_w1", [2048, 256]), b.din("cmp_v_w1", [2048, 256])]
    w2_d = [b.din("cmp_k_w2", [256, 64]), b.din("cmp_v_w2", [256, 64])]
    KVT = b.dscr("KVT", [6, 128, T], BF16)
    VTOK = b.dscr("VTOK", [2, T, 128], BF16)
    KCT = b.dscr("KCT", [2, 2, 64, 256], F32)
    VCT = b.dscr("VCT", [2, 2, 256, 64], F32)
    if "KV" in phases:
        b.mk.label = "KV"
        b.push()
        kvw = b.sb([128, 8, 768], BF16)
        for kc in range(8):
            b.dmac(kvw[:, kc, :], kvw_d.t.ap()[kc * 128:(kc + 1) * 128, :], [kvw_d], [kvw])
        sTr = b.rot(2, [128, 8, 512], BF16); ocr = b.rot(3, [128, 512], BF16); otr = b.rot(3, [128, 128], BF16)
        for blk in range(8):
            cols = slice(blk * 512, (blk + 1) * 512)
            sTb = sTr.next()
            b.dma(sTb[:], ST.t.ap()[:, :, cols].rearrange("k p t -> p k t"), [ST], [sTb])
            for i in range(6):
                pbt = pb[i % 4]
                for kc in range(8):
                    b.mm(pbt[:], kvw[:, kc, i * 128:(i + 1) * 128], sTb[:, kc, :], kc == 0, kc == 7, [kvw, sTb], [pbt])
                oc_ = ocr.next()
                b.act(oc_[:], pbt[:], AF.Copy, [pbt], [oc_])
                b.dma(KVT.t.ap()[i, :, cols], oc_[:], [oc_], [KVT], "gpsimd")
            for j, i in enumerate((3, 5)):
                for tt_ in range(4):
                    pbt = pb[4 + (tt_ % 2)]
                    for kc in range(8):
                        b.mm(pbt[:, 0:128], sTb[:, kc, tt_ * 128:(tt_ + 1) * 128], kvw[:, kc, i * 128:(i + 1) * 128],
                             kc == 0, kc == 7, [sTb, kvw], [pbt])
                    ot_ = otr.next()
                    b.copy(ot_[:], pbt[:, 0:128], [pbt], [ot_])
                    r0 = blk * 512 + tt_ * 128
                    b.dma(VTOK.t.ap()[j, r0:r0 + 128, :], ot_[:], [ot_], [VTOK], "gpsimd")
        b.mk.barrier()
        w1 = b.sb([64, 32, 256], BF16); w2 = b.sb([128, 2, 64], BF16); posT = b.sb([64, 32], BF16)
        tokT = b.sb([64, T], BF16)
        pbias = b.sb([128, 4]); npbias = b.sb([128, 4])
        hid = [b.sb([128, 256], BF16) for _ in range(2)]
        er2 = b.rot(2, [128, 256]); zz2 = b.rot(2, [128, 256])
        kco = b.sb([64, 256]); vco = b.sb([128, 2, 64])
        for kind in range(2):
            for l4 in range(4):
                b.dmac(w1[:, l4 * 8:(l4 + 1) * 8, :],
                       w1_d[kind].t.ap()[l4 * 512:(l4 + 1) * 512, :].rearrange("(l d) h -> d l h", d=64), [w1_d[kind]], [w1])
            b.dmac(w2[:], w2_d[kind].t.ap().rearrange("(c p) d -> p c d", p=128), [w2_d[kind]], [w2])
            b.dmac(posT[:], posT_d.t.ap()[kind], [posT_d], [posT])
            for hc in range(2):
                for l in range(32):
                    b.mm(pb[6][:, hc:hc + 1], w1[:, l, hc * 128:(hc + 1) * 128], posT[:, l:l + 1], l == 0, l == 31, [w1, posT], [pb[6]])
            b.copy(pbias[:, 0:2], pb[6][:, 0:2], [pb[6]], [pbias])
            b.ts(npbias[:, 0:2], pbias[:, 0:2], -1.0, None, ALU.mult, None, [pbias], [npbias])
            for g in range(2):
                b.dma(tokT[:], KVT.t.ap()[kind, g * 64:(g + 1) * 64, :], [KVT], [tokT])
                b.memset(hid[0][:], 0.0, [hid[0]]); b.memset(hid[1][:], 0.0, [hid[1]])
                for hc in range(2):
                    pbt = pb[hc]
                    for l in range(32):
                        b.mm(pbt[:, 0:255], w1[:, l, hc * 128:(hc + 1) * 128], tokT[:, l:l + 16 * 254 + 1:16], l == 0, l == 31, [w1, tokT], [pbt])
                    e_ = er2.next(); zz = zz2.next()
                    b.act(e_[:, 0:255], pbt[:, 0:255], AF.Exp, [pbt, npbias], [e_], bias=npbias[:, hc:hc + 1], scale=-1.0)
                    b.act(zz[:, 0:255], pbt[:, 0:255], AF.Identity, [pbt, pbias], [zz], bias=pbias[:, hc:hc + 1])
                    b.ts(e_[:, 0:255], e_[:, 0:255], 1.0, None, ALU.add, None, [e_], [e_])
                    b.ve(lambda e, e_=e_: e.reciprocal(out=e_[:, 0:255], in_=e_[:, 0:255]), [e_], [e_])
                    b.tt(hid[hc][:, 0:255], zz[:, 0:255], e_[:, 0:255], ALU.mult, [zz, e_], [hid[hc]])
                for hc in range(2):
                    b.mm(pb[2][0:64, 0:256], w2[:, hc, :], hid[hc][:], hc == 0, hc == 1, [w2, hid[hc]], [pb[2]])
                b.copy(kco[:], pb[2][0:64, 0:256], [pb[2]], [kco])
                b.dma(KCT.t.ap()[kind, g], kco[:], [kco], [KCT], "gpsimd")
                for ct in range(2):
                    for hc in range(2):
                        b.mm(pb[3][:, ct * 64:(ct + 1) * 64], hid[hc][:, ct * 128:(ct + 1) * 128], w2[:, hc, :], hc == 0, hc == 1,
                             [hid[hc], w2], [pb[3]])
                b.copy(vco[:].rearrange("p c d -> p (c d)"), pb[3][:, 0:128], [pb[3]], [vco])
                b.dma(VCT.t.ap()[kind, g].rearrange("(c p) d -> p c d", p=128), vco[:], [vco], [VCT], "gpsimd")
        b.pop()

    binw_d = b.din("b_in_w", [D, 4144])
    boutw_d = b.din("b_out_w", [D, D])
    oh_d = b.din("oh_const", [33, 1536])
    ov_d = b.din("ov_const", [128, 2, 64])
    fmw_d = b.din("fmw_const", [128, 126])
    ZG = b.dscr("ZG", [T, 3072], F32)
    Y2 = b.dscr("Y2", [T, D], BF16)
    FD = b.dscr("FD", [16, 1536], F32)
    G1 = b.dscr("G1", [16 * 128 * 1537 + 4096], F32)
    G2 = b.dscr("G2", [16 * 24 * 1552 + 4096], F32)
    OCD = b.dscr("OCD", [3, T, D], F32)
    SELD = b.dscr("SELD", [2, T, 64], F32)
    IMPD = b.dscr("IMPD", [2, T, 64], F32)

    if "Z" in phases:
        b.mk.label = "Z"
        b.push()
        Wz = b.sb([128, 8, 3120], BF16)
        for kc in range(8):
            b.dmac(Wz[:, kc, :], binw_d.t.ap()[kc * 128:(kc + 1) * 128, 1024:4144], [binw_d], [Wz])
        hbr = b.rot(2, [128, 8, 512], BF16)
        zgr = b.rot(2, [128, 3072]); er3 = b.rot(2, [128, 512]); zcr = b.rot(2, [128, 512]); sgr = b.rot(2, [128, 48])
        for blk in range(NQT):
            hb = hbr.next()
            b.dma(hb[:], H1T.t.ap()[:, :, blk * 512:(blk + 1) * 512].rearrange("k p t -> p k t"), [H1T], [hb])
            for tt_ in range(4):
                tsl = slice(tt_ * 128, (tt_ + 1) * 128)
                r0 = blk * 512 + tt_ * 128
                sg = sgr.next(); zg = zgr.next()
                for kc in range(8):
                    b.mm(pb[6][:, 0:48], hb[:, kc, tsl], Wz[:, kc, 3072:3120], kc == 0, kc == 7, [hb, Wz], [pb[6]])
                b.act(sg[:], pb[6][:, 0:48], AF.Exp, [pb[6]], [sg], scale=-1.0)
                b.ts(sg[:], sg[:], 1.0, None, ALU.add, None, [sg], [sg])
                b.ve(lambda e, sg=sg: e.reciprocal(out=sg[:], in_=sg[:]), [sg], [sg])
                for n in range(6):
                    pbt = pb[n % 4]
                    for kc in range(8):
                        b.mm(pbt[:], hb[:, kc, tsl], Wz[:, kc, n * 512:(n + 1) * 512], kc == 0, kc == 7, [hb, Wz], [pbt])
                    e_ = er3.next(); zc = zcr.next()
                    b.act(e_[:], pbt[:], AF.Exp, [pbt], [e_], scale=-1.0)
                    b.act(zc[:], pbt[:], AF.Copy, [pbt], [zc])
                    b.ts(e_[:], e_[:], 1.0, None, ALU.add, None, [e_], [e_])
                    b.ve(lambda e, e_=e_: e.reciprocal(out=e_[:], in_=e_[:]), [e_], [e_])
                    b.tt(zc[:], zc[:], e_[:], ALU.mult, [zc, e_], [zc], "gpsimd")
                    b.tt(zg[:, n * 512:(n + 1) * 512].rearrange("p (h d) -> p h d", d=64),
                         zc[:].rearrange("p (h d) -> p h d", d=64),
                         sg[:, n * 8:(n + 1) * 8].unsqueeze(2).to_broadcast([128, 8, 64]), ALU.mult, [zc, sg], [zg])
                b.dma(ZG.t.ap()[r0:r0 + 128, :], zg[:], [zg], [ZG], "gpsimd")
        b.pop()

    if "NSA" in phases:
        b.mk.label = "NSAsetup"
        b.push()
        cfarB = b.sb([128, 16])
        b.push()
        rb = b.sb([33, 16]); ohs = b.sb([33, 1536]); fsb = b.sb([16, 1536])
        b.dma(rb[0:32, :], relb_d.t.ap(), [relb_d], [rb])
        b.memset(rb[32:33, :], -BIG, [rb])
        b.dma(ohs[:], oh_d.t.ap(), [oh_d], [ohs])
        b.dma(cfarB[:], relb_d.t.ap()[31:32, :].to_broadcast([128, 16]), [relb_d], [cfarB])
        for n in range(3):
            b.mm(pb[n][0:16, :], rb[:], ohs[:, n * 512:(n + 1) * 512], True, True, [rb, ohs], [pb[n]])
            b.copy(fsb[:, n * 512:(n + 1) * 512], pb[n][0:16, :], [pb[n]], [fsb])
        b.dma(FD.t.ap(), fsb[:], [fsb], [FD])
        b.pop()
        kslc = b.sb([128, T], BF16)
        b.memset(kslc[:], 1.0, [kslc])
        b.ve(lambda e: e.affine_select(out=kslc[:], in_=kslc[:], pattern=[[1, T]], compare_op=ALU.is_ge, fill=0.0,
                                       base=4096, channel_multiplier=-64), [kslc], [kslc], "gpsimd")
        b.ve(lambda e: e.affine_select(out=kslc[:], in_=kslc[:], pattern=[[-1, T]], compare_op=ALU.is_ge, fill=0.0,
                                       base=-4033, channel_multiplier=64), [kslc], [kslc], "gpsimd")
        kwin = b.sb([128, T], BF16)
        b.memset(kwin[:], 0.0, [kwin])
        OVt = b.sb([128, 2, 64]); FMW = b.sb([128, 126]); zer = b.sb([128, 232])
        b.dma(OVt[:], ov_d.t.ap(), [ov_d], [OVt]); b.dma(FMW[:], fmw_d.t.ap(), [fmw_d], [FMW])
        b.memset(zer[:], 0.0, [zer])
        VS = b.sb([128, NT, 65], BF16); VW = b.sb([128, NT, 65], BF16)
        kcT = b.sb([128, 256], BF16); VC = b.sb([128, 2, 64], BF16)
        Wq = b.sb([128, 8, 512], BF16)
        EBS = b.sb([128, 8, 1024], BF16); EBW = b.sb([128, 8, 512], BF16); EBc = b.sb([128, 8, 504], BF16)
        hbr = b.rot(1, [128, 8, 512], BF16)
        qA = [b.sb([128, 8, 512], BF16) for _ in range(2)]
        Eer = b.rot(2, [128, 512]); Pbr = b.rot(3, [128, 512], BF16)
        OB = [b.sb([128, 4, 512]) for _ in range(3)]
        OC2 = [OB[0], b.sb([128, 4, 512])]
        zgt = b.rot(2, [128, 3, 512]); yr2 = b.rot(2, [128, 512], BF16); t1r = b.rot(2, [128, 512]); t2r = b.rot(2, [128, 512])
        impP = b.rot(2, [128, 256]); Ecr = b.rot(2, [128, 256]); Pcr = b.rot(2, [128, 256]); Pnr = b.rot(2, [128, 256], BF16)
        PTr = b.rot(2, [128, 2, 128], BF16); rsr = b.rot(4, [128, 4]); impT = b.rot(2, [128, 2, 128])
        impf = b.rot(2, [128, 64]); m8r = b.rot(2, [128, 8]); m8br = b.rot(2, [128, 8]); tmpm = b.rot(2, [128, 64]); seln = b.rot(2, [128, 128])
        for sb_ in seln.bufs:
            b.memset(sb_[:], 0.0, [sb_])
        rs2 = b.rot(8, [128, 1])
        for g in range(2):
            b.dma(kslc[0:64, :], KVT.t.ap()[2, g * 64:(g + 1) * 64, :], [KVT], [kslc])
            b.dma(kwin[0:64, :], KVT.t.ap()[4, g * 64:(g + 1) * 64, :], [KVT], [kwin])
            b.dmac(kcT[0:64, :], KCT.t.ap()[0, g], [KCT], [kcT])
            for q4 in range(4):
                tsl = slice(q4 * 8, (q4 + 1) * 8)
                b.dma(VS[:, tsl, 0:64], VTOK.t.ap()[0, q4 * 1024:(q4 + 1) * 1024, g * 64:(g + 1) * 64].rearrange("(t p) d -> p t d", p=128), [VTOK], [VS])
                b.dma(VW[:, tsl, 0:64], VTOK.t.ap()[1, q4 * 1024:(q4 + 1) * 1024, g * 64:(g + 1) * 64].rearrange("(t p) d -> p t d", p=128), [VTOK], [VW])
            b.memset(VS[:, :, 64:65], 1.0, [VS]); b.memset(VW[:, :, 64:65], 1.0, [VW])
            b.dmac(VC[:], VCT.t.ap()[1, g].rearrange("(c p) d -> p c d", p=128), [VCT], [VC])
            for kc in range(8):
                b.dmac(Wq[:, kc, :], binw_d.t.ap()[kc * 128:(kc + 1) * 128, g * 512:(g + 1) * 512], [binw_d], [Wq])
            b.mk.label = "tables"
            b.push()
            Frep = b.rot(1, [128, 1536]); tS = b.rot(1, [128, 1408]); tC = b.rot(2, [128, 128])
            b.memset(EBc[:], 0.0, [EBc])
            for hh in range(8):
                H = g * 8 + hh
                fr = Frep.next()
                b.dma(fr[:], FD.t.ap()[H:H + 1, :].to_broadcast([128, 1536]), [FD], [fr])
                b.dma(bass.AP(G1.t, H * 128 * 1537, [[1537, 128], [1, 1536]]), fr[:], [fr], [G1])
                b.dma(bass.AP(G2.t, H * 24 * 1552, [[1552, 24], [1, 1536]]), fr[0:24, :], [fr], [G2])
                ts_ = tS.next()
                b.dma(ts_[:], bass.AP(G1.t, H * 128 * 1537 + 128, [[1536, 128], [1, 1408]]), [G1], [ts_])
                b.act(EBS[:, hh, :], ts_[:, 0:1024], AF.Exp, [ts_], [EBS])
                b.act(EBW[:, hh, :], ts_[:, 896:1408], AF.Exp, [ts_], [EBW])
                b.ve(lambda e, hh=hh: e.affine_select(out=EBW[:, hh, :], in_=EBW[:, hh, :], pattern=[[-1, 512]], compare_op=ALU.is_ge,
                                                      fill=0.0, base=-1, channel_multiplier=1), [EBW], [EBW], "gpsimd")
                tc_ = tC.next()
                b.dma(tc_[0:24, :], bass.AP(G2.t, H * 24 * 1552 + 737, [[1536, 24], [1, 128]]), [G2], [tc_])
                b.mm(pb[6][:, 0:24], tc_[0:24, :], ident[0:24, 0:24], True, True, [tc_, ident], [pb[6]])
                b.act(EBc[:, hh, 232:256], pb[6][:, 0:24], AF.Exp, [pb[6]], [EBc])
                b.act(EBc[:, hh, 0:232], zer[:], AF.Exp, [zer, cfarB], [EBc], bias=cfarB[:, H:H + 1])
            b.pop()
            if "EBT" in b.dbg and g == 0:
                dd = b.dscr("EBT", [128, 8, 512], BF16); b.dma(dd.t.ap(), EBW[:], [EBW], [dd])
                dd = b.dscr("EBS_", [128, 8, 1024], BF16); b.dma(dd.t.ap(), EBS[:], [EBS], [dd])
                dd = b.dscr("EBC_", [128, 8, 504], BF16); b.dma(dd.t.ap(), EBc[:], [EBc], [dd])
            def cmp_gen(QT):
                qa = qA[QT % 2]; OC = OC2[QT % 2]
                b.mk.label = "qproj"
                hb = hbr.next()
                b.dma(hb[:], H1T.t.ap()[:, :, QT * 512:(QT + 1) * 512].rearrange("k p t -> p k t"), [H1T], [hb])
                for hh in range(8):
                    for kc in range(8):
                        b.mm(pb[6][0:64, :], Wq[:, kc, hh * 64:(hh + 1) * 64], hb[:, kc, :], kc == 0, kc == 7, [Wq, hb], [pb[6]])
                    b.act(qa[0:64, hh, :], pb[6][0:64, :], AF.Copy, [pb[6]], [qa], scale=0.125)
                    yield
                for qs in range(4):
                    qt = QT * 4 + qs
                    qsl = slice(qs * 128, (qs + 1) * 128)
                    r0 = qt * 128
                    nct = 1 if 8 * qt + 7 < 128 else 2
                    ip = impP.next()
                    for hh in range(8):
                        b.mk.label = "cmp"
                        pS = pb[6][:, 0:256]
                        b.mm(pS, qa[0:64, hh, qsl], kcT[0:64, :], True, True, [qa, kcT], [pb[6]])
                        Ee = Ecr.next(); Pc = Pcr.next(); Pn = Pnr.next(); rs = rsr.next(); PT = PTr.next()
                        yield
                        b.act(Ee[:], pS, AF.Exp, [pb[6]], [Ee])
                        yield
                        j0 = 248 - 8 * qt
                        b.ve(lambda e, Pc=Pc, Ee=Ee, hh=hh, j0=j0, rs=rs: e.scalar_tensor_tensor(
                            out=Pc[:], in0=Ee[:], scalar=1.0, in1=EBc[:, hh, j0:j0 + 256], op0=ALU.mult, op1=ALU.mult,
                            accum_out=rs[:, 0:1]), [Ee, EBc], [Pc, rs])
                        b.ts(rs[:, 0:1], rs[:, 0:1], 1e-30, None, ALU.max, None, [rs], [rs])
                        b.ve(lambda e, rs=rs: e.reciprocal(out=rs[:, 1:2], in_=rs[:, 0:1]), [rs], [rs])
                        yield
                        b.ts(Pn[:], Pc[:], rs[:, 1:2], None, ALU.mult, None, [Pc, rs], [Pn])
                        if hh == 0:
                            b.ts(ip[:], Pc[:], rs[:, 1:2], None, ALU.mult, None, [Pc, rs], [ip])
                        else:
                            b.stt(ip[:], Pc[:], rs[:, 1:2], ip[:], ALU.mult, ALU.add, [Pc, rs, ip], [ip])
                        yield
                        pTv = pb[6][:, 256:384].bitcast(BF16)
                        for ct in range(nct):
                            b.tr(pTv[:, ct * 128:(ct + 1) * 128], Pn[:, ct * 128:(ct + 1) * 128], identb[:], [Pn, identb], [pb[6]])
                        yield
                        b.act(PT[:, 0:nct, :].rearrange("p c q -> p (c q)"), pTv[:, 0:nct * 128], AF.Copy, [pb[6]], [PT])
                        yield
                        for ct in range(nct):
                            b.mm(pb[7][:, 0:64], PT[:, ct, :], VC[:, ct, :], ct == 0, ct == nct - 1, [PT, VC], [pb[7]])
                        yield
                        b.copy(OC[:, qs, hh * 64:(hh + 1) * 64], pb[7][:, 0:64], [pb[7]], [OC])
                        yield
                    it = impT.next()
                    for ct in range(2):
                        b.mm(pb[7][:, 128 + ct * 128:256 + ct * 128], ip[:, ct * 128:(ct + 1) * 128], ident[:], True, True, [ip, ident], [pb[7]])
                    yield
                    b.copy(it[:].rearrange("p c q -> p (c q)"), pb[7][:, 128:384], [pb[7]], [it])
                    yield
                    for ct in range(2):
                        b.mm(pb[7][:, 64:128], it[:, ct, :], OVt[:, ct, :], ct == 0, ct == 1, [it, OVt], [pb[7]])
                    yield
                    imf = impf.next(); m8 = m8r.next(); m8b = m8br.next(); tm = tmpm.next(); sn = seln.next()
                    if "IMPD" in b.dbg:
                        b.copy(tm[:], pb[7][:, 64:128], [pb[7]], [tm])
                        b.dma(IMPD.t.ap()[g, r0:r0 + 128, :], tm[:], [tm], [IMPD], "gpsimd")
                    b.tt(imf[:], pb[7][:, 64:128], FMW[:, 62 - 2 * qt:62 - 2 * qt + 64], ALU.add, [pb[7], FMW], [imf])
                    b.ts(imf[:, 0:1], imf[:, 0:1], 2.0e9, None, ALU.add, None, [imf], [imf])
                    yield
                    b.ve(lambda e, m8=m8, imf=imf: e.max(out=m8[:], in_=imf[:]), [imf], [m8])
                    yield
                    b.ve(lambda e, tm=tm, m8=m8, imf=imf: e.match_replace(out=tm[:], in_to_replace=m8[:], in_values=imf[:], imm_value=-3.0e9),
                         [m8, imf], [tm])
                    yield
                    b.ve(lambda e, m8b=m8b, tm=tm: e.max(out=m8b[:], in_=tm[:]), [tm], [m8b])
                    yield
                    b.ts(sn[:, 64:128], imf[:], m8b[:, 7:8], -BIG, ALU.is_lt, ALU.mult, [imf, m8b], [sn])
                    if "SELD" in b.dbg:
                        b.dma(SELD.t.ap()[g, r0:r0 + 128, :], sn[:, 64:128], [sn], [SELD], "gpsimd")
                    yield
                    b.mm(pb[7][:, 384:512], sn[:], ident[:], True, True, [sn, ident], [pb[7]])
                    yield
                    b.copy(qa[64:128, :, qsl], pb[7][64:128, 384:512].unsqueeze(1).to_broadcast([64, 8, 128]), [pb[7]], [qa])
                    yield

            def sw_gen(QT):
                qa = qA[QT % 2]; OC = OC2[QT % 2]
                b.mk.label = "selwin"
                items = []
                for hh in range(8 if NSTOP > 3 else 0):
                    for br in NBR:
                        kt_lo = 0 if br == 1 else max(0, 4 * QT - 4)
                        for kt in range(kt_lo, 4 * QT + 4):
                            items.append((hh, br, kt, kt == 4 * QT + 3))

                def issue_scores(i):
                    hh, br, kt, _ = items[i]
                    KT_ = kslc if br == 1 else kwin
                    pS = pb[i % 2]
                    ksl = slice(kt * 128, (kt + 1) * 128)
                    b.mm(pS[:], KT_[:, ksl], qa[:, hh, :], True, True, [KT_, qa], [pS])

                if items:
                    issue_scores(0)
                for i, (hh, br, kt, last) in enumerate(items):
                    b.mk.label = "selwin"
                    H = g * 8 + hh
                    V_ = VS if br == 1 else VW
                    relq0 = 512 * QT - 128 * kt
                    pS = pb[i % 2]
                    Pb_ = Pbr.next()
                    if br == 1 and relq0 >= 256:
                        b.act(Pb_[:], pS[:], AF.Exp, [pS, cfarB], [Pb_], bias=cfarB[:, H:H + 1])
                    else:
                        Ee = Eer.next()
                        b.act(Ee[:], pS[:], AF.Exp, [pS], [Ee])
                        yield
                        j0 = relq0 + 384
                        if br == 1 or j0 + 512 <= 896:
                            b.tt(Pb_[:], Ee[:], EBS[:, hh, j0:j0 + 512], ALU.mult, [Ee, EBS], [Pb_])
                        elif j0 == 896:
                            b.tt(Pb_[:], Ee[:], EBW[:, hh, :], ALU.mult, [Ee, EBW], [Pb_])
                        else:
                            n1 = 896 - j0
                            b.tt(Pb_[:, 0:n1], Ee[:, 0:n1], EBS[:, hh, j0:896], ALU.mult, [Ee, EBS], [Pb_])
                            b.tt(Pb_[:, n1:512], Ee[:, n1:512], EBW[:, hh, 0:512 - n1], ALU.mult, [Ee, EBW], [Pb_])
                    if i + 1 < len(items):
                        issue_scores(i + 1)
                    yield
                    for qs in range(4 if NSB >= 2 else 0):
                        qt = 4 * QT + qs
                        lo = 0 if br == 1 else max(0, qt - 4)
                        if kt > qt or kt < lo:
                            continue
                        b.mm(pb[2 + qs][:, 0:65], Pb_[:, qs * 128:(qs + 1) * 128], V_[:, kt, :], kt == lo, kt == qt,
                             [Pb_, V_], [pb[2 + qs]])
                    if last:
                        for qs in range(4 if NSB >= 3 else 0):
                            r_ = rs2.next()
                            b.ve(lambda e, r_=r_, qs=qs: e.reciprocal(out=r_[:], in_=pb[2 + qs][:, 64:65]), [pb[2 + qs]], [r_])
                            b.ts(OB[br][:, qs, hh * 64:(hh + 1) * 64], pb[2 + qs][:, 0:64], r_[:, 0:1], None, ALU.mult, None,
                                 [pb[2 + qs], r_], [OB[br]])
                    yield
                b.mk.label = "gate"
                for qs in range(4 if NSTOP > 4 else 0):
                    r0 = (4 * QT + qs) * 128
                    zt = zgt.next(); t1 = t1r.next(); t2 = t2r.next(); y = yr2.next()
                    b.dma(zt[:], ZG.t.ap()[r0:r0 + 128, :].rearrange("p (br c) -> p br c", br=3)[:, :, g * 512:(g + 1) * 512], [ZG], [zt])
                    if "OCD" in b.dbg:
                        obs = [OC, OB[1], OB[2]]
                        for br in range(3):
                            b.dma(OCD.t.ap()[br, r0:r0 + 128, g * 512:(g + 1) * 512], obs[br][:, qs, :], [obs[br]], [OCD], "gpsimd")
                    b.tt(t1[:], zt[:, 0, :], OC[:, qs, :], ALU.mult, [zt, OC], [t1], "gpsimd")
                    b.tt(t2[:], zt[:, 1, :], OB[1][:, qs, :], ALU.mult, [zt, OB[1]], [t2])
                    yield
                    b.tt(t1[:], t1[:], t2[:], ALU.add, [t1, t2], [t1], "gpsimd")
                    b.tt(t2[:], zt[:, 2, :], OB[2][:, qs, :], ALU.mult, [zt, OB[2]], [t2])
                    yield
                    b.tt(y[:], t1[:], t2[:], ALU.add, [t1, t2], [y])
                    b.dma(Y2.t.ap()[r0:r0 + 128, g * 512:(g + 1) * 512], y[:], [y], [Y2], "gpsimd")
                    yield

            def run_gens(gens):
                gens = list(gens)
                while gens:
                    for g_ in list(gens):
                        try:
                            next(g_)
                        except StopIteration:
                            gens.remove(g_)

            nq = NQT if NSTOP > 1 else 0
            if nq:
                run_gens([cmp_gen(0)])
            for QT in range(nq):
                gl = [sw_gen(QT)]
                if QT + 1 < nq:
                    gl.append(cmp_gen(QT + 1))
                run_gens(gl)
        b.pop()

    fg_d = b.din("final_g", [1, D])
    if "FIN2" in phases:
        b.mk.label = "FIN2"
        b.push()
        Wo2 = b.sb([128, 8, D], BF16)
        for kc in range(8):
            b.dmac(Wo2[:, kc, :], boutw_d.t.ap()[kc * 128:(kc + 1) * 128, :], [boutw_d], [Wo2])
        fgB = b.sb([128, D])
        b.dma(fgB[:], fg_d.t.ap().to_broadcast([128, D]), [fg_d], [fgB])
        y2r = b.rot(2, [128, D], BF16); ytr2 = b.rot(2, [128, 8, 128], BF16); x1r2 = b.rot(2, [128, D]); x2r = b.rot(2, [128, D])
        otr2 = b.rot(2, [128, D]); st = b.rot(4, [128, 4]); jr4 = b.rot(1, [128, D], BF16)
        for ti in range(NQT * 4):
            rows = slice(ti * 128, (ti + 1) * 128)
            y2 = y2r.next(); yt = ytr2.next(); x1t = x1r2.next(); x2 = x2r.next(); ot = otr2.next(); s = st.next(); jk = jr4.next()
            b.dma(y2[:], Y2.t.ap()[rows, :], [Y2], [y2])
            b.dma(x1t[:], X1.t.ap()[rows, :], [X1], [x1t])
            for kc in range(8):
                pbt = pb[6 + (kc % 2)]
                pv = pbt[:, 0:64].bitcast(BF16)
                b.tr(pv, y2[:, kc * 128:(kc + 1) * 128], identb[:], [y2, identb], [pbt])
                b.act(yt[:, kc, :], pv, AF.Copy, [pbt], [yt])
            for n in range(2):
                for kc in range(8):
                    b.mm(pb[n][:], yt[:, kc, :], Wo2[:, kc, n * 512:(n + 1) * 512], kc == 0, kc == 7, [yt, Wo2], [pb[n]])
                b.tt(x2[:, n * 512:(n + 1) * 512], pb[n][:], gateB[1][:, n * 512:(n + 1) * 512], ALU.mult, [pb[n], gateB[1]], [x2])
            b.tt(x2[:], x2[:], x1t[:], ALU.add, [x2, x1t], [x2], "gpsimd")
            if "X2" in b.dbg:
                b.dma(X2.t.ap()[rows, :], x2[:], [x2], [X2], "gpsimd")
            b.act(jk[:], x2[:], AF.Square, [x2], [jk, s], scale=1.0 / 32.0, accum=s[:, 0:1])
            b.act(s[:, 1:2], s[:, 0:1], AF.Ln, [s], [s], bias=1e-6)
            b.act(s[:, 2:3], s[:, 1:2], AF.Exp, [s], [s], scale=-0.5)
            b.stt(ot[:], x2[:], s[:, 2:3], fgB[:], ALU.mult, ALU.mult, [x2, s, fgB], [ot])
            b.dma(out_d.t.ap()[rows, :], ot[:], [ot], [out_d], "gpsimd")
        b.pop()

    if "FIN" in phases:
        b.push()
        fgB = b.sb([128, D])
        b.dma(fgB[:], fg_d.t.ap().to_broadcast([128, D]), [fg_d], [fgB])
        src = X1 if "C" in phases else x_d
        xr = b.rot(2, [128, D]); orr2 = b.rot(2, [128, D]); st = b.rot(4, [128, 4]); jr3 = b.rot(1, [128, D], BF16)
        for ti in range(NT):
            rows = slice(ti * 128, (ti + 1) * 128)
            xt = xr.next(); ot = orr2.next(); s = st.next(); jk = jr3.next()
            b.dma(xt[:], src.t.ap()[rows, :], [src], [xt])
            b.act(jk[:], xt[:], AF.Square, [xt], [jk, s], scale=1.0 / 32.0, accum=s[:, 0:1])
            b.act(s[:, 1:2], s[:, 0:1], AF.Ln, [s], [s], bias=1e-6)
            b.act(s[:, 2:3], s[:, 1:2], AF.Exp, [s], [s], scale=-0.5)
            b.stt(ot[:], xt[:], s[:, 2:3], fgB[:], ALU.mult, ALU.mult, [xt, s, fgB], [ot])
            b.dma(out_d.t.ap()[rows, :], ot[:], [ot], [out_d], "gpsimd")
        b.pop()

    b.mk.finish("sync")
    b.mk.emit()
    return b


def t5_bucket_np(d):
    n = np.maximum(d, 0)
    nf = np.maximum(n, 1).astype(np.float32)
    large = 16 + (np.log(nf / np.float32(16.0)) / np.float32(math.log(8.0)) * np.float32(16.0)).astype(np.int32)
    large = np.minimum(large, 31)
    return np.where(n < 16, n, large)


def nsa_consts():
    dd = np.arange(1536) - 512
    bk = t5_bucket_np(dd)
    oh = np.zeros((33, 1536), np.float32)
    for m in range(1536):
        if dd[m] < 0:
            oh[32, m] = 1.0
        else:
            oh[bk[m], m] = 1.0
    n_cmp = 255
    cells = np.arange(n_cmp)[:, None] + np.arange(2)[None, :]
    ov = (cells[:, None, :] // 4 == np.arange(64)[None, :, None]).sum(-1).astype(np.float32)
    ovp = np.zeros((256, 64), np.float32); ovp[:255] = ov
    ovl = ovp.reshape(2, 128, 64).transpose(1, 0, 2).copy()
    qi = np.arange(128)[:, None]; j = np.arange(126)[None, :]
    sp = j - 62
    curp = (qi >= 64).astype(np.int64)
    fm = np.where((sp == curp) | (sp == curp - 1), 1.0e9, np.where(sp > curp, -1.0e9, 0.0)).astype(np.float32)
    return oh, ovl, fm


def make_consts():
    i = np.arange(128)
    same = (i[:, None] // 64) == (i[None, :] // 64)
    U2 = (same & (i[:, None] <= i[None, :])).astype(np.float32)
    BON = same.astype(np.float32)
    SELA = np.repeat((i < 64).astype(np.float32)[:, None], 128, 1)
    SELB = np.repeat((i >= 64).astype(np.float32)[:, None], 128, 1)
    NMS = np.where(same & (i[:, None] > i[None, :]), 0.0, BIG).astype(np.float32)
    NMT = np.where(same & (i[None, :] >= i[:, None]), 0.0, -BIG).astype(np.float32)
    return np.concatenate([U2, BON, SELA, SELB, NMS, NMT], axis=1)


def core_inputs(inp, bi):
    f = np.ascontiguousarray
    d = {
        "x": f(inp["x"][bi]),
        "c_l": f(inp["c"][bi].reshape(8, 128).T),
        "rel_bias": f(inp["rel_bias"]),
        "ada_w": f(inp["ada_w"]),
        "ada_b": f(inp["ada_b"]),
        "norm_g_l": f(inp["norm_g"].reshape(2, 8, 128).transpose(2, 0, 1)),
        "a_in_w": f(inp["a_in_w"][0]),
        "conv_w_l": f(inp["a_conv_w"][0].reshape(4, 24, 128).transpose(2, 1, 0)),
        "a_A_log": f(inp["a_A_log"]),
        "a_dt_bias": f(inp["a_dt_bias"]),
        "a_onorm_g": f(inp["a_onorm_g"]),
        "a_out_w": f(inp["a_out_w"][0]),
        "consts": make_consts(),
        "kvg_l": f(inp["kv_norm_g"].reshape(8, 128).T),
        "final_g": f(inp["final_g"].reshape(1, D)),
        "kv_w": f(inp["kv_w"]),
        "posT": f(np.stack([inp["cmp_pos_k"].T, inp["cmp_pos_v"].T], 0)),
        "cmp_k_w1": f(inp["cmp_k_w1"]), "cmp_v_w1": f(inp["cmp_v_w1"]),
        "cmp_k_w2": f(inp["cmp_k_w2"]), "cmp_v_w2": f(inp["cmp_v_w2"]),
        "b_in_w": f(inp["b_in_w"][0]), "b_out_w": f(inp["b_out_w"][0]),
    }
    oh, ovl, fm = nsa_consts()
    d["oh_const"] = oh; d["ov_const"] = ovl; d["fmw_const"] = fm
    return d


_CACHE = {}
PHASES = ("mod", "A", "B", "C", "KV", "Z", "NSA", "FIN2")


def kernel(**inputs):
    inp = {k: np.asarray(v) for k, v in inputs.items()}
    if "nc" not in _CACHE:
        _CACHE["nc"] = build(phases=PHASES).nc
    nc = _CACHE["nc"]
    in_maps = [core_inputs(inp, bi) for bi in range(8)]
    res = run_bass_kernel_spmd(nc, in_maps, core_ids=list(range(8)))
    return np.stack([np.asarray(r["out"]).reshape(T, D) for r in res.results], 0).astype(np.float32)
```

```python
import math
from contextlib import ExitStack
import numpy as np
import concourse.bass as bass
import concourse.mybir as mybir
from concourse.bass_utils import run_bass_kernel_spmd

F32 = mybir.dt.float32
BF16 = mybir.dt.bfloat16
ALU = mybir.AluOpType
AF = mybir.ActivationFunctionType
AX = mybir.AxisListType

ENGS = ("sync", "scalar", "vector", "gpsimd", "tensor")
T = 4096
D = 1024
NT = 32
BIG = 30000.0
import os
NH = int(os.environ.get('MK_NH', '8'))
NTB = int(os.environ.get('MK_NTB', '32'))
STOP = int(os.environ.get('MK_STOP', '99'))
KIT = int(os.environ.get('MK_KIT', '5'))
VV = int(os.environ.get('MK_V', '3'))
NQT = int(os.environ.get('MK_NQT', '8'))
ANNOT = bool(os.environ.get('MK_ANNOT'))
CMPR = int(os.environ.get('MK_CMPR', '1'))
NSTOP = int(os.environ.get('MK_NSTOP', '99'))
NSB = int(os.environ.get('MK_NSB', '3'))
NBR = tuple(int(x) for x in os.environ.get('MK_NBR', '1,2').split(','))


class Tok:
    __slots__ = ("w", "r")

    def __init__(self):
        self.w = None
        self.r = []


class MK:
    NDMA = 20

    def __init__(self, nc):
        self.nc = nc
        self.q = {e: [] for e in ENGS}
        self.seen = {e: {} for e in ENGS}
        self.slots = {e: [0] * self.NDMA for e in ("sync", "scalar", "gpsimd")}
        self.rr = {e: 0 for e in ("sync", "scalar", "gpsimd")}
        self.signal = {e: set() for e in ENGS}
        self.label = ""

    def op(self, eng, fn, reads=(), writes=(), dma=False):
        q = self.q[eng]
        idx = len(q)
        waits = {}

        def need(ev):
            if ev is None:
                return
            key, val = ev
            if key[0] == "c" and key[1] == "tensor" and eng == "tensor":
                return
            if waits.get(key, -1) < val:
                waits[key] = val

        for t in reads:
            need(t.w)
        for t in writes:
            need(t.w)
            for ev in t.r:
                need(ev)
        if dma:
            rr = self.rr[eng]
            self.rr[eng] = (rr + 1) % self.NDMA
            prev = self.slots[eng][rr]
            if prev > 0:
                need((("d", eng, rr), prev))
            self.slots[eng][rr] = prev + 1
            ev = (("d", eng, rr), prev + 1)
        else:
            ev = (("c", eng), idx)
        seen = self.seen[eng]
        final = []
        for key, val in waits.items():
            if seen.get(key, -1) >= val:
                continue
            seen[key] = val
            final.append((key, val))
            if key[0] == "c":
                self.signal[key[1]].add(val)
        q.append((fn, final, ev, dma, self.label))
        for t in reads:
            t.r.append(ev)
            if len(t.r) > 64:
                t.r = t.r[-64:] if False else t.r
        for t in writes:
            t.w = ev
            t.r = []
        return ev

    def barrier(self):
        last = {}
        for e in ENGS:
            for i in range(len(self.q[e]) - 1, -1, -1):
                fn, _, ev, dma = self.q[e][i][:4]
                if fn is not None and not dma:
                    last[e] = i
                    break
        for e in ENGS:
            waits = []
            seen = self.seen[e]
            for e2, ix in last.items():
                if e2 == e:
                    continue
                key = ("c", e2)
                if seen.get(key, -1) < ix:
                    seen[key] = ix
                    waits.append((key, ix))
                    self.signal[e2].add(ix)
            for e2 in ("sync", "scalar", "gpsimd"):
                for rr, cnt in enumerate(self.slots[e2]):
                    key = ("d", e2, rr)
                    if cnt > 0 and seen.get(key, -1) < cnt:
                        seen[key] = cnt
                        waits.append((key, cnt))
            self.q[e].append((None, waits, None, False, ""))

    def finish(self, eng="sync"):
        waits = []
        for e in ("sync", "scalar", "gpsimd"):
            for rr, cnt in enumerate(self.slots[e]):
                if cnt > 0:
                    waits.append((("d", e, rr), cnt))
        self.q[eng].append((None, waits, None, False, ""))

    def emit(self):
        nc = self.nc
        csem = {e: nc.alloc_semaphore(f"c_{e}") for e in ENGS}
        dsem = {e: [nc.alloc_semaphore(f"d_{e}_{i}") for i in range(self.NDMA)]
                for e in ("sync", "scalar", "gpsimd")}
        cval = {}
        for e in ENGS:
            s = sorted(self.signal[e])
            cval[e] = {ix: n + 1 for n, ix in enumerate(s)}

        def replay(e, engobj):
            sig = self.signal[e]
            for i, (fn, waits, ev, dma, lab) in enumerate(self.q[e]):
                for key, val in waits:
                    if key[0] == "c":
                        engobj.wait_ge(csem[key[1]], cval[key[1]][val])
                    else:
                        engobj.wait_ge(dsem[key[1]][key[2]], 16 * val)
                if fn is None:
                    continue
                ins = fn(engobj)
                if ANNOT and lab:
                    ins.annotate(lab)
                if dma:
                    ins.then_inc(dsem[e][ev[0][2]], 16)
                elif i in sig:
                    ins.then_inc(csem[e], 1)

        with nc.Block() as block:
            @block.sync
            def _(eng):
                replay("sync", eng)

            @block.scalar
            def _(eng):
                replay("scalar", eng)

            @block.vector
            def _(eng):
                replay("vector", eng)

            @block.gpsimd
            def _(eng):
                replay("gpsimd", eng)

            @block.tensor
            def _(eng):
                replay("tensor", eng)


class Buf:
    def __init__(self, t):
        self.t = t
        self.k = Tok()

    def __getitem__(self, key):
        return self.t[key]


class Rot:
    def __init__(self, bufs):
        self.bufs = bufs
        self.i = 0

    def next(self):
        b = self.bufs[self.i % len(self.bufs)]
        self.i += 1
        return b


class Bld:
    def __init__(self, dbg=()):
        self.nc = bass.Bass("TRN2", target_bir_lowering=False)
        self.mk = MK(self.nc)
        self.dbg = set(dbg)
        self.n = 0
        self.scopes = [ExitStack()]

    def sb(self, shape, dt=F32, name=None):
        self.n += 1
        return Buf(self.scopes[-1].enter_context(self.nc.sbuf_tensor(name or f"sb{self.n}", list(shape), dt)))

    def push(self):
        self.scopes.append(ExitStack())

    def pop(self):
        self.mk.barrier()
        self.scopes.pop().close()

    def rot(self, n, shape, dt=F32):
        return Rot([self.sb(shape, dt) for _ in range(n)])

    def din(self, name, shape, dt=F32):
        return Buf(self.nc.dram_tensor(name, list(shape), dt, kind="ExternalInput"))

    def dscr(self, name, shape, dt=F32, out=False):
        kind = "ExternalOutput" if (out or name in self.dbg) else "Internal"
        if ("REFIN:" + name) in self.dbg:
            kind = "ExternalInput"
        return Buf(self.nc.dram_tensor(name, list(shape), dt, kind=kind))

    def dma(self, out, in_, reads, writes, eng=None, slow=False):
        if eng is None:
            eng = "sync"
        if slow:
            fn = lambda e: e.dma_start(out=out, in_=in_, allow_slow_non_contiguous=True)
        else:
            fn = lambda e: e.dma_start(out=out, in_=in_)
        self.mk.op(eng, fn, reads=[b.k for b in reads], writes=[b.k for b in writes], dma=True)

    def dmac(self, out, in_, reads, writes):
        self.dma(out, in_, reads, writes, eng="gpsimd")

    def mm(self, out, lhsT, rhs, start, stop, reads, writes):
        self.mk.op("tensor", lambda e: e.matmul(out, lhsT=lhsT, rhs=rhs, start=start, stop=stop),
                   reads=[b.k for b in reads], writes=[b.k for b in writes])

    def tr(self, out, in_, ident, reads, writes):
        self.mk.op("tensor", lambda e: e.transpose(out, in_, ident),
                   reads=[b.k for b in reads], writes=[b.k for b in writes])

    def act(self, out, in_, func, reads, writes, bias=None, scale=None, accum=None):
        kw = {}
        if bias is not None:
            kw["bias"] = bias
        if scale is not None:
            kw["scale"] = scale
        if accum is not None:
            kw["accum_out"] = accum
        self.mk.op("scalar", lambda e: e.activation(out=out, in_=in_, func=func, **kw),
                   reads=[b.k for b in reads], writes=[b.k for b in writes])

    def ve(self, fn, reads, writes, eng="vector"):
        self.mk.op(eng, fn, reads=[b.k for b in reads], writes=[b.k for b in writes])

    def copy(self, out, in_, reads, writes, eng="vector"):
        self.ve(lambda e: e.tensor_copy(out=out, in_=in_), reads, writes, eng)

    def tt(self, out, a, b_, op, reads, writes, eng="vector"):
        self.ve(lambda e: e.tensor_tensor(out=out, in0=a, in1=b_, op=op), reads, writes, eng)

    def ts(self, out, a, s1, s2, op0, op1, reads, writes, eng="vector"):
        if op1 is None:
            self.ve(lambda e: e.tensor_scalar(out=out, in0=a, scalar1=s1, scalar2=None, op0=op0), reads, writes, eng)
        else:
            self.ve(lambda e: e.tensor_scalar(out=out, in0=a, scalar1=s1, scalar2=s2, op0=op0, op1=op1), reads, writes, eng)

    def stt(self, out, a, s, b_, op0, op1, reads, writes, eng="vector"):
        self.ve(lambda e: e.scalar_tensor_tensor(out=out, in0=a, scalar=s, in1=b_, op0=op0, op1=op1), reads, writes, eng)

    def memset(self, ap, val, writes, eng="gpsimd"):
        self.ve(lambda e: e.memset(ap, val), [], writes, eng)


def build(dbg=(), phases=("mod", "A", "B", "C", "KV", "Z", "NSA", "FIN")):
    b = Bld(dbg)
    nc = b.nc
    P = {}
    x_d = b.din("x", [T, D])
    cl_d = b.din("c_l", [128, 8])
    relb_d = b.din("rel_bias", [32, 16])
    adaw_d = b.din("ada_w", [2, D, 3 * D])
    adab_d = b.din("ada_b", [2, 3 * D])
    ng_d = b.din("norm_g_l", [128, 2, 8])
    ainw_d = b.din("a_in_w", [D, 4112])
    convw_d = b.din("conv_w_l", [128, 24, 4])
    alog_d = b.din("a_A_log", [1, 8])
    dtb_d = b.din("a_dt_bias", [1, 8])
    aong_d = b.din("a_onorm_g", [1, 128])
    aoutw_d = b.din("a_out_w", [D, D])
    cst_d = b.din("consts", [128, 6 * 128])
    out_d = b.dscr("out", [T, D], F32, out=True)

    ident = b.sb([128, 128]); identb = b.sb([128, 128], BF16)
    ones = b.sb([128, 128])
    cst = b.sb([128, 6 * 128])
    b.memset(ident[:], 1.0, [ident])
    b.ve(lambda e: e.affine_select(out=ident[:], in_=ident[:], pattern=[[-1, 128]], compare_op=ALU.is_equal,
                                   fill=0.0, base=0, channel_multiplier=1), [ident], [ident], "gpsimd")
    b.copy(identb[:], ident[:], [ident], [identb])
    b.memset(ones[:], 1.0, [ones])
    b.dma(cst[:], cst_d.t.ap(), [cst_d], [cst])
    U2 = cst[:, 0:128]; BONES = cst[:, 128:256]; SELA = cst[:, 256:384]; SELB = cst[:, 384:512]
    NMS = cst[:, 512:640]; NMT = cst[:, 640:768]

    pb = [Buf(nc.alloc_psum_tensor(f"pb{i}", [128, 512], F32)) for i in range(8)]

    gateB = [b.sb([128, D]) for _ in range(2)]
    modcol = [b.sb([128, 24]) for _ in range(2)]
    Acol = [b.sb([128, 8]) for _ in range(2)]
    ng = b.sb([128, 2, 8])
    if "mod" in phases:
        b.mk.label = "mod"
        b.push()
        modB = b.sb([128, 3 * D])
        cs = b.sb([128, 8]); csb = b.sb([128, 8, 128])
        adabB = b.sb([128, 3 * D])
        awr = b.rot(2, [128, 3 * D])
        b.dma(cs[:], cl_d.t.ap(), [cl_d], [cs])
        b.dma(ng[:], ng_d.t.ap(), [ng_d], [ng])
        b.act(cs[:], cs[:], AF.Silu, [cs], [cs])
        for kc in range(8):
            b.copy(csb[:, kc, :], cs[:, kc:kc + 1].to_broadcast([128, 128]), [cs], [csb])
        for l in range(2):
            b.dma(adabB[:], adab_d.t.ap()[l:l + 1, :].to_broadcast([128, 3 * D]), [adab_d], [adabB])
            for kc in range(8):
                aw = awr.next()
                b.dma(aw[:], adaw_d.t.ap()[l, kc * 128:(kc + 1) * 128, :], [adaw_d], [aw])
                for n in range(6):
                    b.mm(pb[n][:], csb[:, kc, :], aw[:, n * 512:(n + 1) * 512], kc == 0, kc == 7, [csb, aw], [pb[n]])
            for n in range(6):
                b.tt(modB[:, n * 512:(n + 1) * 512], pb[n][:], adabB[:, n * 512:(n + 1) * 512], ALU.add,
                     [pb[n], adabB], [modB])
            for j in range(24):
                b.mm(pb[6][:, j:j + 1], modB[0:1, j * 128:(j + 1) * 128], ones[0:1, 0:1], True, True,
                     [modB, ones], [pb[6]])
            b.copy(modcol[l][:], pb[6][:, 0:24], [pb[6]], [modcol[l]])
            b.copy(gateB[l][:], modB[:, 2 * D:3 * D], [modB], [gateB[l]])
            b.stt(Acol[l][:], modcol[l][:, 8:16], 1.0, ng[:, l, :], ALU.add, ALU.mult, [modcol[l], ng], [Acol[l]])
            if l == 0 and "modB0" in b.dbg:
                dd = b.dscr("modB0", [128, 3 * D])
                b.dma(dd.t.ap(), modB[:], [modB], [dd])
                dd2 = b.dscr("modcol0", [128, 24])
                b.dma(dd2.t.ap(), modcol[0][:], [modcol[0]], [dd2])
        b.pop()

    QT = b.dscr("QT", [8, 128, T], BF16)
    KT = b.dscr("KT", [8, 128, T], BF16)
    VT = b.dscr("VT", [8, 128, T], BF16)
    ZS = b.dscr("ZS", [T, D], F32)
    GB = b.dscr("GB", [128, NT, 16], F32)

    g_all = b.sb([128, NT, 8]); beta_all = b.sb([128, NT, 8])

    def silu_from(dst, src, e_buf, reads, eng2="gpsimd"):
        b.act(e_buf[:], src, AF.Exp, reads, [e_buf], scale=-1.0)
        b.ts(e_buf[:], e_buf[:], 1.0, None, ALU.add, None, [e_buf], [e_buf])
        b.ve(lambda e: e.reciprocal(out=e_buf[:], in_=e_buf[:]), [e_buf], [e_buf])
        return e_buf

    if "A" in phases:
        b.mk.label = "A"
        b.push()
        Win = b.sb([128, 8, 4112], BF16)
        for kc in range(8):
            b.dmac(Win[:, kc, :], ainw_d.t.ap()[kc * 128:(kc + 1) * 128, :], [ainw_d], [Win])
        convw = b.sb([128, 24, 4])
        b.dma(convw[:], convw_d.t.ap(), [convw_d], [convw])
        alogB = b.sb([128, 8]); dtbB = b.sb([128, 8]); negA = b.sb([128, 8])
        b.dma(alogB[:], alog_d.t.ap().to_broadcast([128, 8]), [alog_d], [alogB])
        b.dma(dtbB[:], dtb_d.t.ap().to_broadcast([128, 8]), [dtb_d], [dtbB])
        b.act(negA[:], alogB[:], AF.Exp, [alogB], [negA])
        b.ts(negA[:], negA[:], -1.0, None, ALU.mult, None, [negA], [negA])
        pre = b.sb([128, 24, 515])
        b.memset(pre[:], 0.0, [pre])
        hT = b.rot(2, [128, 8, 512], BF16)
        xr = b.rot(2, [128, D]); xnr = b.rot(2, [128, D], BF16)
        st = b.rot(4, [128, 4])
        accr = b.rot(3, [128, 512]); er = b.rot(3, [128, 512]); sr = b.rot(3, [128, 512])
        sqr = b.rot(3, [128, 512]); rir = b.rot(3, [128, 512])
        obr = b.rot(4, [128, 512], BF16)
        zr = b.rot(2, [128, D]); ezr = b.rot(2, [128, 512]); bar = b.rot(2, [128, 16])
        for blk in range(8):
            h = hT.next()
            for tt_ in range(4):
                ti = blk * 4 + tt_
                xt = xr.next(); xn = xnr.next(); s = st.next()
                b.dma(xt[:], x_d.t.ap()[ti * 128:(ti + 1) * 128, :], [x_d], [xt])
                b.act(xn[:], xt[:], AF.Square, [xt], [xn, s], scale=1.0 / 32.0, accum=s[:, 0:1])
                b.act(s[:, 1:2], s[:, 0:1], AF.Ln, [s], [s], bias=1e-6)
                b.act(s[:, 2:3], s[:, 1:2], AF.Exp, [s], [s], scale=-0.5)
                b.ts(xn[:], xt[:], s[:, 2:3], None, ALU.mult, None, [xt, s], [xn])
                for kc in range(8):
                    pbt = pb[6 + (kc % 2)]
                    pv = pbt[:, 0:64].bitcast(BF16)
                    b.tr(pv, xn[:, kc * 128:(kc + 1) * 128], identb[:], [xn, identb], [pbt])
                    b.act(h[:, kc, tt_ * 128:(tt_ + 1) * 128], pv, AF.Identity, [pbt, Acol[0], modcol[0]], [h],
                          bias=modcol[0][:, kc:kc + 1], scale=Acol[0][:, kc:kc + 1])
            def chunk_gen(oc, h=h, blk=blk):
                hh = oc % 8
                pbt = pb[oc % 4]
                for kc in range(8):
                    b.mm(pbt[:], Win[:, kc, oc * 128:(oc + 1) * 128], h[:, kc, :], kc == 0, kc == 7, [Win, h], [pbt])
                yield
                b.act(pre[:, oc, 3:515], pbt[:], AF.Copy, [pbt], [pre])
                yield
                acc = accr.next()
                b.ts(acc[:], pre[:, oc, 0:512], convw[:, oc, 0:1], None, ALU.mult, None, [pre, convw], [acc])
                for k in range(1, 4):
                    b.stt(acc[:], pre[:, oc, k:k + 512], convw[:, oc, k:k + 1], acc[:], ALU.mult, ALU.add,
                          [pre, convw, acc], [acc])
                yield
                b.copy(pre[:, oc, 0:3], pre[:, oc, 512:515], [pre], [pre], "gpsimd")
                e_ = er.next()
                b.act(e_[:], acc[:], AF.Exp, [acc], [e_], scale=-1.0)
                yield
                b.ts(e_[:], e_[:], 1.0, None, ALU.add, None, [e_], [e_])
                b.ve(lambda e, e_=e_: e.reciprocal(out=e_[:], in_=e_[:]), [e_], [e_])
                yield
                sv = sr.next()
                b.tt(sv[:], acc[:], e_[:], ALU.mult, [acc, e_], [sv], "gpsimd")
                ob = obr.next()
                if oc < 16:
                    sq = sqr.next(); ri = rir.next()
                    b.tt(sq[:], sv[:], sv[:], ALU.mult, [sv], [sq], "gpsimd")
                    yield
                    pb2 = pb[4 + (oc % 2)]
                    b.mm(pb2[:], ones[:], sq[:], True, True, [ones, sq], [pb2])
                    yield
                    b.act(ri[:], pb2[:], AF.Ln, [pb2], [ri], bias=1e-6)
                    yield
                    b.act(ri[:], ri[:], AF.Exp, [ri], [ri], scale=-0.5)
                    yield
                    if oc < 8:
                        b.stt(ob[:], sv[:], 128.0 ** -0.5, ri[:], ALU.mult, ALU.mult, [sv, ri], [ob])
                    else:
                        b.tt(ob[:], sv[:], ri[:], ALU.mult, [sv, ri], [ob])
                    dst = QT if oc < 8 else KT
                else:
                    yield
                    b.copy(ob[:], sv[:], [sv], [ob], "gpsimd")
                    dst = VT
                yield
                b.dma(dst.t.ap()[hh, :, blk * 512:(blk + 1) * 512], ob[:], [ob], [])

            pend = list(range(24)); live = []
            while pend or live:
                while pend and len(live) < 2:
                    live.append(chunk_gen(pend.pop(0)))
                for g_ in list(live):
                    try:
                        next(g_)
                    except StopIteration:
                        live.remove(g_)
            for tt_ in range(4):
                ti = blk * 4 + tt_
                z = zr.next(); ba = bar.next()
                for n in range(2):
                    pbt = pb[4 + n]
                    for kc in range(8):
                        b.mm(pbt[:], h[:, kc, tt_ * 128:(tt_ + 1) * 128], Win[:, kc, 3072 + n * 512:3072 + (n + 1) * 512],
                             kc == 0, kc == 7, [h, Win], [pbt])
                    e_ = silu_from(None, pbt[:], ezr.next(), [pbt])
                    b.tt(z[:, n * 512:(n + 1) * 512], pbt[:], e_[:], ALU.mult, [pbt, e_], [z])
                pbt = pb[6]
                for kc in range(8):
                    b.mm(pbt[:, 0:16], h[:, kc, tt_ * 128:(tt_ + 1) * 128], Win[:, kc, 4096:4112], kc == 0, kc == 7, [h, Win], [pbt])
                b.copy(ba[:], pbt[:, 0:16], [pbt], [ba])
                b.dma(ZS.t.ap()[ti * 128:(ti + 1) * 128, :], z[:], [z], [])
                b.copy(beta_all[:, ti, :], ba[:, 0:8], [ba], [beta_all])
                b.tt(g_all[:, ti, :], ba[:, 8:16], dtbB[:], ALU.add, [ba, dtbB], [g_all])
        bf_ = beta_all[:].rearrange("p t h -> p (t h)")
        b.act(bf_, bf_, AF.Exp, [beta_all], [beta_all], scale=-1.0)
        b.ts(bf_, bf_, 1.0, None, ALU.add, None, [beta_all], [beta_all])
        b.ve(lambda e: e.reciprocal(out=bf_, in_=bf_), [beta_all], [beta_all])
        gf = g_all[:].rearrange("p t h -> p (t h)")
        b.act(gf, gf, AF.Exp, [g_all], [g_all])
        b.act(gf, gf, AF.Ln, [g_all], [g_all], bias=1.0)
        b.tt(g_all[:], g_all[:], negA[:].unsqueeze(1).to_broadcast([128, NT, 8]), ALU.mult, [g_all, negA], [g_all])
        if "GB" in b.dbg:
            b.dma(GB.t.ap()[:, :, 0:8], g_all[:], [g_all], [GB])
            b.dma(GB.t.ap()[:, :, 8:16], beta_all[:], [beta_all], [GB])
        b.pop()

    if "A" not in phases:
        b.push()
        b.memset(g_all[:], -0.05, [g_all]); b.memset(beta_all[:], 0.5, [beta_all])
        zt = b.sb([128, T], BF16); zt2 = b.sb([128, D])
        b.memset(zt[:], 0.01, [zt]); b.memset(zt2[:], 0.5, [zt2])
        for h_ in range(8):
            for dd_ in (QT, KT, VT):
                b.dma(dd_.t.ap()[h_], zt[:], [zt], [dd_])
        for ti in range(NT):
            b.dma(ZS.t.ap()[ti * 128:(ti + 1) * 128, :], zt2[:], [zt2], [ZS])
        b.pop()

    YT = b.dscr("YT", [8, 128, T], BF16)
    OD = b.dscr("OD", [8, T, 128], F32)
    if "B" in phases:
        b.mk.label = "B"
        b.push()
        sc = {n: b.sb([128, NT * 8]) for n in ("gc", "ngc", "gam", "bG", "kap", "nbeta", "GlA", "GlB")}
        gflat = g_all[:].rearrange("p t h -> p (t h)")
        bflat = beta_all[:].rearrange("p t h -> p (t h)")
        for i_, (lh, nm) in enumerate(((U2, "gc"), (BONES, "kap"), (SELA, "GlA"), (SELB, "GlB"))):
            b.mm(pb[i_][:, 0:256], lh, gflat, True, True, [cst, g_all], [pb[i_]])
            b.copy(sc[nm][:], pb[i_][:, 0:256], [pb[i_]], [sc[nm]])
        b.tt(sc["kap"][:], sc["kap"][:], sc["gc"][:], ALU.subtract, [sc["kap"], sc["gc"]], [sc["kap"]])
        b.act(sc["kap"][:], sc["kap"][:], AF.Exp, [sc["kap"]], [sc["kap"]])
        b.act(sc["GlA"][:], sc["GlA"][:], AF.Exp, [sc["GlA"]], [sc["GlA"]])
        b.act(sc["GlB"][:], sc["GlB"][:], AF.Exp, [sc["GlB"]], [sc["GlB"]])
        b.act(sc["gam"][:], sc["gc"][:], AF.Exp, [sc["gc"]], [sc["gam"]])
        b.ts(sc["ngc"][:], sc["gc"][:], -1.0, None, ALU.mult, None, [sc["gc"]], [sc["ngc"]])
        b.ts(sc["nbeta"][:], bflat, -1.0, None, ALU.mult, None, [beta_all], [sc["nbeta"]])
        b.tt(sc["bG"][:], bflat, sc["gam"][:], ALU.mult, [beta_all, sc["gam"]], [sc["bG"]])
        ongB = b.sb([128, 128])
        b.dma(ongB[:], aong_d.t.ap().to_broadcast([128, 128]), [aong_d], [ongB])
        class Reg:
            def __init__(self, bank, lo, hi, bf=False, shared=False):
                self.ap = bank.t[:, lo:hi].bitcast(BF16) if bf else bank.t[:, lo:hi]
                self.k = bank.k if shared else Tok()

        def mkctx(hp):
            bb = 4 * hp
            c = {}
            X1_, X2_, Y1_, Y2_ = pb[bb], pb[bb + 1], pb[bb + 2], pb[bb + 3]
            c["RD"] = Reg(X1_, 0, 128, False, True); c["RT"] = Reg(X1_, 128, 256, False, True)
            c["Nt"] = Reg(X1_, 256, 384, False, True); c["N2"] = Reg(X1_, 384, 512, False, True)
            c["Nt2"] = Reg(X2_, 0, 128, False, True); c["kt"] = Reg(X2_, 128, 192, True, True); c["vt"] = Reg(X2_, 192, 256, True, True)
            c["yt"] = Reg(X2_, 256, 320, True, True); c["U"] = Reg(X2_, 320, 448, False, True)
            c["KK"] = Reg(Y1_, 0, 128); c["QK"] = Reg(Y1_, 128, 256); c["A"] = Reg(Y1_, 256, 384); c["P1"] = Reg(Y1_, 384, 512)
            c["O1"] = Reg(Y2_, 0, 128); c["O2"] = Reg(Y2_, 128, 256); c["S"] = Reg(Y2_, 256, 384); c["W"] = Reg(Y2_, 384, 512)
            c["qT"] = b.sb([128, T], BF16); c["kT"] = b.sb([128, T], BF16); c["vT"] = b.sb([128, T], BF16)
            c["zs"] = b.sb([128, NT, 128]); c["yT"] = b.sb([128, T], BF16)
            c["Sf"] = b.sb([128, 128]); c["Sb"] = b.sb([128, 128], BF16)
            for nm in ("dg", "D", "DT", "u", "tmp", "o", "gz", "jk"):
                c[nm] = b.rot(2, [128, 128])
            for nm in ("N", "Ntb", "Acc"):
                c[nm] = b.rot(3, [128, 128])
            for nm in ("TT", "att", "kbg", "kde", "vb", "wT", "vn", "y"):
                c[nm] = b.rot(2, [128, 128], BF16)
            c["s4"] = b.rot(4, [128, 4])
            return c

        def head_gen(c, h):
            qT, kT, vT, zs, yT, S, Sb = c["qT"], c["kT"], c["vT"], c["zs"], c["yT"], c["Sf"], c["Sb"]
            b.dma(qT[:], QT.t.ap()[h], [QT], [qT]); b.dma(kT[:], KT.t.ap()[h], [KT], [kT]); b.dma(vT[:], VT.t.ap()[h], [VT], [vT])
            for q4 in range(4):
                b.dma(zs[:, q4 * 8:(q4 + 1) * 8, :],
                      ZS.t.ap()[q4 * 1024:(q4 + 1) * 1024, h * 128:(h + 1) * 128].rearrange("(t p) v -> p t v", p=128), [ZS], [zs])
            b.memset(S[:], 0.0, [S]); b.memset(Sb[:], 0.0, [Sb])
            yield
            for t in range(NTB):
                cols = slice(t * 128, (t + 1) * 128)
                th = slice(t * 8 + h, t * 8 + h + 1)
                KK, QK, RD, RT = c["KK"], c["QK"], c["RD"], c["RT"]
                b.mm(KK.ap, kT[:, cols], kT[:, cols], True, True, [kT], [KK])
                b.mm(QK.ap, kT[:, cols], qT[:, cols], True, True, [kT, qT], [QK])
                dg = c["dg"].next()
                b.ts(dg[:], ident[:], sc["gc"][:, th], None, ALU.mult, None, [ident, sc["gc"]], [dg])
                yield
                b.mm(RD.ap, ones[:], dg[:], True, False, [ones, dg], [RD])
                b.mm(RD.ap, ident[:], NMS, False, True, [ident, cst], [RD])
                b.mm(RT.ap, ones[:], dg[:], True, False, [ones, dg], [RT])
                b.mm(RT.ap, ident[:], NMT, False, True, [ident, cst], [RT])
                yield
                Dm = c["D"].next(); DTm = c["DT"].next()
                b.act(Dm[:], RD.ap, AF.Exp, [RD, sc["gc"]], [Dm], bias=sc["gc"][:, th], scale=-1.0)
                b.act(DTm[:], RT.ap, AF.Exp, [RT, sc["ngc"]], [DTm], bias=sc["ngc"][:, th], scale=1.0)
                yield
                N = c["N"].next()
                b.stt(N[:], KK.ap, sc["nbeta"][:, th], Dm[:], ALU.mult, ALU.mult, [KK, sc["nbeta"], Dm], [N])
                att = c["att"].next()
                b.tt(att[:], QK.ap, DTm[:], ALU.mult, [QK, DTm], [att])
                yield
                pNt, pA, pN2, pNt2 = c["Nt"], c["A"], c["N2"], c["Nt2"]
                b.mm(pNt.ap, N[:], ident[:], True, True, [N, ident], [pNt])
                Nt = c["Ntb"].next(); Acc = c["Acc"].next()
                yield
                b.act(Nt[:], pNt.ap, AF.Copy, [pNt], [Nt])
                yield
                b.tt(Acc[:], Nt[:], ident[:], ALU.add, [Nt, ident], [Acc])
                TTb = c["TT"].next()
                for k in range(1, 6):
                    N2 = c["N"].next()
                    b.mm(pN2.ap, Nt[:], N[:], True, True, [Nt, N], [pN2])
                    if k < 5:
                        Nt2 = c["Ntb"].next()
                        b.mm(pNt2.ap, N[:], Nt[:], True, True, [N, Nt], [pNt2])
                    yield
                    b.act(N2[:], pN2.ap, AF.Copy, [pN2], [N2])
                    if k < 5:
                        b.act(Nt2[:], pNt2.ap, AF.Copy, [pNt2], [Nt2])
                    yield
                    b.mm(pA.ap, N2[:], Acc[:], True, True, [N2, Acc], [pA])
                    yield
                    if k < 5:
                        Acc2 = c["Acc"].next()
                        b.tt(Acc2[:], pA.ap, Acc[:], ALU.add, [pA, Acc], [Acc2])
                        Acc = Acc2; Nt = Nt2
                    else:
                        b.tt(TTb[:], pA.ap, Acc[:], ALU.add, [pA, Acc], [TTb])
                    N = N2
                    yield
                pkt, pvt, pyt = c["kt"], c["vt"], c["yt"]
                b.tr(pkt.ap, kT[:, cols], identb[:], [kT, identb], [pkt])
                b.tr(pvt.ap, vT[:, cols], identb[:], [vT, identb], [pvt])
                yield
                kbg = c["kbg"].next(); kde = c["kde"].next(); vb = c["vb"].next()
                b.act(kbg[:], pkt.ap, AF.Identity, [pkt, sc["bG"]], [kbg], scale=sc["bG"][:, th])
                b.act(kde[:], pkt.ap, AF.Identity, [pkt, sc["kap"]], [kde], scale=sc["kap"][:, th])
                b.act(vb[:], pvt.ap, AF.Identity, [pvt, beta_all], [vb], scale=bflat[:, th])
                yield
                pU, pW = c["U"], c["W"]
                u = c["u"].next(); wT = c["wT"].next()
                b.mm(pU.ap, TTb[:], vb[:], True, True, [TTb, vb], [pU])
                b.mm(pW.ap, kbg[:], TTb[:], True, True, [kbg, TTb], [pW])
                yield
                b.act(u[:], pU.ap, AF.Copy, [pU], [u])
                b.copy(wT[:], pW.ap, [pW], [wT])
                yield
                vn = c["vn"].next(); o = c["o"].next()
                pP1, pO1, pO2, pS = c["P1"], c["O1"], c["O2"], c["S"]
                for hf in range(2):
                    rows = slice(hf * 64, hf * 64 + 64)
                    b.mm(pP1.ap, wT[:], Sb[:], True, True, [wT, Sb], [pP1])
                    b.mm(pO1.ap, qT[:, cols], Sb[:], True, True, [qT, Sb], [pO1])
                    yield
                    b.tt(vn[rows, :], u[rows, :], pP1.ap[rows, :], ALU.subtract, [u, pP1], [vn])
                    yield
                    b.mm(pS.ap, kde[rows, :], vn[rows, :], True, True, [kde, vn], [pS])
                    b.mm(pO2.ap, att[rows, :], vn[rows, :], True, True, [att, vn], [pO2])
                    yield
                    gl = sc["GlA"] if hf == 0 else sc["GlB"]
                    b.stt(S[:], S[:], gl[:, th], pS.ap, ALU.mult, ALU.add, [S, gl, pS], [S])
                    yield
                    b.act(Sb[:], S[:], AF.Copy, [S], [Sb])
                    tmp = c["tmp"].next()
                    b.copy(tmp[rows, :], pO2.ap[rows, :], [pO2], [tmp])
                    b.stt(o[rows, :], pO1.ap[rows, :], sc["gam"][rows, th], tmp[rows, :], ALU.mult, ALU.add,
                          [pO1, sc["gam"], tmp], [o])
                    yield
                if "OD" in b.dbg:
                    b.dma(OD.t.ap()[h, t * 128:(t + 1) * 128, :], o[:], [o], [OD], "gpsimd")
                s = c["s4"].next(); jk = c["jk"].next(); gz = c["gz"].next(); y = c["y"].next()
                b.act(jk[:], o[:], AF.Square, [o], [jk, s], scale=128.0 ** -0.5, accum=s[:, 0:1])
                b.tt(gz[:], zs[:, t, :], ongB[:], ALU.mult, [zs, ongB], [gz], "gpsimd")
                yield
                b.act(s[:, 1:2], s[:, 0:1], AF.Ln, [s], [s], bias=1e-6)
                yield
                b.act(s[:, 2:3], s[:, 1:2], AF.Exp, [s], [s], scale=-0.5)
                yield
                b.stt(y[:], o[:], s[:, 2:3], gz[:], ALU.mult, ALU.mult, [o, s, gz], [y])
                yield
                b.tr(pyt.ap, y[:], identb[:], [y, identb], [pyt])
                yield
                b.act(yT[:, cols], pyt.ap, AF.Copy, [pyt], [yT])
                yield
            b.dma(YT.t.ap()[h], yT[:], [yT], [], "gpsimd")

        ctxs = [mkctx(0), mkctx(1)]
        b.mk.barrier()
        for hp in range(0, NH, 2):
            gens = [head_gen(ctxs[i], hp + i) for i in range(2) if hp + i < NH]
            while gens:
                for g_ in list(gens):
                    try:
                        next(g_)
                    except StopIteration:
                        gens.remove(g_)
        b.pop()

    X1 = b.dscr("X1", [T, D], F32)
    X2 = b.dscr("X2", [T, D], F32)
    ST = b.dscr("ST", [8, 128, T], BF16)
    H1T = b.dscr("H1T", [8, 128, T], BF16)
    kvg_d = b.din("kvg_l", [128, 8])
    if "C" in phases:
        b.mk.label = "C"
        b.push()
        Wo = b.sb([128, 8, D], BF16)
        for kc in range(8):
            b.dmac(Wo[:, kc, :], aoutw_d.t.ap()[kc * 128:(kc + 1) * 128, :], [aoutw_d], [Wo])
        kvg = b.sb([128, 8])
        b.dma(kvg[:], kvg_d.t.ap(), [kvg_d], [kvg])
        ytl = b.rot(2, [128, 8, 128], BF16); xr = b.rot(2, [128, D]); x1r = b.rot(2, [128, D])
        xnr = b.rot(2, [128, D], BF16); st = b.rot(4, [128, 4]); jr2 = b.rot(1, [128, D], BF16)
        sTt = b.rot(2, [128, 8, 128], BF16); hTt = b.rot(2, [128, 8, 128], BF16)
        for ti in range(NT):
            rows = slice(ti * 128, (ti + 1) * 128)
            yt = ytl.next(); xt = xr.next(); x1 = x1r.next(); xn = xnr.next(); s = st.next()
            b.dma(yt[:], YT.t.ap()[:, :, rows].rearrange("h v t -> v h t"), [YT], [yt])
            b.dma(xt[:], x_d.t.ap()[rows, :], [x_d], [xt])
            for n in range(2):
                for kc in range(8):
                    b.mm(pb[n][:], yt[:, kc, :], Wo[:, kc, n * 512:(n + 1) * 512], kc == 0, kc == 7, [yt, Wo], [pb[n]])
                b.tt(x1[:, n * 512:(n + 1) * 512], pb[n][:], gateB[0][:, n * 512:(n + 1) * 512], ALU.mult, [pb[n], gateB[0]], [x1])
            b.tt(x1[:], x1[:], xt[:], ALU.add, [x1, xt], [x1], "gpsimd")
            b.dma(X1.t.ap()[rows, :], x1[:], [x1], [], "gpsimd")
            jk = jr2.next()
            b.act(jk[:], x1[:], AF.Square, [x1], [jk, s], scale=1.0 / 32.0, accum=s[:, 0:1])
            b.act(s[:, 1:2], s[:, 0:1], AF.Ln, [s], [s], bias=1e-6)
            b.act(s[:, 2:3], s[:, 1:2], AF.Exp, [s], [s], scale=-0.5)
            b.ts(xn[:], x1[:], s[:, 2:3], None, ALU.mult, None, [x1, s], [xn])
            sT_ = sTt.next(); hT_ = hTt.next()
            for kc in range(8):
                pbt = pb[6 + (kc % 2)]
                pv = pbt[:, 0:64].bitcast(BF16)
                b.tr(pv, xn[:, kc * 128:(kc + 1) * 128], identb[:], [xn, identb], [pbt])
                b.act(hT_[:, kc, :], pv, AF.Identity, [pbt, Acol[1], modcol[1]], [hT_],
                      bias=modcol[1][:, kc:kc + 1], scale=Acol[1][:, kc:kc + 1])
                b.act(sT_[:, kc, :], pv, AF.Identity, [pbt, kvg], [sT_], scale=kvg[:, kc:kc + 1])
            b.dma(ST.t.ap()[:, :, rows].rearrange("k p t -> p k t"), sT_[:], [sT_], [], "gpsimd")
            b.dma(H1T.t.ap()[:, :, rows].rearrange("k p t -> p k t"), hT_[:], [hT_], [], "gpsimd")
        b.pop()

    kvw_d = b.din("kv_w", [D, 768])
    posT_d = b.din("posT", [2, 64, 32])
    w1_d = [b.din("cmp_k_w1", [2048, 256]), b.din("cmp_v_w1", [2048, 256])]
    w2_d = [b.din("cmp_k_w2", [256, 64]), b.din("cmp_v_w2", [256, 64])]
    KVT = b.dscr("KVT", [6, 128, T], BF16)
    VTOK = b.dscr("VTOK", [2, T, 128], BF16)
    KCT = b.dscr("KCT", [2, 2, 64, 256], F32)
    VCT = b.dscr("VCT", [2, 2, 256, 64], F32)
    if "KV" in phases:
        b.mk.label = "KV"
        b.push()
        kvw = b.sb([128, 8, 768], BF16)
        for kc in range(8):
            b.dmac(kvw[:, kc, :], kvw_d.t.ap()[kc * 128:(kc + 1) * 128, :], [kvw_d], [kvw])
        sTr = b.rot(2, [128, 8, 512], BF16); ocr = b.rot(3, [128, 512], BF16); otr = b.rot(3, [128, 128], BF16)
        for blk in range(8):
            cols = slice(blk * 512, (blk + 1) * 512)
            sTb = sTr.next()
            b.dma(sTb[:], ST.t.ap()[:, :, cols].rearrange("k p t -> p k t"), [ST], [sTb])
            for i in range(6):
                pbt = pb[i % 4]
                for kc in range(8):
                    b.mm(pbt[:], kvw[:, kc, i * 128:(i + 1) * 128], sTb[:, kc, :], kc == 0, kc == 7, [kvw, sTb], [pbt])
                oc_ = ocr.next()
                b.act(oc_[:], pbt[:], AF.Copy, [pbt], [oc_])
                b.dma(KVT.t.ap()[i, :, cols], oc_[:], [oc_], [], "gpsimd")
            for j, i in enumerate((3, 5)):
                for tt_ in range(4):
                    pbt = pb[4 + (tt_ % 2)]
                    for kc in range(8):
                        b.mm(pbt[:, 0:128], sTb[:, kc, tt_ * 128:(tt_ + 1) * 128], kvw[:, kc, i * 128:(i + 1) * 128],
                             kc == 0, kc == 7, [sTb, kvw], [pbt])
                    ot_ = otr.next()
                    b.copy(ot_[:], pbt[:, 0:128], [pbt], [ot_])
                    r0 = blk * 512 + tt_ * 128
                    b.dma(VTOK.t.ap()[j, r0:r0 + 128, :], ot_[:], [ot_], [], "gpsimd")
        b.mk.barrier()
        w1 = b.sb([64, 32, 256], BF16); w2 = b.sb([128, 2, 64], BF16); posT = b.sb([64, 32], BF16)
        tokT = b.sb([64, T], BF16)
        pbias = b.sb([128, 4]); npbias = b.sb([128, 4])
        hid = [b.sb([128, 256], BF16) for _ in range(2)]
        er2 = b.rot(2, [128, 256]); zz2 = b.rot(2, [128, 256])
        kco = b.sb([64, 256]); vco = b.sb([128, 2, 64])
        for kind in range(2):
            for l4 in range(4):
                b.dmac(w1[:, l4 * 8:(l4 + 1) * 8, :],
                       w1_d[kind].t.ap()[l4 * 512:(l4 + 1) * 512, :].rearrange("(l d) h -> d l h", d=64), [w1_d[kind]], [w1])
            b.dmac(w2[:], w2_d[kind].t.ap().rearrange("(c p) d -> p c d", p=128), [w2_d[kind]], [w2])
            b.dmac(posT[:], posT_d.t.ap()[kind], [posT_d], [posT])
            for hc in range(2):
                for l in range(32):
                    b.mm(pb[6][:, hc:hc + 1], w1[:, l, hc * 128:(hc + 1) * 128], posT[:, l:l + 1], l == 0, l == 31, [w1, posT], [pb[6]])
            b.copy(pbias[:, 0:2], pb[6][:, 0:2], [pb[6]], [pbias])
            b.ts(npbias[:, 0:2], pbias[:, 0:2], -1.0, None, ALU.mult, None, [pbias], [npbias])
            for g in range(2):
                b.dma(tokT[:], KVT.t.ap()[kind, g * 64:(g + 1) * 64, :], [KVT], [tokT])
                b.memset(hid[0][:], 0.0, [hid[0]]); b.memset(hid[1][:], 0.0, [hid[1]])
                for hc in range(2):
                    pbt = pb[hc]
                    for l in range(32):
                        b.mm(pbt[:, 0:255], w1[:, l, hc * 128:(hc + 1) * 128], tokT[:, l:l + 16 * 254 + 1:16], l == 0, l == 31, [w1, tokT], [pbt])
                    e_ = er2.next(); zz = zz2.next()
                    b.act(e_[:, 0:255], pbt[:, 0:255], AF.Exp, [pbt, npbias], [e_], bias=npbias[:, hc:hc + 1], scale=-1.0)
                    b.act(zz[:, 0:255], pbt[:, 0:255], AF.Identity, [pbt, pbias], [zz], bias=pbias[:, hc:hc + 1])
                    b.ts(e_[:, 0:255], e_[:, 0:255], 1.0, None, ALU.add, None, [e_], [e_])
                    b.ve(lambda e, e_=e_: e.reciprocal(out=e_[:, 0:255], in_=e_[:, 0:255]), [e_], [e_])
                    b.tt(hid[hc][:, 0:255], zz[:, 0:255], e_[:, 0:255], ALU.mult, [zz, e_], [hid[hc]])
                for hc in range(2):
                    b.mm(pb[2][0:64, 0:256], w2[:, hc, :], hid[hc][:], hc == 0, hc == 1, [w2, hid[hc]], [pb[2]])
                b.copy(kco[:], pb[2][0:64, 0:256], [pb[2]], [kco])
                b.dma(KCT.t.ap()[kind, g], kco[:], [kco], [], "gpsimd")
                for ct in range(2):
                    for hc in range(2):
                        b.mm(pb[3][:, ct * 64:(ct + 1) * 64], hid[hc][:, ct * 128:(ct + 1) * 128], w2[:, hc, :], hc == 0, hc == 1,
                             [hid[hc], w2], [pb[3]])
                b.copy(vco[:].rearrange("p c d -> p (c d)"), pb[3][:, 0:128], [pb[3]], [vco])
                b.dma(VCT.t.ap()[kind, g].rearrange("(c p) d -> p c d", p=128), vco[:], [vco], [], "gpsimd")
        b.pop()

    binw_d = b.din("b_in_w", [D, 4144])
    boutw_d = b.din("b_out_w", [D, D])
    oh_d = b.din("oh_const", [33, 1536])
    ov_d = b.din("ov_const", [128, 2, 64])
    fmw_d = b.din("fmw_const", [128, 126])
    ZG = b.dscr("ZG", [T, 3072], F32)
    Y2 = b.dscr("Y2", [T, D], BF16)
    FD = b.dscr("FD", [16, 1536], F32)
    G1 = b.dscr("G1", [16 * 128 * 1537 + 4096], F32)
    G2 = b.dscr("G2", [16 * 24 * 1552 + 4096], F32)
    OCD = b.dscr("OCD", [3, T, D], F32)
    SELD = b.dscr("SELD", [2, T, 64], F32)
    IMPD = b.dscr("IMPD", [2, T, 64], F32)

    if "Z" in phases:
        b.mk.label = "Z"
        b.push()
        Wz = b.sb([128, 8, 3120], BF16)
        for kc in range(8):
            b.dmac(Wz[:, kc, :], binw_d.t.ap()[kc * 128:(kc + 1) * 128, 1024:4144], [binw_d], [Wz])
        hbr = b.rot(2, [128, 8, 512], BF16)
        zgr = b.rot(2, [128, 3072]); er3 = b.rot(2, [128, 512]); zcr = b.rot(2, [128, 512]); sgr = b.rot(2, [128, 48])
        for blk in range(NQT):
            hb = hbr.next()
            b.dma(hb[:], H1T.t.ap()[:, :, blk * 512:(blk + 1) * 512].rearrange("k p t -> p k t"), [H1T], [hb])
            for tt_ in range(4):
                tsl = slice(tt_ * 128, (tt_ + 1) * 128)
                r0 = blk * 512 + tt_ * 128
                sg = sgr.next(); zg = zgr.next()
                for kc in range(8):
                    b.mm(pb[6][:, 0:48], hb[:, kc, tsl], Wz[:, kc, 3072:3120], kc == 0, kc == 7, [hb, Wz], [pb[6]])
                b.act(sg[:], pb[6][:, 0:48], AF.Exp, [pb[6]], [sg], scale=-1.0)
                b.ts(sg[:], sg[:], 1.0, None, ALU.add, None, [sg], [sg])
                b.ve(lambda e, sg=sg: e.reciprocal(out=sg[:], in_=sg[:]), [sg], [sg])
                for n in range(6):
                    pbt = pb[n % 4]
                    for kc in range(8):
                        b.mm(pbt[:], hb[:, kc, tsl], Wz[:, kc, n * 512:(n + 1) * 512], kc == 0, kc == 7, [hb, Wz], [pbt])
                    e_ = er3.next(); zc = zcr.next()
                    b.act(e_[:], pbt[:], AF.Exp, [pbt], [e_], scale=-1.0)
                    b.act(zc[:], pbt[:], AF.Copy, [pbt], [zc])
                    b.ts(e_[:], e_[:], 1.0, None, ALU.add, None, [e_], [e_])
                    b.ve(lambda e, e_=e_: e.reciprocal(out=e_[:], in_=e_[:]), [e_], [e_])
                    b.tt(zc[:], zc[:], e_[:], ALU.mult, [zc, e_], [zc], "gpsimd")
                    b.tt(zg[:, n * 512:(n + 1) * 512].rearrange("p (h d) -> p h d", d=64),
                         zc[:].rearrange("p (h d) -> p h d", d=64),
                         sg[:, n * 8:(n + 1) * 8].unsqueeze(2).to_broadcast([128, 8, 64]), ALU.mult, [zc, sg], [zg])
                b.dma(ZG.t.ap()[r0:r0 + 128, :], zg[:], [zg], [], "gpsimd")
        b.pop()

    if "NSA" in phases:
        b.mk.label = "NSAsetup"
        b.push()
        cfarB = b.sb([128, 16])
        b.push()
        rb = b.sb([33, 16]); ohs = b.sb([33, 1536]); fsb = b.sb([16, 1536])
        b.dma(rb[0:32, :], relb_d.t.ap(), [relb_d], [rb])
        b.memset(rb[32:33, :], -BIG, [rb])
        b.dma(ohs[:], oh_d.t.ap(), [oh_d], [ohs])
        b.dma(cfarB[:], relb_d.t.ap()[31:32, :].to_broadcast([128, 16]), [relb_d], [cfarB])
        for n in range(3):
            b.mm(pb[n][0:16, :], rb[:], ohs[:, n * 512:(n + 1) * 512], True, True, [rb, ohs], [pb[n]])
            b.copy(fsb[:, n * 512:(n + 1) * 512], pb[n][0:16, :], [pb[n]], [fsb])
        b.dma(FD.t.ap(), fsb[:], [fsb], [FD])
        b.pop()
        kslc = b.sb([128, T], BF16)
        b.memset(kslc[:], 1.0, [kslc])
        b.ve(lambda e: e.affine_select(out=kslc[:], in_=kslc[:], pattern=[[1, T]], compare_op=ALU.is_ge, fill=0.0,
                                       base=4096, channel_multiplier=-64), [kslc], [kslc], "gpsimd")
        b.ve(lambda e: e.affine_select(out=kslc[:], in_=kslc[:], pattern=[[-1, T]], compare_op=ALU.is_ge, fill=0.0,
                                       base=-4033, channel_multiplier=64), [kslc], [kslc], "gpsimd")
        kwin = b.sb([128, T], BF16)
        b.memset(kwin[:], 0.0, [kwin])
        OVt = b.sb([128, 2, 64]); FMW = b.sb([128, 126]); zer = b.sb([128, 232])
        b.dma(OVt[:], ov_d.t.ap(), [ov_d], [OVt]); b.dma(FMW[:], fmw_d.t.ap(), [fmw_d], [FMW])
        b.memset(zer[:], 0.0, [zer])
        VS = b.sb([128, NT, 65], BF16); VW = b.sb([128, NT, 65], BF16)
        kcT = b.sb([128, 256], BF16); VC = b.sb([128, 2, 64], BF16)
        Wq = b.sb([128, 8, 512], BF16)
        EBS = b.sb([128, 8, 1024], BF16); EBW = b.sb([128, 8, 512], BF16); EBc = b.sb([128, 8, 504], BF16)
        hbr = b.rot(1, [128, 8, 512], BF16)
        qA = [b.sb([128, 8, 512], BF16) for _ in range(2)]
        Eer = b.rot(3, [128, 512]); Pbr = b.rot(4, [128, 512], BF16)
        OB = [b.sb([128, 4, 512]) for _ in range(3)]
        OC2 = [OB[0], b.sb([128, 4, 512])]
        zgt = b.rot(2, [128, 3, 512]); yr2 = b.rot(2, [128, 512], BF16); t1r = b.rot(2, [128, 512]); t2r = b.rot(2, [128, 512])
        impP = b.rot(2, [128, 256]); Ecr = b.rot(2, [128, 256]); Pcr = b.rot(2, [128, 256]); Pnr = b.rot(2, [128, 256], BF16)
        PTr = b.rot(2, [128, 2, 128], BF16); rsr = b.rot(4, [128, 4]); impT = b.rot(2, [128, 2, 128])
        impf = b.rot(2, [128, 64]); m8r = b.rot(2, [128, 8]); m8br = b.rot(2, [128, 8]); tmpm = b.rot(2, [128, 64]); seln = b.rot(2, [128, 128])
        for sb_ in seln.bufs:
            b.memset(sb_[:], 0.0, [sb_])
        rs2 = b.rot(8, [128, 1])
        zrow = b.sb([1, 512], BF16)
        b.memset(zrow[:], 0.0, [zrow])
        SCB = [pb[0], pb[1], pb[3], pb[4]]
        PVB = pb[2]
        for g in range(2):
            b.dma(kslc[0:64, :], KVT.t.ap()[2, g * 64:(g + 1) * 64, :], [KVT], [kslc])
            b.dma(kwin[0:64, :], KVT.t.ap()[4, g * 64:(g + 1) * 64, :], [KVT], [kwin])
            b.dmac(kcT[0:64, :], KCT.t.ap()[0, g], [KCT], [kcT])
            for q4 in range(4):
                tsl = slice(q4 * 8, (q4 + 1) * 8)
                b.dma(VS[:, tsl, 0:64], VTOK.t.ap()[0, q4 * 1024:(q4 + 1) * 1024, g * 64:(g + 1) * 64].rearrange("(t p) d -> p t d", p=128), [VTOK], [VS])
                b.dma(VW[:, tsl, 0:64], VTOK.t.ap()[1, q4 * 1024:(q4 + 1) * 1024, g * 64:(g + 1) * 64].rearrange("(t p) d -> p t d", p=128), [VTOK], [VW])
            b.memset(VS[:, :, 64:65], 1.0, [VS]); b.memset(VW[:, :, 64:65], 1.0, [VW])
            b.dmac(VC[:], VCT.t.ap()[1, g].rearrange("(c p) d -> p c d", p=128), [VCT], [VC])
            for kc in range(8):
                b.dmac(Wq[:, kc, :], binw_d.t.ap()[kc * 128:(kc + 1) * 128, g * 512:(g + 1) * 512], [binw_d], [Wq])
            b.mk.label = "tables"
            b.push()
            Frep = b.rot(1, [128, 1536]); tS = b.rot(1, [128, 1408]); tC = b.rot(2, [128, 128])
            b.memset(EBc[:], 0.0, [EBc])
            for hh in range(8):
                H = g * 8 + hh
                fr = Frep.next()
                b.dma(fr[:], FD.t.ap()[H:H + 1, :].to_broadcast([128, 1536]), [FD], [fr])
                b.dma(bass.AP(G1.t, H * 128 * 1537, [[1537, 128], [1, 1536]]), fr[:], [fr], [G1])
                b.dma(bass.AP(G2.t, H * 24 * 1552, [[1552, 24], [1, 1536]]), fr[0:24, :], [fr], [G2])
                ts_ = tS.next()
                b.dma(ts_[:], bass.AP(G1.t, H * 128 * 1537 + 128, [[1536, 128], [1, 1408]]), [G1], [ts_])
                b.act(EBS[:, hh, :], ts_[:, 0:1024], AF.Exp, [ts_], [EBS])
                b.act(EBW[:, hh, :], ts_[:, 896:1408], AF.Exp, [ts_], [EBW])
                b.ve(lambda e, hh=hh: e.affine_select(out=EBW[:, hh, :], in_=EBW[:, hh, :], pattern=[[-1, 512]], compare_op=ALU.is_ge,
                                                      fill=0.0, base=-1, channel_multiplier=1), [EBW], [EBW], "gpsimd")
                tc_ = tC.next()
                b.dma(tc_[0:24, :], bass.AP(G2.t, H * 24 * 1552 + 737, [[1536, 24], [1, 128]]), [G2], [tc_])
                b.mm(pb[6][:, 0:24], tc_[0:24, :], ident[0:24, 0:24], True, True, [tc_, ident], [pb[6]])
                b.act(EBc[:, hh, 232:256], pb[6][:, 0:24], AF.Exp, [pb[6]], [EBc])
                b.act(EBc[:, hh, 0:232], zer[:], AF.Exp, [zer, cfarB], [EBc], bias=cfarB[:, H:H + 1])
            b.pop()
            if "EBT" in b.dbg and g == 0:
                dd = b.dscr("EBT", [128, 8, 512], BF16); b.dma(dd.t.ap(), EBW[:], [EBW], [dd])
                dd = b.dscr("EBS_", [128, 8, 1024], BF16); b.dma(dd.t.ap(), EBS[:], [EBS], [dd])
                dd = b.dscr("EBC_", [128, 8, 504], BF16); b.dma(dd.t.ap(), EBc[:], [EBc], [dd])
            def cmp_gen(QT):
                qa = qA[QT % 2]; OC = OC2[QT % 2]
                b.mk.label = "qproj"
                hb = hbr.next()
                b.dma(hb[:], H1T.t.ap()[:, :, QT * 512:(QT + 1) * 512].rearrange("k p t -> p k t"), [H1T], [hb])
                for hh in range(8):
                    for kc in range(8):
                        b.mm(pb[6][0:64, :], Wq[:, kc, hh * 64:(hh + 1) * 64], hb[:, kc, :], kc == 0, kc == 7, [Wq, hb], [pb[6]])
                    b.act(qa[0:64, hh, :], pb[6][0:64, :], AF.Copy, [pb[6]], [qa], scale=0.125)
                    yield
                for qs in range(4):
                    qt = QT * 4 + qs
                    qsl = slice(qs * 128, (qs + 1) * 128)
                    r0 = qt * 128
                    nct = 1 if 8 * qt + 7 < 128 else 2
                    ip = impP.next()
                    for hh in range(8):
                        b.mk.label = "cmp"
                        pS = pb[6][:, 0:256]
                        b.mm(pS, qa[0:64, hh, qsl], kcT[0:64, :], True, True, [qa, kcT], [pb[6]])
                        Ee = Ecr.next(); Pc = Pcr.next(); Pn = Pnr.next(); rs = rsr.next(); PT = PTr.next()
                        yield
                        b.act(Ee[:], pS, AF.Exp, [pb[6]], [Ee])
                        yield
                        j0 = 248 - 8 * qt
                        b.ve(lambda e, Pc=Pc, Ee=Ee, hh=hh, j0=j0, rs=rs: e.scalar_tensor_tensor(
                            out=Pc[:], in0=Ee[:], scalar=1.0, in1=EBc[:, hh, j0:j0 + 256], op0=ALU.mult, op1=ALU.mult,
                            accum_out=rs[:, 0:1]), [Ee, EBc], [Pc, rs])
                        b.ts(rs[:, 0:1], rs[:, 0:1], 1e-30, None, ALU.max, None, [rs], [rs])
                        b.ve(lambda e, rs=rs: e.reciprocal(out=rs[:, 1:2], in_=rs[:, 0:1]), [rs], [rs])
                        yield
                        b.ts(Pn[:], Pc[:], rs[:, 1:2], None, ALU.mult, None, [Pc, rs], [Pn])
                        if hh == 0:
                            b.ts(ip[:], Pc[:], rs[:, 1:2], None, ALU.mult, None, [Pc, rs], [ip])
                        else:
                            b.stt(ip[:], Pc[:], rs[:, 1:2], ip[:], ALU.mult, ALU.add, [Pc, rs, ip], [ip])
                        yield
                        pTv = pb[6][:, 256:384].bitcast(BF16)
                        for ct in range(nct):
                            b.tr(pTv[:, ct * 128:(ct + 1) * 128], Pn[:, ct * 128:(ct + 1) * 128], identb[:], [Pn, identb], [pb[6]])
                        yield
                        b.act(PT[:, 0:nct, :].rearrange("p c q -> p (c q)"), pTv[:, 0:nct * 128], AF.Copy, [pb[6]], [PT])
                        yield
                        for ct in range(nct):
                            b.mm(pb[7][:, 0:64], PT[:, ct, :], VC[:, ct, :], ct == 0, ct == nct - 1, [PT, VC], [pb[7]])
                        yield
                        b.copy(OC[:, qs, hh * 64:(hh + 1) * 64], pb[7][:, 0:64], [pb[7]], [OC])
                        yield
                    it = impT.next()
                    for ct in range(2):
                        b.mm(pb[7][:, 128 + ct * 128:256 + ct * 128], ip[:, ct * 128:(ct + 1) * 128], ident[:], True, True, [ip, ident], [pb[7]])
                    yield
                    b.copy(it[:].rearrange("p c q -> p (c q)"), pb[7][:, 128:384], [pb[7]], [it])
                    yield
                    for ct in range(2):
                        b.mm(pb[7][:, 64:128], it[:, ct, :], OVt[:, ct, :], ct == 0, ct == 1, [it, OVt], [pb[7]])
                    yield
                    imf = impf.next(); m8 = m8r.next(); m8b = m8br.next(); tm = tmpm.next(); sn = seln.next()
                    if "IMPD" in b.dbg:
                        b.copy(tm[:], pb[7][:, 64:128], [pb[7]], [tm])
                        b.dma(IMPD.t.ap()[g, r0:r0 + 128, :], tm[:], [tm], [IMPD], "gpsimd")
                    b.tt(imf[:], pb[7][:, 64:128], FMW[:, 62 - 2 * qt:62 - 2 * qt + 64], ALU.add, [pb[7], FMW], [imf])
                    b.ts(imf[:, 0:1], imf[:, 0:1], 2.0e9, None, ALU.add, None, [imf], [imf])
                    yield
                    b.ve(lambda e, m8=m8, imf=imf: e.max(out=m8[:], in_=imf[:]), [imf], [m8])
                    yield
                    b.ve(lambda e, tm=tm, m8=m8, imf=imf: e.match_replace(out=tm[:], in_to_replace=m8[:], in_values=imf[:], imm_value=-3.0e9),
                         [m8, imf], [tm])
                    yield
                    b.ve(lambda e, m8b=m8b, tm=tm: e.max(out=m8b[:], in_=tm[:]), [tm], [m8b])
                    yield
                    b.ts(sn[:, 64:128], imf[:], m8b[:, 7:8], -BIG, ALU.is_lt, ALU.mult, [imf, m8b], [sn])
                    if "SELD" in b.dbg:
                        b.dma(SELD.t.ap()[g, r0:r0 + 128, :], sn[:, 64:128], [sn], [SELD], "gpsimd")
                    yield
                    b.mm(pb[7][:, 384:512], sn[:], ident[:], True, True, [sn, ident], [pb[7]])
                    yield
                    b.copy(qa[64:128, :, qsl], pb[7][64:128, 384:512].unsqueeze(1).to_broadcast([64, 8, 128]), [pb[7]], [qa])
                    yield

            def sw_gen(QT):
                qa = qA[QT % 2]; OC = OC2[QT % 2]
                b.mk.label = "selwin"
                items = []
                for hh in range(8 if NSTOP > 3 else 0):
                    for br in NBR:
                        kt_lo = 0 if br == 1 else max(0, 4 * QT - 4)
                        for kt in range(kt_lo, 4 * QT + 4):
                            items.append((hh, br, kt, kt == 4 * QT + 3))

                def issue_scores(i):
                    hh, br, kt, _ = items[i]
                    KT_ = kslc if br == 1 else kwin
                    pS = SCB[i % 4]
                    ksl = slice(kt * 128, (kt + 1) * 128)
                    b.mm(pS[:], KT_[:, ksl], qa[:, hh, :], True, True, [KT_, qa], [pS])

                for i0_ in range(min(3, len(items))):
                    issue_scores(i0_)
                for i, (hh, br, kt, last) in enumerate(items):
                    b.mk.label = "selwin"
                    H = g * 8 + hh
                    V_ = VS if br == 1 else VW
                    relq0 = 512 * QT - 128 * kt
                    pS = SCB[i % 4]
                    Pb_ = Pbr.next()
                    if br == 1 and relq0 >= 256:
                        b.act(Pb_[:], pS[:], AF.Exp, [pS, cfarB], [Pb_], bias=cfarB[:, H:H + 1])
                    else:
                        Ee = Eer.next()
                        b.act(Ee[:], pS[:], AF.Exp, [pS], [Ee])
                        yield
                        j0 = relq0 + 384
                        if br == 1 or j0 + 512 <= 896:
                            b.tt(Pb_[:], Ee[:], EBS[:, hh, j0:j0 + 512], ALU.mult, [Ee, EBS], [Pb_])
                        elif j0 == 896:
                            b.tt(Pb_[:], Ee[:], EBW[:, hh, :], ALU.mult, [Ee, EBW], [Pb_])
                        else:
                            n1 = 896 - j0
                            b.tt(Pb_[:, 0:n1], Ee[:, 0:n1], EBS[:, hh, j0:896], ALU.mult, [Ee, EBS], [Pb_])
                            b.tt(Pb_[:, n1:512], Ee[:, n1:512], EBW[:, hh, 0:512 - n1], ALU.mult, [Ee, EBW], [Pb_])
                    if i + 3 < len(items):
                        issue_scores(i + 3)
                    yield
                    kt_first = 0 if br == 1 else max(0, 4 * QT - 4)
                    if kt == kt_first:
                        b.mm(PVB[:, 0:320], zrow[0:1, 0:128], zrow[0:1, 0:320], True, False, [zrow], [PVB])
                    for qs in range(4 if NSB >= 2 else 0):
                        qt = 4 * QT + qs
                        lo = 0 if br == 1 else max(0, qt - 4)
                        if kt > qt or kt < lo:
                            continue
                        b.mm(PVB[:, qs * 80:qs * 80 + 65], Pb_[:, qs * 128:(qs + 1) * 128], V_[:, kt, :], False, kt == qt,
                             [Pb_, V_], [PVB])
                    if last:
                        for qs in range(4 if NSB >= 3 else 0):
                            r_ = rs2.next()
                            b.ve(lambda e, r_=r_, qs=qs: e.reciprocal(out=r_[:], in_=PVB[:, qs * 80 + 64:qs * 80 + 65]), [PVB], [r_])
                            b.ts(OB[br][:, qs, hh * 64:(hh + 1) * 64], PVB[:, qs * 80:qs * 80 + 64], r_[:, 0:1], None, ALU.mult, None,
                                 [PVB, r_], [OB[br]])
                    yield
                b.mk.label = "gate"
                for qs in range(4 if NSTOP > 4 else 0):
                    r0 = (4 * QT + qs) * 128
                    zt = zgt.next(); t1 = t1r.next(); t2 = t2r.next(); y = yr2.next()
                    b.dma(zt[:], ZG.t.ap()[r0:r0 + 128, :].rearrange("p (br c) -> p br c", br=3)[:, :, g * 512:(g + 1) * 512], [ZG], [zt])
                    if "OCD" in b.dbg:
                        obs = [OC, OB[1], OB[2]]
                        for br in range(3):
                            b.dma(OCD.t.ap()[br, r0:r0 + 128, g * 512:(g + 1) * 512], obs[br][:, qs, :], [obs[br]], [OCD], "gpsimd")
                    b.tt(t1[:], zt[:, 0, :], OC[:, qs, :], ALU.mult, [zt, OC], [t1], "gpsimd")
                    b.tt(t2[:], zt[:, 1, :], OB[1][:, qs, :], ALU.mult, [zt, OB[1]], [t2])
                    yield
                    b.tt(t1[:], t1[:], t2[:], ALU.add, [t1, t2], [t1], "gpsimd")
                    b.tt(t2[:], zt[:, 2, :], OB[2][:, qs, :], ALU.mult, [zt, OB[2]], [t2])
                    yield
                    b.tt(y[:], t1[:], t2[:], ALU.add, [t1, t2], [y])
                    b.dma(Y2.t.ap()[r0:r0 + 128, g * 512:(g + 1) * 512], y[:], [y], [], "gpsimd")
                    yield

            def run_gens(gens, weights=None):
                gens = list(gens)
                weights = list(weights or [1] * len(gens))
                while gens:
                    for g_, w_ in list(zip(gens, weights)):
                        for _ in range(w_):
                            try:
                                next(g_)
                            except StopIteration:
                                k_ = gens.index(g_)
                                gens.pop(k_); weights.pop(k_)
                                break

            nq = NQT if NSTOP > 1 else 0
            if nq:
                run_gens([cmp_gen(0)])
            for QT in range(nq):
                gl = [sw_gen(QT)]; wl = [1]
                if QT + 1 < nq:
                    gl.append(cmp_gen(QT + 1)); wl.append(CMPR)
                run_gens(gl, wl)
        b.pop()

    fg_d = b.din("final_g", [1, D])
    if "FIN2" in phases:
        b.mk.label = "FIN2"
        b.push()
        Wo2 = b.sb([128, 8, D], BF16)
        for kc in range(8):
            b.dmac(Wo2[:, kc, :], boutw_d.t.ap()[kc * 128:(kc + 1) * 128, :], [boutw_d], [Wo2])
        fgB = b.sb([128, D])
        b.dma(fgB[:], fg_d.t.ap().to_broadcast([128, D]), [fg_d], [fgB])
        y2r = b.rot(2, [128, D], BF16); ytr2 = b.rot(2, [128, 8, 128], BF16); x1r2 = b.rot(2, [128, D]); x2r = b.rot(2, [128, D])
        otr2 = b.rot(2, [128, D]); st = b.rot(4, [128, 4]); jr4 = b.rot(1, [128, D], BF16)
        for ti in range(NQT * 4):
            rows = slice(ti * 128, (ti + 1) * 128)
            y2 = y2r.next(); yt = ytr2.next(); x1t = x1r2.next(); x2 = x2r.next(); ot = otr2.next(); s = st.next(); jk = jr4.next()
            b.dma(y2[:], Y2.t.ap()[rows, :], [Y2], [y2])
            b.dma(x1t[:], X1.t.ap()[rows, :], [X1], [x1t])
            for kc in range(8):
                pbt = pb[6 + (kc % 2)]
                pv = pbt[:, 0:64].bitcast(BF16)
                b.tr(pv, y2[:, kc * 128:(kc + 1) * 128], identb[:], [y2, identb], [pbt])
                b.act(yt[:, kc, :], pv, AF.Copy, [pbt], [yt])
            for n in range(2):
                for kc in range(8):
                    b.mm(pb[n][:], yt[:, kc, :], Wo2[:, kc, n * 512:(n + 1) * 512], kc == 0, kc == 7, [yt, Wo2], [pb[n]])
                b.tt(x2[:, n * 512:(n + 1) * 512], pb[n][:], gateB[1][:, n * 512:(n + 1) * 512], ALU.mult, [pb[n], gateB[1]], [x2])
            b.tt(x2[:], x2[:], x1t[:], ALU.add, [x2, x1t], [x2], "gpsimd")
            if "X2" in b.dbg:
                b.dma(X2.t.ap()[rows, :], x2[:], [x2], [X2], "gpsimd")
            b.act(jk[:], x2[:], AF.Square, [x2], [jk, s], scale=1.0 / 32.0, accum=s[:, 0:1])
            b.act(s[:, 1:2], s[:, 0:1], AF.Ln, [s], [s], bias=1e-6)
            b.act(s[:, 2:3], s[:, 1:2], AF.Exp, [s], [s], scale=-0.5)
            b.stt(ot[:], x2[:], s[:, 2:3], fgB[:], ALU.mult, ALU.mult, [x2, s, fgB], [ot])
            b.dma(out_d.t.ap()[rows, :], ot[:], [ot], [], "gpsimd")
        b.pop()

    if "FIN" in phases:
        b.push()
        fgB = b.sb([128, D])
        b.dma(fgB[:], fg_d.t.ap().to_broadcast([128, D]), [fg_d], [fgB])
        src = X1 if "C" in phases else x_d
        xr = b.rot(2, [128, D]); orr2 = b.rot(2, [128, D]); st = b.rot(4, [128, 4]); jr3 = b.rot(1, [128, D], BF16)
        for ti in range(NT):
            rows = slice(ti * 128, (ti + 1) * 128)
            xt = xr.next(); ot = orr2.next(); s = st.next(); jk = jr3.next()
            b.dma(xt[:], src.t.ap()[rows, :], [src], [xt])
            b.act(jk[:], xt[:], AF.Square, [xt], [jk, s], scale=1.0 / 32.0, accum=s[:, 0:1])
            b.act(s[:, 1:2], s[:, 0:1], AF.Ln, [s], [s], bias=1e-6)
            b.act(s[:, 2:3], s[:, 1:2], AF.Exp, [s], [s], scale=-0.5)
            b.stt(ot[:], xt[:], s[:, 2:3], fgB[:], ALU.mult, ALU.mult, [xt, s, fgB], [ot])
            b.dma(out_d.t.ap()[rows, :], ot[:], [ot], [], "gpsimd")
        b.pop()

    b.mk.finish("sync")
    b.mk.emit()
    return b


def t5_bucket_np(d):
    n = np.maximum(d, 0)
    nf = np.maximum(n, 1).astype(np.float32)
    large = 16 + (np.log(nf / np.float32(16.0)) / np.float32(math.log(8.0)) * np.float32(16.0)).astype(np.int32)
    large = np.minimum(large, 31)
    return np.where(n < 16, n, large)


def nsa_consts():
    dd = np.arange(1536) - 512
    bk = t5_bucket_np(dd)
    oh = np.zeros((33, 1536), np.float32)
    for m in range(1536):
        if dd[m] < 0:
            oh[32, m] = 1.0
        else:
            oh[bk[m], m] = 1.0
    n_cmp = 255
    cells = np.arange(n_cmp)[:, None] + np.arange(2)[None, :]
    ov = (cells[:, None, :] // 4 == np.arange(64)[None, :, None]).sum(-1).astype(np.float32)
    ovp = np.zeros((256, 64), np.float32); ovp[:255] = ov
    ovl = ovp.reshape(2, 128, 64).transpose(1, 0, 2).copy()
    qi = np.arange(128)[:, None]; j = np.arange(126)[None, :]
    sp = j - 62
    curp = (qi >= 64).astype(np.int64)
    fm = np.where((sp == curp) | (sp == curp - 1), 1.0e9, np.where(sp > curp, -1.0e9, 0.0)).astype(np.float32)
    return oh, ovl, fm


def make_consts():
    i = np.arange(128)
    same = (i[:, None] // 64) == (i[None, :] // 64)
    U2 = (same & (i[:, None] <= i[None, :])).astype(np.float32)
    BON = same.astype(np.float32)
    SELA = np.repeat((i < 64).astype(np.float32)[:, None], 128, 1)
    SELB = np.repeat((i >= 64).astype(np.float32)[:, None], 128, 1)
    NMS = np.where(same & (i[:, None] > i[None, :]), 0.0, BIG).astype(np.float32)
    NMT = np.where(same & (i[None, :] >= i[:, None]), 0.0, -BIG).astype(np.float32)
    return np.concatenate([U2, BON, SELA, SELB, NMS, NMT], axis=1)


def core_inputs(inp, bi):
    f = np.ascontiguousarray
    d = {
        "x": f(inp["x"][bi]),
        "c_l": f(inp["c"][bi].reshape(8, 128).T),
        "rel_bias": f(inp["rel_bias"]),
        "ada_w": f(inp["ada_w"]),
        "ada_b": f(inp["ada_b"]),
        "norm_g_l": f(inp["norm_g"].reshape(2, 8, 128).transpose(2, 0, 1)),
        "a_in_w": f(inp["a_in_w"][0]),
        "conv_w_l": f(inp["a_conv_w"][0].reshape(4, 24, 128).transpose(2, 1, 0)),
        "a_A_log": f(inp["a_A_log"]),
        "a_dt_bias": f(inp["a_dt_bias"]),
        "a_onorm_g": f(inp["a_onorm_g"]),
        "a_out_w": f(inp["a_out_w"][0]),
        "consts": make_consts(),
        "kvg_l": f(inp["kv_norm_g"].reshape(8, 128).T),
        "final_g": f(inp["final_g"].reshape(1, D)),
        "kv_w": f(inp["kv_w"]),
        "posT": f(np.stack([inp["cmp_pos_k"].T, inp["cmp_pos_v"].T], 0)),
        "cmp_k_w1": f(inp["cmp_k_w1"]), "cmp_v_w1": f(inp["cmp_v_w1"]),
        "cmp_k_w2": f(inp["cmp_k_w2"]), "cmp_v_w2": f(inp["cmp_v_w2"]),
        "b_in_w": f(inp["b_in_w"][0]), "b_out_w": f(inp["b_out_w"][0]),
    }
    oh, ovl, fm = nsa_consts()
    d["oh_const"] = oh; d["ov_const"] = ovl; d["fmw_const"] = fm
    return d


_CACHE = {}
PHASES = ("mod", "A", "B", "C", "KV", "Z", "NSA", "FIN2")


def kernel(**inputs):
    inp = {k: np.asarray(v) for k, v in inputs.items()}
    if "nc" not in _CACHE:
        _CACHE["nc"] = build(phases=PHASES).nc
    nc = _CACHE["nc"]
    in_maps = [core_inputs(inp, bi) for bi in range(8)]
    res = run_bass_kernel_spmd(nc, in_maps, core_ids=list(range(8)))
    return np.stack([np.asarray(r["out"]).reshape(T, D) for r in res.results], 0).astype(np.float32)
```

```python
import math
from contextlib import ExitStack
import numpy as np
import concourse.bass as bass
import concourse.mybir as mybir
from concourse.bass_utils import run_bass_kernel_spmd

F32 = mybir.dt.float32
BF16 = mybir.dt.bfloat16
ALU = mybir.AluOpType
AF = mybir.ActivationFunctionType
AX = mybir.AxisListType

ENGS = ("sync", "scalar", "vector", "gpsimd", "tensor")
T = 4096
D = 1024
NT = 32
BIG = 30000.0
import os
NH = int(os.environ.get('MK_NH', '8'))
NTB = int(os.environ.get('MK_NTB', '32'))
STOP = int(os.environ.get('MK_STOP', '99'))
KIT = int(os.environ.get('MK_KIT', '5'))
VV = int(os.environ.get('MK_V', '3'))
NQT = int(os.environ.get('MK_NQT', '8'))
ANNOT = bool(os.environ.get('MK_ANNOT'))
CMPR = int(os.environ.get('MK_CMPR', '1'))
NSTOP = int(os.environ.get('MK_NSTOP', '99'))
NSB = int(os.environ.get('MK_NSB', '3'))
NBR = tuple(int(x) for x in os.environ.get('MK_NBR', '1,2').split(','))


class Tok:
    __slots__ = ("w", "r")

    def __init__(self):
        self.w = None
        self.r = []


class MK:
    NDMA = 20

    def __init__(self, nc):
        self.nc = nc
        self.q = {e: [] for e in ENGS}
        self.seen = {e: {} for e in ENGS}
        self.slots = {e: [0] * self.NDMA for e in ("sync", "scalar", "gpsimd")}
        self.rr = {e: 0 for e in ("sync", "scalar", "gpsimd")}
        self.signal = {e: set() for e in ENGS}
        self.label = ""

    def op(self, eng, fn, reads=(), writes=(), dma=False):
        q = self.q[eng]
        idx = len(q)
        waits = {}

        def need(ev):
            if ev is None:
                return
            key, val = ev
            if key[0] == "c" and key[1] == "tensor" and eng == "tensor":
                return
            if waits.get(key, -1) < val:
                waits[key] = val

        for t in reads:
            need(t.w)
        for t in writes:
            need(t.w)
            for ev in t.r:
                need(ev)
        if dma:
            rr = self.rr[eng]
            self.rr[eng] = (rr + 1) % self.NDMA
            prev = self.slots[eng][rr]
            if prev > 0:
                need((("d", eng, rr), prev))
            self.slots[eng][rr] = prev + 1
            ev = (("d", eng, rr), prev + 1)
        else:
            ev = (("c", eng), idx)
        seen = self.seen[eng]
        final = []
        for key, val in waits.items():
            if seen.get(key, -1) >= val:
                continue
            seen[key] = val
            final.append((key, val))
            if key[0] == "c":
                self.signal[key[1]].add(val)
        q.append((fn, final, ev, dma, self.label))
        for t in reads:
            t.r.append(ev)
            if len(t.r) > 64:
                t.r = t.r[-64:] if False else t.r
        for t in writes:
            t.w = ev
            t.r = []
        return ev

    def barrier(self):
        last = {}
        for e in ENGS:
            for i in range(len(self.q[e]) - 1, -1, -1):
                fn, _, ev, dma = self.q[e][i][:4]
                if fn is not None and not dma:
                    last[e] = i
                    break
        for e in ENGS:
            waits = []
            seen = self.seen[e]
            for e2, ix in last.items():
                if e2 == e:
                    continue
                key = ("c", e2)
                if seen.get(key, -1) < ix:
                    seen[key] = ix
                    waits.append((key, ix))
                    self.signal[e2].add(ix)
            for e2 in ("sync", "scalar", "gpsimd"):
                for rr, cnt in enumerate(self.slots[e2]):
                    key = ("d", e2, rr)
                    if cnt > 0 and seen.get(key, -1) < cnt:
                        seen[key] = cnt
                        waits.append((key, cnt))
            self.q[e].append((None, waits, None, False, ""))

    def finish(self, eng="sync"):
        waits = []
        for e in ("sync", "scalar", "gpsimd"):
            for rr, cnt in enumerate(self.slots[e]):
                if cnt > 0:
                    waits.append((("d", e, rr), cnt))
        self.q[eng].append((None, waits, None, False, ""))

    def emit(self):
        nc = self.nc
        csem = {e: nc.alloc_semaphore(f"c_{e}") for e in ENGS}
        dsem = {e: [nc.alloc_semaphore(f"d_{e}_{i}") for i in range(self.NDMA)]
                for e in ("sync", "scalar", "gpsimd")}
        cval = {}
        for e in ENGS:
            s = sorted(self.signal[e])
            cval[e] = {ix: n + 1 for n, ix in enumerate(s)}

        def replay(e, engobj):
            sig = self.signal[e]
            for i, (fn, waits, ev, dma, lab) in enumerate(self.q[e]):
                for key, val in waits:
                    if key[0] == "c":
                        engobj.wait_ge(csem[key[1]], cval[key[1]][val])
                    else:
                        engobj.wait_ge(dsem[key[1]][key[2]], 16 * val)
                if fn is None:
                    continue
                ins = fn(engobj)
                if ANNOT and lab:
                    ins.annotate(lab)
                if dma:
                    ins.then_inc(dsem[e][ev[0][2]], 16)
                elif i in sig:
                    ins.then_inc(csem[e], 1)

        with nc.Block() as block:
            @block.sync
            def _(eng):
                replay("sync", eng)

            @block.scalar
            def _(eng):
                replay("scalar", eng)

            @block.vector
            def _(eng):
                replay("vector", eng)

            @block.gpsimd
            def _(eng):
                replay("gpsimd", eng)

            @block.tensor
            def _(eng):
                replay("tensor", eng)


class Buf:
    def __init__(self, t):
        self.t = t
        self.k = Tok()

    def __getitem__(self, key):
        return self.t[key]


class Rot:
    def __init__(self, bufs):
        self.bufs = bufs
        self.i = 0

    def next(self):
        b = self.bufs[self.i % len(self.bufs)]
        self.i += 1
        return b


class Bld:
    def __init__(self, dbg=()):
        self.nc = bass.Bass("TRN2", target_bir_lowering=False)
        self.mk = MK(self.nc)
        self.dbg = set(dbg)
        self.n = 0
        self.scopes = [ExitStack()]

    def sb(self, shape, dt=F32, name=None):
        self.n += 1
        return Buf(self.scopes[-1].enter_context(self.nc.sbuf_tensor(name or f"sb{self.n}", list(shape), dt)))

    def push(self):
        self.scopes.append(ExitStack())

    def pop(self):
        self.mk.barrier()
        self.scopes.pop().close()

    def rot(self, n, shape, dt=F32):
        return Rot([self.sb(shape, dt) for _ in range(n)])

    def din(self, name, shape, dt=F32):
        return Buf(self.nc.dram_tensor(name, list(shape), dt, kind="ExternalInput"))

    def dscr(self, name, shape, dt=F32, out=False):
        kind = "ExternalOutput" if (out or name in self.dbg) else "Internal"
        if ("REFIN:" + name) in self.dbg:
            kind = "ExternalInput"
        return Buf(self.nc.dram_tensor(name, list(shape), dt, kind=kind))

    def dma(self, out, in_, reads, writes, eng=None, slow=False):
        if eng is None:
            eng = "sync"
        if slow:
            fn = lambda e: e.dma_start(out=out, in_=in_, allow_slow_non_contiguous=True)
        else:
            fn = lambda e: e.dma_start(out=out, in_=in_)
        self.mk.op(eng, fn, reads=[b.k for b in reads], writes=[b.k for b in writes], dma=True)

    def dmac(self, out, in_, reads, writes):
        self.dma(out, in_, reads, writes, eng="gpsimd")

    def mm(self, out, lhsT, rhs, start, stop, reads, writes):
        self.mk.op("tensor", lambda e: e.matmul(out, lhsT=lhsT, rhs=rhs, start=start, stop=stop),
                   reads=[b.k for b in reads], writes=[b.k for b in writes])

    def tr(self, out, in_, ident, reads, writes):
        self.mk.op("tensor", lambda e: e.transpose(out, in_, ident),
                   reads=[b.k for b in reads], writes=[b.k for b in writes])

    def act(self, out, in_, func, reads, writes, bias=None, scale=None, accum=None):
        kw = {}
        if bias is not None:
            kw["bias"] = bias
        if scale is not None:
            kw["scale"] = scale
        if accum is not None:
            kw["accum_out"] = accum
        self.mk.op("scalar", lambda e: e.activation(out=out, in_=in_, func=func, **kw),
                   reads=[b.k for b in reads], writes=[b.k for b in writes])

    def ve(self, fn, reads, writes, eng="vector"):
        self.mk.op(eng, fn, reads=[b.k for b in reads], writes=[b.k for b in writes])

    def copy(self, out, in_, reads, writes, eng="vector"):
        self.ve(lambda e: e.tensor_copy(out=out, in_=in_), reads, writes, eng)

    def tt(self, out, a, b_, op, reads, writes, eng="vector"):
        self.ve(lambda e: e.tensor_tensor(out=out, in0=a, in1=b_, op=op), reads, writes, eng)

    def ts(self, out, a, s1, s2, op0, op1, reads, writes, eng="vector"):
        if op1 is None:
            self.ve(lambda e: e.tensor_scalar(out=out, in0=a, scalar1=s1, scalar2=None, op0=op0), reads, writes, eng)
        else:
            self.ve(lambda e: e.tensor_scalar(out=out, in0=a, scalar1=s1, scalar2=s2, op0=op0, op1=op1), reads, writes, eng)

    def stt(self, out, a, s, b_, op0, op1, reads, writes, eng="vector"):
        self.ve(lambda e: e.scalar_tensor_tensor(out=out, in0=a, scalar=s, in1=b_, op0=op0, op1=op1), reads, writes, eng)

    def memset(self, ap, val, writes, eng="gpsimd"):
        self.ve(lambda e: e.memset(ap, val), [], writes, eng)


def build(dbg=(), phases=("mod", "A", "B", "C", "KV", "Z", "NSA", "FIN")):
    b = Bld(dbg)
    nc = b.nc
    P = {}
    x_d = b.din("x", [T, D])
    cl_d = b.din("c_l", [128, 8])
    relb_d = b.din("rel_bias", [32, 16])
    adaw_d = b.din("ada_w", [2, D, 3 * D])
    adab_d = b.din("ada_b", [2, 3 * D])
    ng_d = b.din("norm_g_l", [128, 2, 8])
    ainw_d = b.din("a_in_w", [D, 4112])
    convw_d = b.din("conv_w_l", [128, 24, 4])
    alog_d = b.din("a_A_log", [1, 8])
    dtb_d = b.din("a_dt_bias", [1, 8])
    aong_d = b.din("a_onorm_g", [1, 128])
    aoutw_d = b.din("a_out_w", [D, D])
    cst_d = b.din("consts", [128, 6 * 128])
    out_d = b.dscr("out", [T, D], F32, out=True)

    ident = b.sb([128, 128]); identb = b.sb([128, 128], BF16)
    ones = b.sb([128, 128])
    cst = b.sb([128, 6 * 128])
    b.memset(ident[:], 1.0, [ident])
    b.ve(lambda e: e.affine_select(out=ident[:], in_=ident[:], pattern=[[-1, 128]], compare_op=ALU.is_equal,
                                   fill=0.0, base=0, channel_multiplier=1), [ident], [ident], "gpsimd")
    b.copy(identb[:], ident[:], [ident], [identb])
    b.memset(ones[:], 1.0, [ones])
    b.dma(cst[:], cst_d.t.ap(), [cst_d], [cst])
    U2 = cst[:, 0:128]; BONES = cst[:, 128:256]; SELA = cst[:, 256:384]; SELB = cst[:, 384:512]
    NMS = cst[:, 512:640]; NMT = cst[:, 640:768]

    pb = [Buf(nc.alloc_psum_tensor(f"pb{i}", [128, 512], F32)) for i in range(8)]

    gateB = [b.sb([128, D]) for _ in range(2)]
    modcol = [b.sb([128, 24]) for _ in range(2)]
    Acol = [b.sb([128, 8]) for _ in range(2)]
    ng = b.sb([128, 2, 8])
    if "mod" in phases:
        b.mk.label = "mod"
        b.push()
        modB = b.sb([128, 3 * D])
        cs = b.sb([128, 8]); csb = b.sb([128, 8, 128])
        adabB = b.sb([128, 3 * D])
        awr = b.rot(2, [128, 3 * D])
        b.dma(cs[:], cl_d.t.ap(), [cl_d], [cs])
        b.dma(ng[:], ng_d.t.ap(), [ng_d], [ng])
        b.act(cs[:], cs[:], AF.Silu, [cs], [cs])
        for kc in range(8):
            b.copy(csb[:, kc, :], cs[:, kc:kc + 1].to_broadcast([128, 128]), [cs], [csb])
        for l in range(2):
            b.dma(adabB[:], adab_d.t.ap()[l:l + 1, :].to_broadcast([128, 3 * D]), [adab_d], [adabB])
            for kc in range(8):
                aw = awr.next()
                b.dma(aw[:], adaw_d.t.ap()[l, kc * 128:(kc + 1) * 128, :], [adaw_d], [aw])
                for n in range(6):
                    b.mm(pb[n][:], csb[:, kc, :], aw[:, n * 512:(n + 1) * 512], kc == 0, kc == 7, [csb, aw], [pb[n]])
            for n in range(6):
                b.tt(modB[:, n * 512:(n + 1) * 512], pb[n][:], adabB[:, n * 512:(n + 1) * 512], ALU.add,
                     [pb[n], adabB], [modB])
            for j in range(24):
                b.mm(pb[6][:, j:j + 1], modB[0:1, j * 128:(j + 1) * 128], ones[0:1, 0:1], True, True,
                     [modB, ones], [pb[6]])
            b.copy(modcol[l][:], pb[6][:, 0:24], [pb[6]], [modcol[l]])
            b.copy(gateB[l][:], modB[:, 2 * D:3 * D], [modB], [gateB[l]])
            b.stt(Acol[l][:], modcol[l][:, 8:16], 1.0, ng[:, l, :], ALU.add, ALU.mult, [modcol[l], ng], [Acol[l]])
            if l == 0 and "modB0" in b.dbg:
                dd = b.dscr("modB0", [128, 3 * D])
                b.dma(dd.t.ap(), modB[:], [modB], [dd])
                dd2 = b.dscr("modcol0", [128, 24])
                b.dma(dd2.t.ap(), modcol[0][:], [modcol[0]], [dd2])
        b.pop()

    QT = b.dscr("QT", [8, 128, T], BF16)
    KT = b.dscr("KT", [8, 128, T], BF16)
    VT = b.dscr("VT", [8, 128, T], BF16)
    ZS = b.dscr("ZS", [T, D], F32)
    GB = b.dscr("GB", [128, NT, 16], F32)

    g_all = b.sb([128, NT, 8]); beta_all = b.sb([128, NT, 8])

    def silu_from(dst, src, e_buf, reads, eng2="gpsimd"):
        b.act(e_buf[:], src, AF.Exp, reads, [e_buf], scale=-1.0)
        b.ts(e_buf[:], e_buf[:], 1.0, None, ALU.add, None, [e_buf], [e_buf])
        b.ve(lambda e: e.reciprocal(out=e_buf[:], in_=e_buf[:]), [e_buf], [e_buf])
        return e_buf

    if "A" in phases:
        b.mk.label = "A"
        b.push()
        Win = b.sb([128, 8, 4112], BF16)
        for kc in range(8):
            b.dmac(Win[:, kc, :], ainw_d.t.ap()[kc * 128:(kc + 1) * 128, :], [ainw_d], [Win])
        convw = b.sb([128, 24, 4])
        b.dma(convw[:], convw_d.t.ap(), [convw_d], [convw])
        alogB = b.sb([128, 8]); dtbB = b.sb([128, 8]); negA = b.sb([128, 8])
        b.dma(alogB[:], alog_d.t.ap().to_broadcast([128, 8]), [alog_d], [alogB])
        b.dma(dtbB[:], dtb_d.t.ap().to_broadcast([128, 8]), [dtb_d], [dtbB])
        b.act(negA[:], alogB[:], AF.Exp, [alogB], [negA])
        b.ts(negA[:], negA[:], -1.0, None, ALU.mult, None, [negA], [negA])
        pre = b.sb([128, 24, 515])
        b.memset(pre[:], 0.0, [pre])
        hT = b.rot(2, [128, 8, 512], BF16)
        xr = b.rot(2, [128, D]); xnr = b.rot(2, [128, D], BF16)
        st = b.rot(4, [128, 4])
        accr = b.rot(3, [128, 512]); er = b.rot(3, [128, 512]); sr = b.rot(3, [128, 512])
        sqr = b.rot(3, [128, 512]); rir = b.rot(3, [128, 512])
        obr = b.rot(4, [128, 512], BF16)
        zr = b.rot(2, [128, D]); ezr = b.rot(2, [128, 512]); bar = b.rot(2, [128, 16])
        for blk in range(8):
            h = hT.next()
            for tt_ in range(4):
                ti = blk * 4 + tt_
                xt = xr.next(); xn = xnr.next(); s = st.next()
                b.dma(xt[:], x_d.t.ap()[ti * 128:(ti + 1) * 128, :], [x_d], [xt])
                b.act(xn[:], xt[:], AF.Square, [xt], [xn, s], scale=1.0 / 32.0, accum=s[:, 0:1])
                b.act(s[:, 1:2], s[:, 0:1], AF.Ln, [s], [s], bias=1e-6)
                b.act(s[:, 2:3], s[:, 1:2], AF.Exp, [s], [s], scale=-0.5)
                b.ts(xn[:], xt[:], s[:, 2:3], None, ALU.mult, None, [xt, s], [xn])
                for kc in range(8):
                    pbt = pb[6 + (kc % 2)]
                    pv = pbt[:, 0:64].bitcast(BF16)
                    b.tr(pv, xn[:, kc * 128:(kc + 1) * 128], identb[:], [xn, identb], [pbt])
                    b.act(h[:, kc, tt_ * 128:(tt_ + 1) * 128], pv, AF.Identity, [pbt, Acol[0], modcol[0]], [h],
                          bias=modcol[0][:, kc:kc + 1], scale=Acol[0][:, kc:kc + 1])
            def chunk_gen(oc, h=h, blk=blk):
                hh = oc % 8
                pbt = pb[oc % 4]
                for kc in range(8):
                    b.mm(pbt[:], Win[:, kc, oc * 128:(oc + 1) * 128], h[:, kc, :], kc == 0, kc == 7, [Win, h], [pbt])
                yield
                b.act(pre[:, oc, 3:515], pbt[:], AF.Copy, [pbt], [pre])
                yield
                acc = accr.next()
                b.ts(acc[:], pre[:, oc, 0:512], convw[:, oc, 0:1], None, ALU.mult, None, [pre, convw], [acc])
                for k in range(1, 4):
                    b.stt(acc[:], pre[:, oc, k:k + 512], convw[:, oc, k:k + 1], acc[:], ALU.mult, ALU.add,
                          [pre, convw, acc], [acc])
                yield
                b.copy(pre[:, oc, 0:3], pre[:, oc, 512:515], [pre], [pre], "gpsimd")
                e_ = er.next()
                b.act(e_[:], acc[:], AF.Exp, [acc], [e_], scale=-1.0)
                yield
                b.ts(e_[:], e_[:], 1.0, None, ALU.add, None, [e_], [e_])
                b.ve(lambda e, e_=e_: e.reciprocal(out=e_[:], in_=e_[:]), [e_], [e_])
                yield
                sv = sr.next()
                b.tt(sv[:], acc[:], e_[:], ALU.mult, [acc, e_], [sv], "gpsimd")
                ob = obr.next()
                if oc < 16:
                    sq = sqr.next(); ri = rir.next()
                    b.tt(sq[:], sv[:], sv[:], ALU.mult, [sv], [sq], "gpsimd")
                    yield
                    pb2 = pb[4 + (oc % 2)]
                    b.mm(pb2[:], ones[:], sq[:], True, True, [ones, sq], [pb2])
                    yield
                    b.act(ri[:], pb2[:], AF.Ln, [pb2], [ri], bias=1e-6)
                    yield
                    b.act(ri[:], ri[:], AF.Exp, [ri], [ri], scale=-0.5)
                    yield
                    if oc < 8:
                        b.stt(ob[:], sv[:], 128.0 ** -0.5, ri[:], ALU.mult, ALU.mult, [sv, ri], [ob])
                    else:
                        b.tt(ob[:], sv[:], ri[:], ALU.mult, [sv, ri], [ob])
                    dst = QT if oc < 8 else KT
                else:
                    yield
                    b.copy(ob[:], sv[:], [sv], [ob], "gpsimd")
                    dst = VT
                yield
                b.dma(dst.t.ap()[hh, :, blk * 512:(blk + 1) * 512], ob[:], [ob], [])

            pend = list(range(24)); live = []
            while pend or live:
                while pend and len(live) < 2:
                    live.append(chunk_gen(pend.pop(0)))
                for g_ in list(live):
                    try:
                        next(g_)
                    except StopIteration:
                        live.remove(g_)
            for tt_ in range(4):
                ti = blk * 4 + tt_
                z = zr.next(); ba = bar.next()
                for n in range(2):
                    pbt = pb[4 + n]
                    for kc in range(8):
                        b.mm(pbt[:], h[:, kc, tt_ * 128:(tt_ + 1) * 128], Win[:, kc, 3072 + n * 512:3072 + (n + 1) * 512],
                             kc == 0, kc == 7, [h, Win], [pbt])
                    e_ = silu_from(None, pbt[:], ezr.next(), [pbt])
                    b.tt(z[:, n * 512:(n + 1) * 512], pbt[:], e_[:], ALU.mult, [pbt, e_], [z])
                pbt = pb[6]
                for kc in range(8):
                    b.mm(pbt[:, 0:16], h[:, kc, tt_ * 128:(tt_ + 1) * 128], Win[:, kc, 4096:4112], kc == 0, kc == 7, [h, Win], [pbt])
                b.copy(ba[:], pbt[:, 0:16], [pbt], [ba])
                b.dma(ZS.t.ap()[ti * 128:(ti + 1) * 128, :], z[:], [z], [])
                b.copy(beta_all[:, ti, :], ba[:, 0:8], [ba], [beta_all])
                b.tt(g_all[:, ti, :], ba[:, 8:16], dtbB[:], ALU.add, [ba, dtbB], [g_all])
        bf_ = beta_all[:].rearrange("p t h -> p (t h)")
        b.act(bf_, bf_, AF.Exp, [beta_all], [beta_all], scale=-1.0)
        b.ts(bf_, bf_, 1.0, None, ALU.add, None, [beta_all], [beta_all])
        b.ve(lambda e: e.reciprocal(out=bf_, in_=bf_), [beta_all], [beta_all])
        gf = g_all[:].rearrange("p t h -> p (t h)")
        b.act(gf, gf, AF.Exp, [g_all], [g_all])
        b.act(gf, gf, AF.Ln, [g_all], [g_all], bias=1.0)
        b.tt(g_all[:], g_all[:], negA[:].unsqueeze(1).to_broadcast([128, NT, 8]), ALU.mult, [g_all, negA], [g_all])
        if "GB" in b.dbg:
            b.dma(GB.t.ap()[:, :, 0:8], g_all[:], [g_all], [GB])
            b.dma(GB.t.ap()[:, :, 8:16], beta_all[:], [beta_all], [GB])
        b.pop()

    if "A" not in phases:
        b.push()
        b.memset(g_all[:], -0.05, [g_all]); b.memset(beta_all[:], 0.5, [beta_all])
        zt = b.sb([128, T], BF16); zt2 = b.sb([128, D])
        b.memset(zt[:], 0.01, [zt]); b.memset(zt2[:], 0.5, [zt2])
        for h_ in range(8):
            for dd_ in (QT, KT, VT):
                b.dma(dd_.t.ap()[h_], zt[:], [zt], [dd_])
        for ti in range(NT):
            b.dma(ZS.t.ap()[ti * 128:(ti + 1) * 128, :], zt2[:], [zt2], [ZS])
        b.pop()

    YT = b.dscr("YT", [8, 128, T], BF16)
    OD = b.dscr("OD", [8, T, 128], F32)
    if "B" in phases:
        b.mk.label = "B"
        b.push()
        sc = {n: b.sb([128, NT * 8]) for n in ("gc", "ngc", "gam", "bG", "kap", "nbeta", "GlA", "GlB")}
        gflat = g_all[:].rearrange("p t h -> p (t h)")
        bflat = beta_all[:].rearrange("p t h -> p (t h)")
        for i_, (lh, nm) in enumerate(((U2, "gc"), (BONES, "kap"), (SELA, "GlA"), (SELB, "GlB"))):
            b.mm(pb[i_][:, 0:256], lh, gflat, True, True, [cst, g_all], [pb[i_]])
            b.copy(sc[nm][:], pb[i_][:, 0:256], [pb[i_]], [sc[nm]])
        b.tt(sc["kap"][:], sc["kap"][:], sc["gc"][:], ALU.subtract, [sc["kap"], sc["gc"]], [sc["kap"]])
        b.act(sc["kap"][:], sc["kap"][:], AF.Exp, [sc["kap"]], [sc["kap"]])
        b.act(sc["GlA"][:], sc["GlA"][:], AF.Exp, [sc["GlA"]], [sc["GlA"]])
        b.act(sc["GlB"][:], sc["GlB"][:], AF.Exp, [sc["GlB"]], [sc["GlB"]])
        b.act(sc["gam"][:], sc["gc"][:], AF.Exp, [sc["gc"]], [sc["gam"]])
        b.ts(sc["ngc"][:], sc["gc"][:], -1.0, None, ALU.mult, None, [sc["gc"]], [sc["ngc"]])
        b.ts(sc["nbeta"][:], bflat, -1.0, None, ALU.mult, None, [beta_all], [sc["nbeta"]])
        b.tt(sc["bG"][:], bflat, sc["gam"][:], ALU.mult, [beta_all, sc["gam"]], [sc["bG"]])
        ongB = b.sb([128, 128])
        b.dma(ongB[:], aong_d.t.ap().to_broadcast([128, 128]), [aong_d], [ongB])
        class Reg:
            def __init__(self, bank, lo, hi, bf=False, shared=False):
                self.ap = bank.t[:, lo:hi].bitcast(BF16) if bf else bank.t[:, lo:hi]
                self.k = bank.k if shared else Tok()

        def mkctx(hp):
            bb = 4 * hp
            c = {}
            X1_, X2_, Y1_, Y2_ = pb[bb], pb[bb + 1], pb[bb + 2], pb[bb + 3]
            c["RD"] = Reg(X1_, 0, 128, False, True); c["RT"] = Reg(X1_, 128, 256, False, True)
            c["Nt"] = Reg(X1_, 256, 384, False, True); c["N2"] = Reg(X1_, 384, 512, False, True)
            c["Nt2"] = Reg(X2_, 0, 128, False, True); c["kt"] = Reg(X2_, 128, 192, True, True); c["vt"] = Reg(X2_, 192, 256, True, True)
            c["yt"] = Reg(X2_, 256, 320, True, True); c["U"] = Reg(X2_, 320, 448, False, True)
            c["KK"] = Reg(Y1_, 0, 128); c["QK"] = Reg(Y1_, 128, 256); c["A"] = Reg(Y1_, 256, 384); c["P1"] = Reg(Y1_, 384, 512)
            c["O1"] = Reg(Y2_, 0, 128); c["O2"] = Reg(Y2_, 128, 256); c["S"] = Reg(Y2_, 256, 384); c["W"] = Reg(Y2_, 384, 512)
            c["qT"] = b.sb([128, T], BF16); c["kT"] = b.sb([128, T], BF16); c["vT"] = b.sb([128, T], BF16)
            c["zs"] = b.sb([128, NT, 128]); c["yT"] = b.sb([128, T], BF16)
            c["Sf"] = b.sb([128, 128]); c["Sb"] = b.sb([128, 128], BF16)
            for nm in ("dg", "D", "DT", "u", "tmp", "o", "gz", "jk"):
                c[nm] = b.rot(2, [128, 128])
            for nm in ("N", "Ntb", "Acc"):
                c[nm] = b.rot(3, [128, 128])
            for nm in ("TT", "att", "kbg", "kde", "vb", "wT", "vn", "y"):
                c[nm] = b.rot(2, [128, 128], BF16)
            c["s4"] = b.rot(4, [128, 4])
            return c

        def head_gen(c, h):
            qT, kT, vT, zs, yT, S, Sb = c["qT"], c["kT"], c["vT"], c["zs"], c["yT"], c["Sf"], c["Sb"]
            b.dma(qT[:], QT.t.ap()[h], [QT], [qT]); b.dma(kT[:], KT.t.ap()[h], [KT], [kT]); b.dma(vT[:], VT.t.ap()[h], [VT], [vT])
            for q4 in range(4):
                b.dma(zs[:, q4 * 8:(q4 + 1) * 8, :],
                      ZS.t.ap()[q4 * 1024:(q4 + 1) * 1024, h * 128:(h + 1) * 128].rearrange("(t p) v -> p t v", p=128), [ZS], [zs])
            b.memset(S[:], 0.0, [S]); b.memset(Sb[:], 0.0, [Sb])
            yield
            for t in range(NTB):
                cols = slice(t * 128, (t + 1) * 128)
                th = slice(t * 8 + h, t * 8 + h + 1)
                KK, QK, RD, RT = c["KK"], c["QK"], c["RD"], c["RT"]
                b.mm(KK.ap, kT[:, cols], kT[:, cols], True, True, [kT], [KK])
                b.mm(QK.ap, kT[:, cols], qT[:, cols], True, True, [kT, qT], [QK])
                dg = c["dg"].next()
                b.ts(dg[:], ident[:], sc["gc"][:, th], None, ALU.mult, None, [ident, sc["gc"]], [dg])
                yield
                b.mm(RD.ap, ones[:], dg[:], True, False, [ones, dg], [RD])
                b.mm(RD.ap, ident[:], NMS, False, True, [ident, cst], [RD])
                b.mm(RT.ap, ones[:], dg[:], True, False, [ones, dg], [RT])
                b.mm(RT.ap, ident[:], NMT, False, True, [ident, cst], [RT])
                yield
                Dm = c["D"].next(); DTm = c["DT"].next()
                b.act(Dm[:], RD.ap, AF.Exp, [RD, sc["gc"]], [Dm], bias=sc["gc"][:, th], scale=-1.0)
                b.act(DTm[:], RT.ap, AF.Exp, [RT, sc["ngc"]], [DTm], bias=sc["ngc"][:, th], scale=1.0)
                yield
                N = c["N"].next()
                b.stt(N[:], KK.ap, sc["nbeta"][:, th], Dm[:], ALU.mult, ALU.mult, [KK, sc["nbeta"], Dm], [N])
                att = c["att"].next()
                b.tt(att[:], QK.ap, DTm[:], ALU.mult, [QK, DTm], [att])
                yield
                pNt, pA, pN2, pNt2 = c["Nt"], c["A"], c["N2"], c["Nt2"]
                b.mm(pNt.ap, N[:], ident[:], True, True, [N, ident], [pNt])
                Nt = c["Ntb"].next(); Acc = c["Acc"].next()
                yield
                b.act(Nt[:], pNt.ap, AF.Copy, [pNt], [Nt])
                yield
                b.tt(Acc[:], Nt[:], ident[:], ALU.add, [Nt, ident], [Acc])
                TTb = c["TT"].next()
                for k in range(1, 6):
                    N2 = c["N"].next()
                    b.mm(pN2.ap, Nt[:], N[:], True, True, [Nt, N], [pN2])
                    if k < 5:
                        Nt2 = c["Ntb"].next()
                        b.mm(pNt2.ap, N[:], Nt[:], True, True, [N, Nt], [pNt2])
                    yield
                    b.act(N2[:], pN2.ap, AF.Copy, [pN2], [N2])
                    if k < 5:
                        b.act(Nt2[:], pNt2.ap, AF.Copy, [pNt2], [Nt2])
                    yield
                    b.mm(pA.ap, N2[:], Acc[:], True, True, [N2, Acc], [pA])
                    yield
                    if k < 5:
                        Acc2 = c["Acc"].next()
                        b.tt(Acc2[:], pA.ap, Acc[:], ALU.add, [pA, Acc], [Acc2])
                        Acc = Acc2; Nt = Nt2
                    else:
                        b.tt(TTb[:], pA.ap, Acc[:], ALU.add, [pA, Acc], [TTb])
                    N = N2
                    yield
                pkt, pvt, pyt = c["kt"], c["vt"], c["yt"]
                b.tr(pkt.ap, kT[:, cols], identb[:], [kT, identb], [pkt])
                b.tr(pvt.ap, vT[:, cols], identb[:], [vT, identb], [pvt])
                yield
                kbg = c["kbg"].next(); kde = c["kde"].next(); vb = c["vb"].next()
                b.act(kbg[:], pkt.ap, AF.Identity, [pkt, sc["bG"]], [kbg], scale=sc["bG"][:, th])
                b.act(kde[:], pkt.ap, AF.Identity, [pkt, sc["kap"]], [kde], scale=sc["kap"][:, th])
                b.act(vb[:], pvt.ap, AF.Identity, [pvt, beta_all], [vb], scale=bflat[:, th])
                yield
                pU, pW = c["U"], c["W"]
                u = c["u"].next(); wT = c["wT"].next()
                b.mm(pU.ap, TTb[:], vb[:], True, True, [TTb, vb], [pU])
                b.mm(pW.ap, kbg[:], TTb[:], True, True, [kbg, TTb], [pW])
                yield
                b.act(u[:], pU.ap, AF.Copy, [pU], [u])
                b.copy(wT[:], pW.ap, [pW], [wT])
                yield
                vn = c["vn"].next(); o = c["o"].next()
                pP1, pO1, pO2, pS = c["P1"], c["O1"], c["O2"], c["S"]
                for hf in range(2):
                    rows = slice(hf * 64, hf * 64 + 64)
                    b.mm(pP1.ap, wT[:], Sb[:], True, True, [wT, Sb], [pP1])
                    b.mm(pO1.ap, qT[:, cols], Sb[:], True, True, [qT, Sb], [pO1])
                    yield
                    b.tt(vn[rows, :], u[rows, :], pP1.ap[rows, :], ALU.subtract, [u, pP1], [vn])
                    yield
                    b.mm(pS.ap, kde[rows, :], vn[rows, :], True, True, [kde, vn], [pS])
                    b.mm(pO2.ap, att[rows, :], vn[rows, :], True, True, [att, vn], [pO2])
                    yield
                    gl = sc["GlA"] if hf == 0 else sc["GlB"]
                    b.stt(S[:], S[:], gl[:, th], pS.ap, ALU.mult, ALU.add, [S, gl, pS], [S])
                    yield
                    b.act(Sb[:], S[:], AF.Copy, [S], [Sb])
                    tmp = c["tmp"].next()
                    b.copy(tmp[rows, :], pO2.ap[rows, :], [pO2], [tmp])
                    b.stt(o[rows, :], pO1.ap[rows, :], sc["gam"][rows, th], tmp[rows, :], ALU.mult, ALU.add,
                          [pO1, sc["gam"], tmp], [o])
                    yield
                if "OD" in b.dbg:
                    b.dma(OD.t.ap()[h, t * 128:(t + 1) * 128, :], o[:], [o], [OD], "gpsimd")
                s = c["s4"].next(); jk = c["jk"].next(); gz = c["gz"].next(); y = c["y"].next()
                b.act(jk[:], o[:], AF.Square, [o], [jk, s], scale=128.0 ** -0.5, accum=s[:, 0:1])
                b.tt(gz[:], zs[:, t, :], ongB[:], ALU.mult, [zs, ongB], [gz], "gpsimd")
                yield
                b.act(s[:, 1:2], s[:, 0:1], AF.Ln, [s], [s], bias=1e-6)
                yield
                b.act(s[:, 2:3], s[:, 1:2], AF.Exp, [s], [s], scale=-0.5)
                yield
                b.stt(y[:], o[:], s[:, 2:3], gz[:], ALU.mult, ALU.mult, [o, s, gz], [y])
                yield
                b.tr(pyt.ap, y[:], identb[:], [y, identb], [pyt])
                yield
                b.act(yT[:, cols], pyt.ap, AF.Copy, [pyt], [yT])
                yield
            b.dma(YT.t.ap()[h], yT[:], [yT], [], "gpsimd")

        ctxs = [mkctx(0), mkctx(1)]
        b.mk.barrier()
        for hp in range(0, NH, 2):
            gens = [head_gen(ctxs[i], hp + i) for i in range(2) if hp + i < NH]
            while gens:
                for g_ in list(gens):
                    try:
                        next(g_)
                    except StopIteration:
                        gens.remove(g_)
        b.pop()

    X1 = b.dscr("X1", [T, D], F32)
    X2 = b.dscr("X2", [T, D], F32)
    ST = b.dscr("ST", [8, 128, T], BF16)
    H1T = b.dscr("H1T", [8, 128, T], BF16)
    kvg_d = b.din("kvg_l", [128, 8])
    if "C" in phases:
        b.mk.label = "C"
        b.push()
        Wo = b.sb([128, 8, D], BF16)
        for kc in range(8):
            b.dmac(Wo[:, kc, :], aoutw_d.t.ap()[kc * 128:(kc + 1) * 128, :], [aoutw_d], [Wo])
        kvg = b.sb([128, 8])
        b.dma(kvg[:], kvg_d.t.ap(), [kvg_d], [kvg])
        ytl = b.rot(2, [128, 8, 128], BF16); xr = b.rot(2, [128, D]); x1r = b.rot(2, [128, D])
        xnr = b.rot(2, [128, D], BF16); st = b.rot(4, [128, 4]); jr2 = b.rot(1, [128, D], BF16)
        sTt = b.rot(2, [128, 8, 128], BF16); hTt = b.rot(2, [128, 8, 128], BF16)
        for ti in range(NT):
            rows = slice(ti * 128, (ti + 1) * 128)
            yt = ytl.next(); xt = xr.next(); x1 = x1r.next(); xn = xnr.next(); s = st.next()
            b.dma(yt[:], YT.t.ap()[:, :, rows].rearrange("h v t -> v h t"), [YT], [yt])
            b.dma(xt[:], x_d.t.ap()[rows, :], [x_d], [xt])
            for n in range(2):
                for kc in range(8):
                    b.mm(pb[n][:], yt[:, kc, :], Wo[:, kc, n * 512:(n + 1) * 512], kc == 0, kc == 7, [yt, Wo], [pb[n]])
                b.tt(x1[:, n * 512:(n + 1) * 512], pb[n][:], gateB[0][:, n * 512:(n + 1) * 512], ALU.mult, [pb[n], gateB[0]], [x1])
            b.tt(x1[:], x1[:], xt[:], ALU.add, [x1, xt], [x1], "gpsimd")
            b.dma(X1.t.ap()[rows, :], x1[:], [x1], [], "gpsimd")
            jk = jr2.next()
            b.act(jk[:], x1[:], AF.Square, [x1], [jk, s], scale=1.0 / 32.0, accum=s[:, 0:1])
            b.act(s[:, 1:2], s[:, 0:1], AF.Ln, [s], [s], bias=1e-6)
            b.act(s[:, 2:3], s[:, 1:2], AF.Exp, [s], [s], scale=-0.5)
            b.ts(xn[:], x1[:], s[:, 2:3], None, ALU.mult, None, [x1, s], [xn])
            sT_ = sTt.next(); hT_ = hTt.next()
            for kc in range(8):
                pbt = pb[6 + (kc % 2)]
                pv = pbt[:, 0:64].bitcast(BF16)
                b.tr(pv, xn[:, kc * 128:(kc + 1) * 128], identb[:], [xn, identb], [pbt])
                b.act(hT_[:, kc, :], pv, AF.Identity, [pbt, Acol[1], modcol[1]], [hT_],
                      bias=modcol[1][:, kc:kc + 1], scale=Acol[1][:, kc:kc + 1])
                b.act(sT_[:, kc, :], pv, AF.Identity, [pbt, kvg], [sT_], scale=kvg[:, kc:kc + 1])
            b.dma(ST.t.ap()[:, :, rows].rearrange("k p t -> p k t"), sT_[:], [sT_], [], "gpsimd")
            b.dma(H1T.t.ap()[:, :, rows].rearrange("k p t -> p k t"), hT_[:], [hT_], [], "gpsimd")
        b.pop()

    kvw_d = b.din("kv_w", [D, 768])
    posT_d = b.din("posT", [2, 64, 32])
    w1_d = [b.din("cmp_k_w1", [2048, 256]), b.din("cmp_v_w1", [2048, 256])]
    w2_d = [b.din("cmp_k_w2", [256, 64]), b.din("cmp_v_w2", [256, 64])]
    KVT = b.dscr("KVT", [6, 128, T], BF16)
    VTOK = b.dscr("VTOK", [2, T, 128], BF16)
    KCT = b.dscr("KCT", [2, 2, 64, 256], F32)
    VCT = b.dscr("VCT", [2, 2, 256, 64], F32)
    if "KV" in phases:
        b.mk.label = "KV"
        b.push()
        kvw = b.sb([128, 8, 768], BF16)
        for kc in range(8):
            b.dmac(kvw[:, kc, :], kvw_d.t.ap()[kc * 128:(kc + 1) * 128, :], [kvw_d], [kvw])
        sTr = b.rot(2, [128, 8, 512], BF16); ocr = b.rot(3, [128, 512], BF16); otr = b.rot(3, [128, 128], BF16)
        for blk in range(8):
            cols = slice(blk * 512, (blk + 1) * 512)
            sTb = sTr.next()
            b.dma(sTb[:], ST.t.ap()[:, :, cols].rearrange("k p t -> p k t"), [ST], [sTb])
            for i in range(6):
                pbt = pb[i % 4]
                for kc in range(8):
                    b.mm(pbt[:], kvw[:, kc, i * 128:(i + 1) * 128], sTb[:, kc, :], kc == 0, kc == 7, [kvw, sTb], [pbt])
                oc_ = ocr.next()
                b.act(oc_[:], pbt[:], AF.Copy, [pbt], [oc_])
                b.dma(KVT.t.ap()[i, :, cols], oc_[:], [oc_], [], "gpsimd")
            for j, i in enumerate((3, 5)):
                for tt_ in range(4):
                    pbt = pb[4 + (tt_ % 2)]
                    for kc in range(8):
                        b.mm(pbt[:, 0:128], sTb[:, kc, tt_ * 128:(tt_ + 1) * 128], kvw[:, kc, i * 128:(i + 1) * 128],
                             kc == 0, kc == 7, [sTb, kvw], [pbt])
                    ot_ = otr.next()
                    b.copy(ot_[:], pbt[:, 0:128], [pbt], [ot_])
                    r0 = blk * 512 + tt_ * 128
                    b.dma(VTOK.t.ap()[j, r0:r0 + 128, :], ot_[:], [ot_], [], "gpsimd")
        b.mk.barrier()
        w1 = b.sb([64, 32, 256], BF16); w2 = b.sb([128, 2, 64], BF16); posT = b.sb([64, 32], BF16)
        tokT = b.sb([64, T], BF16)
        pbias = b.sb([128, 4]); npbias = b.sb([128, 4])
        hid = [b.sb([128, 256], BF16) for _ in range(2)]
        er2 = b.rot(2, [128, 256]); zz2 = b.rot(2, [128, 256])
        kco = b.sb([64, 256]); vco = b.sb([128, 2, 64])
        for kind in range(2):
            for l4 in range(4):
                b.dmac(w1[:, l4 * 8:(l4 + 1) * 8, :],
                       w1_d[kind].t.ap()[l4 * 512:(l4 + 1) * 512, :].rearrange("(l d) h -> d l h", d=64), [w1_d[kind]], [w1])
            b.dmac(w2[:], w2_d[kind].t.ap().rearrange("(c p) d -> p c d", p=128), [w2_d[kind]], [w2])
            b.dmac(posT[:], posT_d.t.ap()[kind], [posT_d], [posT])
            for hc in range(2):
                for l in range(32):
                    b.mm(pb[6][:, hc:hc + 1], w1[:, l, hc * 128:(hc + 1) * 128], posT[:, l:l + 1], l == 0, l == 31, [w1, posT], [pb[6]])
            b.copy(pbias[:, 0:2], pb[6][:, 0:2], [pb[6]], [pbias])
            b.ts(npbias[:, 0:2], pbias[:, 0:2], -1.0, None, ALU.mult, None, [pbias], [npbias])
            for g in range(2):
                b.dma(tokT[:], KVT.t.ap()[kind, g * 64:(g + 1) * 64, :], [KVT], [tokT])
                b.memset(hid[0][:], 0.0, [hid[0]]); b.memset(hid[1][:], 0.0, [hid[1]])
                for hc in range(2):
                    pbt = pb[hc]
                    for l in range(32):
                        b.mm(pbt[:, 0:255], w1[:, l, hc * 128:(hc + 1) * 128], tokT[:, l:l + 16 * 254 + 1:16], l == 0, l == 31, [w1, tokT], [pbt])
                    e_ = er2.next(); zz = zz2.next()
                    b.act(e_[:, 0:255], pbt[:, 0:255], AF.Exp, [pbt, npbias], [e_], bias=npbias[:, hc:hc + 1], scale=-1.0)
                    b.act(zz[:, 0:255], pbt[:, 0:255], AF.Identity, [pbt, pbias], [zz], bias=pbias[:, hc:hc + 1])
                    b.ts(e_[:, 0:255], e_[:, 0:255], 1.0, None, ALU.add, None, [e_], [e_])
                    b.ve(lambda e, e_=e_: e.reciprocal(out=e_[:, 0:255], in_=e_[:, 0:255]), [e_], [e_])
                    b.tt(hid[hc][:, 0:255], zz[:, 0:255], e_[:, 0:255], ALU.mult, [zz, e_], [hid[hc]])
                for hc in range(2):
                    b.mm(pb[2][0:64, 0:256], w2[:, hc, :], hid[hc][:], hc == 0, hc == 1, [w2, hid[hc]], [pb[2]])
                b.copy(kco[:], pb[2][0:64, 0:256], [pb[2]], [kco])
                b.dma(KCT.t.ap()[kind, g], kco[:], [kco], [], "gpsimd")
                for ct in range(2):
                    for hc in range(2):
                        b.mm(pb[3][:, ct * 64:(ct + 1) * 64], hid[hc][:, ct * 128:(ct + 1) * 128], w2[:, hc, :], hc == 0, hc == 1,
                             [hid[hc], w2], [pb[3]])
                b.copy(vco[:].rearrange("p c d -> p (c d)"), pb[3][:, 0:128], [pb[3]], [vco])
                b.dma(VCT.t.ap()[kind, g].rearrange("(c p) d -> p c d", p=128), vco[:], [vco], [], "gpsimd")
        b.pop()

    binw_d = b.din("b_in_w", [D, 4144])
    boutw_d = b.din("b_out_w", [D, D])
    oh_d = b.din("oh_const", [33, 1536])
    ov_d = b.din("ov_const", [128, 2, 64])
    fmw_d = b.din("fmw_const", [128, 126])
    ZG = b.dscr("ZG", [T, 3072], F32)
    Y2 = b.dscr("Y2", [T, D], BF16)
    FD = b.dscr("FD", [16, 1536], F32)
    G1 = b.dscr("G1", [16 * 128 * 1537 + 4096], F32)
    G2 = b.dscr("G2", [16 * 24 * 1552 + 4096], F32)
    OCD = b.dscr("OCD", [3, T, D], F32)
    SELD = b.dscr("SELD", [2, T, 64], F32)
    IMPD = b.dscr("IMPD", [2, T, 64], F32)

    if "Z" in phases:
        b.mk.label = "Z"
        b.push()
        Wz = b.sb([128, 8, 3120], BF16)
        for kc in range(8):
            b.dmac(Wz[:, kc, :], binw_d.t.ap()[kc * 128:(kc + 1) * 128, 1024:4144], [binw_d], [Wz])
        hbr = b.rot(2, [128, 8, 512], BF16)
        zgr = b.rot(2, [128, 3072]); er3 = b.rot(2, [128, 512]); zcr = b.rot(2, [128, 512]); sgr = b.rot(2, [128, 48])
        for blk in range(NQT):
            hb = hbr.next()
            b.dma(hb[:], H1T.t.ap()[:, :, blk * 512:(blk + 1) * 512].rearrange("k p t -> p k t"), [H1T], [hb])
            for tt_ in range(4):
                tsl = slice(tt_ * 128, (tt_ + 1) * 128)
                r0 = blk * 512 + tt_ * 128
                sg = sgr.next(); zg = zgr.next()
                for kc in range(8):
                    b.mm(pb[6][:, 0:48], hb[:, kc, tsl], Wz[:, kc, 3072:3120], kc == 0, kc == 7, [hb, Wz], [pb[6]])
                b.act(sg[:], pb[6][:, 0:48], AF.Tanh, [pb[6]], [sg], scale=0.5)
                b.ts(sg[:], sg[:], 1.0, 0.25, ALU.add, ALU.mult, [sg], [sg])
                for n in range(6):
                    pbt = pb[n % 4]
                    for kc in range(8):
                        b.mm(pbt[:], hb[:, kc, tsl], Wz[:, kc, n * 512:(n + 1) * 512], kc == 0, kc == 7, [hb, Wz], [pbt])
                    e_ = er3.next(); zc = zcr.next()
                    b.act(e_[:], pbt[:], AF.Tanh, [pbt], [e_], scale=0.5)
                    b.act(zc[:], pbt[:], AF.Copy, [pbt], [zc])
                    b.stt(zc[:], e_[:], 1.0, zc[:], ALU.add, ALU.mult, [e_, zc], [zc])
                    b.tt(zg[:, n * 512:(n + 1) * 512].rearrange("p (h d) -> p h d", d=64),
                         zc[:].rearrange("p (h d) -> p h d", d=64),
                         sg[:, n * 8:(n + 1) * 8].unsqueeze(2).to_broadcast([128, 8, 64]), ALU.mult, [zc, sg], [zg])
                b.dma(ZG.t.ap()[r0:r0 + 128, :], zg[:], [zg], [], "gpsimd")
        b.pop()

    if "NSA" in phases:
        b.mk.label = "NSAsetup"
        b.push()
        cfarB = b.sb([128, 16])
        b.push()
        rb = b.sb([33, 16]); ohs = b.sb([33, 1536]); fsb = b.sb([16, 1536])
        b.dma(rb[0:32, :], relb_d.t.ap(), [relb_d], [rb])
        b.memset(rb[32:33, :], -BIG, [rb])
        b.dma(ohs[:], oh_d.t.ap(), [oh_d], [ohs])
        b.dma(cfarB[:], relb_d.t.ap()[31:32, :].to_broadcast([128, 16]), [relb_d], [cfarB])
        for n in range(3):
            b.mm(pb[n][0:16, :], rb[:], ohs[:, n * 512:(n + 1) * 512], True, True, [rb, ohs], [pb[n]])
            b.copy(fsb[:, n * 512:(n + 1) * 512], pb[n][0:16, :], [pb[n]], [fsb])
        b.dma(FD.t.ap(), fsb[:], [fsb], [FD])
        b.pop()
        kslc = b.sb([128, T], BF16)
        b.memset(kslc[:], 1.0, [kslc])
        b.ve(lambda e: e.affine_select(out=kslc[:], in_=kslc[:], pattern=[[1, T]], compare_op=ALU.is_ge, fill=0.0,
                                       base=4096, channel_multiplier=-64), [kslc], [kslc], "gpsimd")
        b.ve(lambda e: e.affine_select(out=kslc[:], in_=kslc[:], pattern=[[-1, T]], compare_op=ALU.is_ge, fill=0.0,
                                       base=-4033, channel_multiplier=64), [kslc], [kslc], "gpsimd")
        kwin = b.sb([128, T], BF16)
        b.memset(kwin[:], 0.0, [kwin])
        OVt = b.sb([128, 2, 64]); FMW = b.sb([128, 126]); zer = b.sb([128, 232])
        b.dma(OVt[:], ov_d.t.ap(), [ov_d], [OVt]); b.dma(FMW[:], fmw_d.t.ap(), [fmw_d], [FMW])
        b.memset(zer[:], 0.0, [zer])
        VS = b.sb([128, NT, 65], BF16); VW = b.sb([128, NT, 65], BF16)
        kcT = b.sb([128, 256], BF16); VC = b.sb([128, 2, 64], BF16)
        Wq = b.sb([128, 8, 512], BF16)
        EBS = b.sb([128, 8, 1024], BF16); EBW = b.sb([128, 8, 512], BF16); EBc = b.sb([128, 8, 504], BF16)
        hbr = b.rot(1, [128, 8, 512], BF16)
        qA = [b.sb([128, 8, 512], BF16) for _ in range(2)]
        Eer = b.rot(3, [128, 512]); Pbr = b.rot(4, [128, 512], BF16)
        OB = [b.sb([128, 4, 512]) for _ in range(3)]
        OC2 = [OB[0], b.sb([128, 4, 512])]
        zgt = b.rot(2, [128, 3, 512]); yr2 = b.rot(2, [128, 512], BF16); t1r = b.rot(2, [128, 512]); t2r = b.rot(2, [128, 512])
        impP = b.rot(2, [128, 256]); Ecr = b.rot(2, [128, 256]); Pcr = b.rot(2, [128, 256]); Pnr = b.rot(2, [128, 256], BF16)
        PTr = b.rot(2, [128, 2, 128], BF16); rsr = b.rot(4, [128, 4]); impT = b.rot(2, [128, 2, 128])
        impf = b.rot(2, [128, 64]); m8r = b.rot(2, [128, 8]); m8br = b.rot(2, [128, 8]); tmpm = b.rot(2, [128, 64]); seln = b.rot(2, [128, 128])
        for sb_ in seln.bufs:
            b.memset(sb_[:], 0.0, [sb_])
        rs2 = b.rot(8, [128, 1])
        zrow = b.sb([1, 512], BF16)
        b.memset(zrow[:], 0.0, [zrow])
        SCB = [pb[0], pb[1], pb[3], pb[4]]
        PVB = pb[2]
        for g in range(2):
            b.dma(kslc[0:64, :], KVT.t.ap()[2, g * 64:(g + 1) * 64, :], [KVT], [kslc])
            b.dma(kwin[0:64, :], KVT.t.ap()[4, g * 64:(g + 1) * 64, :], [KVT], [kwin])
            b.dmac(kcT[0:64, :], KCT.t.ap()[0, g], [KCT], [kcT])
            for q4 in range(4):
                tsl = slice(q4 * 8, (q4 + 1) * 8)
                b.dma(VS[:, tsl, 0:64], VTOK.t.ap()[0, q4 * 1024:(q4 + 1) * 1024, g * 64:(g + 1) * 64].rearrange("(t p) d -> p t d", p=128), [VTOK], [VS])
                b.dma(VW[:, tsl, 0:64], VTOK.t.ap()[1, q4 * 1024:(q4 + 1) * 1024, g * 64:(g + 1) * 64].rearrange("(t p) d -> p t d", p=128), [VTOK], [VW])
            b.memset(VS[:, :, 64:65], 1.0, [VS]); b.memset(VW[:, :, 64:65], 1.0, [VW])
            b.dmac(VC[:], VCT.t.ap()[1, g].rearrange("(c p) d -> p c d", p=128), [VCT], [VC])
            for kc in range(8):
                b.dmac(Wq[:, kc, :], binw_d.t.ap()[kc * 128:(kc + 1) * 128, g * 512:(g + 1) * 512], [binw_d], [Wq])
            b.mk.label = "tables"
            b.push()
            Frep = b.rot(1, [128, 1536]); tS = b.rot(1, [128, 1408]); tC = b.rot(2, [128, 128])
            b.memset(EBc[:], 0.0, [EBc])
            for hh in range(8):
                H = g * 8 + hh
                fr = Frep.next()
                b.dma(fr[:], FD.t.ap()[H:H + 1, :].to_broadcast([128, 1536]), [FD], [fr])
                b.dma(bass.AP(G1.t, H * 128 * 1537, [[1537, 128], [1, 1536]]), fr[:], [fr], [G1])
                b.dma(bass.AP(G2.t, H * 24 * 1552, [[1552, 24], [1, 1536]]), fr[0:24, :], [fr], [G2])
                ts_ = tS.next()
                b.dma(ts_[:], bass.AP(G1.t, H * 128 * 1537 + 128, [[1536, 128], [1, 1408]]), [G1], [ts_])
                b.act(EBS[:, hh, :], ts_[:, 0:1024], AF.Exp, [ts_], [EBS])
                b.act(EBW[:, hh, :], ts_[:, 896:1408], AF.Exp, [ts_], [EBW])
                b.ve(lambda e, hh=hh: e.affine_select(out=EBW[:, hh, :], in_=EBW[:, hh, :], pattern=[[-1, 512]], compare_op=ALU.is_ge,
                                                      fill=0.0, base=-1, channel_multiplier=1), [EBW], [EBW], "gpsimd")
                tc_ = tC.next()
                b.dma(tc_[0:24, :], bass.AP(G2.t, H * 24 * 1552 + 737, [[1536, 24], [1, 128]]), [G2], [tc_])
                b.mm(pb[6][:, 0:24], tc_[0:24, :], ident[0:24, 0:24], True, True, [tc_, ident], [pb[6]])
                b.act(EBc[:, hh, 232:256], pb[6][:, 0:24], AF.Exp, [pb[6]], [EBc])
                b.act(EBc[:, hh, 0:232], zer[:], AF.Exp, [zer, cfarB], [EBc], bias=cfarB[:, H:H + 1])
            b.pop()
            if "EBT" in b.dbg and g == 0:
                dd = b.dscr("EBT", [128, 8, 512], BF16); b.dma(dd.t.ap(), EBW[:], [EBW], [dd])
                dd = b.dscr("EBS_", [128, 8, 1024], BF16); b.dma(dd.t.ap(), EBS[:], [EBS], [dd])
                dd = b.dscr("EBC_", [128, 8, 504], BF16); b.dma(dd.t.ap(), EBc[:], [EBc], [dd])
            def cmp_gen(QT):
                qa = qA[QT % 2]; OC = OC2[QT % 2]
                b.mk.label = "qproj"
                hb = hbr.next()
                b.dma(hb[:], H1T.t.ap()[:, :, QT * 512:(QT + 1) * 512].rearrange("k p t -> p k t"), [H1T], [hb])
                for hh in range(8):
                    for kc in range(8):
                        b.mm(pb[6][0:64, :], Wq[:, kc, hh * 64:(hh + 1) * 64], hb[:, kc, :], kc == 0, kc == 7, [Wq, hb], [pb[6]])
                    b.act(qa[0:64, hh, :], pb[6][0:64, :], AF.Copy, [pb[6]], [qa], scale=0.125)
                    yield
                for qs in range(4):
                    qt = QT * 4 + qs
                    qsl = slice(qs * 128, (qs + 1) * 128)
                    r0 = qt * 128
                    nct = 1 if 8 * qt + 7 < 128 else 2
                    ip = impP.next()
                    for hh in range(8):
                        b.mk.label = "cmp"
                        pS = pb[6][:, 0:256]
                        b.mm(pS, qa[0:64, hh, qsl], kcT[0:64, :], True, True, [qa, kcT], [pb[6]])
                        Ee = Ecr.next(); Pc = Pcr.next(); Pn = Pnr.next(); rs = rsr.next(); PT = PTr.next()
                        yield
                        b.act(Ee[:], pS, AF.Exp, [pb[6]], [Ee])
                        yield
                        j0 = 248 - 8 * qt
                        b.ve(lambda e, Pc=Pc, Ee=Ee, hh=hh, j0=j0, rs=rs: e.scalar_tensor_tensor(
                            out=Pc[:], in0=Ee[:], scalar=1.0, in1=EBc[:, hh, j0:j0 + 256], op0=ALU.mult, op1=ALU.mult,
                            accum_out=rs[:, 0:1]), [Ee, EBc], [Pc, rs])
                        b.ts(rs[:, 0:1], rs[:, 0:1], 1e-30, None, ALU.max, None, [rs], [rs])
                        b.ve(lambda e, rs=rs: e.reciprocal(out=rs[:, 1:2], in_=rs[:, 0:1]), [rs], [rs])
                        yield
                        b.ts(Pn[:], Pc[:], rs[:, 1:2], None, ALU.mult, None, [Pc, rs], [Pn])
                        if hh == 0:
                            b.ts(ip[:], Pc[:], rs[:, 1:2], None, ALU.mult, None, [Pc, rs], [ip])
                        else:
                            b.stt(ip[:], Pc[:], rs[:, 1:2], ip[:], ALU.mult, ALU.add, [Pc, rs, ip], [ip])
                        yield
                        pTv = pb[6][:, 256:384].bitcast(BF16)
                        for ct in range(nct):
                            b.tr(pTv[:, ct * 128:(ct + 1) * 128], Pn[:, ct * 128:(ct + 1) * 128], identb[:], [Pn, identb], [pb[6]])
                        yield
                        b.act(PT[:, 0:nct, :].rearrange("p c q -> p (c q)"), pTv[:, 0:nct * 128], AF.Copy, [pb[6]], [PT])
                        yield
                        for ct in range(nct):
                            b.mm(pb[7][:, 0:64], PT[:, ct, :], VC[:, ct, :], ct == 0, ct == nct - 1, [PT, VC], [pb[7]])
                        yield
                        b.copy(OC[:, qs, hh * 64:(hh + 1) * 64], pb[7][:, 0:64], [pb[7]], [OC])
                        yield
                    it = impT.next()
                    for ct in range(2):
                        b.mm(pb[7][:, 128 + ct * 128:256 + ct * 128], ip[:, ct * 128:(ct + 1) * 128], ident[:], True, True, [ip, ident], [pb[7]])
                    yield
                    b.copy(it[:].rearrange("p c q -> p (c q)"), pb[7][:, 128:384], [pb[7]], [it])
                    yield
                    for ct in range(2):
                        b.mm(pb[7][:, 64:128], it[:, ct, :], OVt[:, ct, :], ct == 0, ct == 1, [it, OVt], [pb[7]])
                    yield
                    imf = impf.next(); m8 = m8r.next(); m8b = m8br.next(); tm = tmpm.next(); sn = seln.next()
                    if "IMPD" in b.dbg:
                        b.copy(tm[:], pb[7][:, 64:128], [pb[7]], [tm])
                        b.dma(IMPD.t.ap()[g, r0:r0 + 128, :], tm[:], [tm], [IMPD], "gpsimd")
                    b.tt(imf[:], pb[7][:, 64:128], FMW[:, 62 - 2 * qt:62 - 2 * qt + 64], ALU.add, [pb[7], FMW], [imf])
                    b.ts(imf[:, 0:1], imf[:, 0:1], 2.0e9, None, ALU.add, None, [imf], [imf])
                    yield
                    b.ve(lambda e, m8=m8, imf=imf: e.max(out=m8[:], in_=imf[:]), [imf], [m8])
                    yield
                    b.ve(lambda e, tm=tm, m8=m8, imf=imf: e.match_replace(out=tm[:], in_to_replace=m8[:], in_values=imf[:], imm_value=-3.0e9),
                         [m8, imf], [tm])
                    yield
                    b.ve(lambda e, m8b=m8b, tm=tm: e.max(out=m8b[:], in_=tm[:]), [tm], [m8b])
                    yield
                    b.ts(sn[:, 64:128], imf[:], m8b[:, 7:8], -BIG, ALU.is_lt, ALU.mult, [imf, m8b], [sn])
                    if "SELD" in b.dbg:
                        b.dma(SELD.t.ap()[g, r0:r0 + 128, :], sn[:, 64:128], [sn], [SELD], "gpsimd")
                    yield
                    b.mm(pb[7][:, 384:512], sn[:], ident[:], True, True, [sn, ident], [pb[7]])
                    yield
                    b.copy(qa[64:128, :, qsl], pb[7][64:128, 384:512].unsqueeze(1).to_broadcast([64, 8, 128]), [pb[7]], [qa])
                    yield

            def sw_gen(QT):
                qa = qA[QT % 2]; OC = OC2[QT % 2]
                b.mk.label = "selwin"
                items = []
                for hh in range(8 if NSTOP > 3 else 0):
                    for br in NBR:
                        kt_lo = 0 if br == 1 else max(0, 4 * QT - 4)
                        for kt in range(kt_lo, 4 * QT + 4):
                            items.append((hh, br, kt, kt == 4 * QT + 3))

                def issue_scores(i):
                    hh, br, kt, _ = items[i]
                    KT_ = kslc if br == 1 else kwin
                    pS = SCB[i % 4]
                    ksl = slice(kt * 128, (kt + 1) * 128)
                    b.mm(pS[:], KT_[:, ksl], qa[:, hh, :], True, True, [KT_, qa], [pS])

                for i0_ in range(min(3, len(items))):
                    issue_scores(i0_)
                for i, (hh, br, kt, last) in enumerate(items):
                    b.mk.label = "selwin"
                    H = g * 8 + hh
                    V_ = VS if br == 1 else VW
                    relq0 = 512 * QT - 128 * kt
                    pS = SCB[i % 4]
                    Pb_ = Pbr.next()
                    if br == 1 and relq0 >= 256:
                        b.act(Pb_[:], pS[:], AF.Exp, [pS, cfarB], [Pb_], bias=cfarB[:, H:H + 1])
                    else:
                        Ee = Eer.next()
                        b.act(Ee[:], pS[:], AF.Exp, [pS], [Ee])
                        yield
                        j0 = relq0 + 384
                        if br == 1 or j0 + 512 <= 896:
                            b.tt(Pb_[:], Ee[:], EBS[:, hh, j0:j0 + 512], ALU.mult, [Ee, EBS], [Pb_])
                        elif j0 == 896:
                            b.tt(Pb_[:], Ee[:], EBW[:, hh, :], ALU.mult, [Ee, EBW], [Pb_])
                        else:
                            n1 = 896 - j0
                            b.tt(Pb_[:, 0:n1], Ee[:, 0:n1], EBS[:, hh, j0:896], ALU.mult, [Ee, EBS], [Pb_])
                            b.tt(Pb_[:, n1:512], Ee[:, n1:512], EBW[:, hh, 0:512 - n1], ALU.mult, [Ee, EBW], [Pb_])
                    if i + 3 < len(items):
                        issue_scores(i + 3)
                    yield
                    kt_first = 0 if br == 1 else max(0, 4 * QT - 4)
                    if kt == kt_first:
                        b.mm(PVB[:, 0:320], zrow[0:1, 0:128], zrow[0:1, 0:320], True, False, [zrow], [PVB])
                    for qs in range(4 if NSB >= 2 else 0):
                        qt = 4 * QT + qs
                        lo = 0 if br == 1 else max(0, qt - 4)
                        if kt > qt or kt < lo:
                            continue
                        b.mm(PVB[:, qs * 80:qs * 80 + 65], Pb_[:, qs * 128:(qs + 1) * 128], V_[:, kt, :], False, kt == qt,
                             [Pb_, V_], [PVB])
                    if last:
                        for qs in range(4 if NSB >= 3 else 0):
                            r_ = rs2.next()
                            b.ve(lambda e, r_=r_, qs=qs: e.reciprocal(out=r_[:], in_=PVB[:, qs * 80 + 64:qs * 80 + 65]), [PVB], [r_])
                            b.ts(OB[br][:, qs, hh * 64:(hh + 1) * 64], PVB[:, qs * 80:qs * 80 + 64], r_[:, 0:1], None, ALU.mult, None,
                                 [PVB, r_], [OB[br]])
                    yield
                b.mk.label = "gate"
                for qs in range(4 if NSTOP > 4 else 0):
                    r0 = (4 * QT + qs) * 128
                    zt = zgt.next(); t1 = t1r.next(); t2 = t2r.next(); y = yr2.next()
                    b.dma(zt[:], ZG.t.ap()[r0:r0 + 128, :].rearrange("p (br c) -> p br c", br=3)[:, :, g * 512:(g + 1) * 512], [ZG], [zt])
                    if "OCD" in b.dbg:
                        obs = [OC, OB[1], OB[2]]
                        for br in range(3):
                            b.dma(OCD.t.ap()[br, r0:r0 + 128, g * 512:(g + 1) * 512], obs[br][:, qs, :], [obs[br]], [OCD], "gpsimd")
                    b.tt(t1[:], zt[:, 0, :], OC[:, qs, :], ALU.mult, [zt, OC], [t1], "gpsimd")
                    b.tt(t2[:], zt[:, 1, :], OB[1][:, qs, :], ALU.mult, [zt, OB[1]], [t2])
                    yield
                    b.tt(t1[:], t1[:], t2[:], ALU.add, [t1, t2], [t1], "gpsimd")
                    b.tt(t2[:], zt[:, 2, :], OB[2][:, qs, :], ALU.mult, [zt, OB[2]], [t2])
                    yield
                    b.tt(y[:], t1[:], t2[:], ALU.add, [t1, t2], [y])
                    b.dma(Y2.t.ap()[r0:r0 + 128, g * 512:(g + 1) * 512], y[:], [y], [], "gpsimd")
                    yield

            def run_gens(gens, weights=None):
                gens = list(gens)
                weights = list(weights or [1] * len(gens))
                while gens:
                    for g_, w_ in list(zip(gens, weights)):
                        for _ in range(w_):
                            try:
                                next(g_)
                            except StopIteration:
                                k_ = gens.index(g_)
                                gens.pop(k_); weights.pop(k_)
                                break

            nq = NQT if NSTOP > 1 else 0
            if nq:
                run_gens([cmp_gen(0)])
            for QT in range(nq):
                gl = [sw_gen(QT)]; wl = [1]
                if QT + 1 < nq:
                    gl.append(cmp_gen(QT + 1)); wl.append(CMPR)
                run_gens(gl, wl)
        b.pop()

    fg_d = b.din("final_g", [1, D])
    if "FIN2" in phases:
        b.mk.label = "FIN2"
        b.push()
        Wo2 = b.sb([128, 8, D], BF16)
        for kc in range(8):
            b.dmac(Wo2[:, kc, :], boutw_d.t.ap()[kc * 128:(kc + 1) * 128, :], [boutw_d], [Wo2])
        fgB = b.sb([128, D])
        b.dma(fgB[:], fg_d.t.ap().to_broadcast([128, D]), [fg_d], [fgB])
        y2r = b.rot(2, [128, D], BF16); ytr2 = b.rot(2, [128, 8, 128], BF16); x1r2 = b.rot(2, [128, D]); x2r = b.rot(2, [128, D])
        otr2 = b.rot(2, [128, D]); st = b.rot(4, [128, 4]); jr4 = b.rot(1, [128, D], BF16)
        for ti in range(NQT * 4):
            rows = slice(ti * 128, (ti + 1) * 128)
            y2 = y2r.next(); yt = ytr2.next(); x1t = x1r2.next(); x2 = x2r.next(); ot = otr2.next(); s = st.next(); jk = jr4.next()
            b.dma(y2[:], Y2.t.ap()[rows, :], [Y2], [y2])
            b.dma(x1t[:], X1.t.ap()[rows, :], [X1], [x1t])
            for kc in range(8):
                pbt = pb[6 + (kc % 2)]
                pv = pbt[:, 0:64].bitcast(BF16)
                b.tr(pv, y2[:, kc * 128:(kc + 1) * 128], identb[:], [y2, identb], [pbt])
                b.act(yt[:, kc, :], pv, AF.Copy, [pbt], [yt])
            for n in range(2):
                for kc in range(8):
                    b.mm(pb[n][:], yt[:, kc, :], Wo2[:, kc, n * 512:(n + 1) * 512], kc == 0, kc == 7, [yt, Wo2], [pb[n]])
                b.tt(x2[:, n * 512:(n + 1) * 512], pb[n][:], gateB[1][:, n * 512:(n + 1) * 512], ALU.mult, [pb[n], gateB[1]], [x2])
            b.tt(x2[:], x2[:], x1t[:], ALU.add, [x2, x1t], [x2], "gpsimd")
            if "X2" in b.dbg:
                b.dma(X2.t.ap()[rows, :], x2[:], [x2], [X2], "gpsimd")
            b.act(jk[:], x2[:], AF.Square, [x2], [jk, s], scale=1.0 / 32.0, accum=s[:, 0:1])
            b.act(s[:, 1:2], s[:, 0:1], AF.Ln, [s], [s], bias=1e-6)
            b.act(s[:, 2:3], s[:, 1:2], AF.Exp, [s], [s], scale=-0.5)
            b.stt(ot[:], x2[:], s[:, 2:3], fgB[:], ALU.mult, ALU.mult, [x2, s, fgB], [ot])
            b.dma(out_d.t.ap()[rows, :], ot[:], [ot], [], "gpsimd")
        b.pop()

    if "FIN" in phases:
        b.push()
        fgB = b.sb([128, D])
        b.dma(fgB[:], fg_d.t.ap().to_broadcast([128, D]), [fg_d], [fgB])
        src = X1 if "C" in phases else x_d
        xr = b.rot(2, [128, D]); orr2 = b.rot(2, [128, D]); st = b.rot(4, [128, 4]); jr3 = b.rot(1, [128, D], BF16)
        for ti in range(NT):
            rows = slice(ti * 128, (ti + 1) * 128)
            xt = xr.next(); ot = orr2.next(); s = st.next(); jk = jr3.next()
            b.dma(xt[:], src.t.ap()[rows, :], [src], [xt])
            b.act(jk[:], xt[:], AF.Square, [xt], [jk, s], scale=1.0 / 32.0, accum=s[:, 0:1])
            b.act(s[:, 1:2], s[:, 0:1], AF.Ln, [s], [s], bias=1e-6)
            b.act(s[:, 2:3], s[:, 1:2], AF.Exp, [s], [s], scale=-0.5)
            b.stt(ot[:], xt[:], s[:, 2:3], fgB[:], ALU.mult, ALU.mult, [xt, s, fgB], [ot])
            b.dma(out_d.t.ap()[rows, :], ot[:], [ot], [], "gpsimd")
        b.pop()

    b.mk.finish("sync")
    b.mk.emit()
    return b


def t5_bucket_np(d):
    n = np.maximum(d, 0)
    nf = np.maximum(n, 1).astype(np.float32)
    large = 16 + (np.log(nf / np.float32(16.0)) / np.float32(math.log(8.0)) * np.float32(16.0)).astype(np.int32)
    large = np.minimum(large, 31)
    return np.where(n < 16, n, large)


def nsa_consts():
    dd = np.arange(1536) - 512
    bk = t5_bucket_np(dd)
    oh = np.zeros((33, 1536), np.float32)
    for m in range(1536):
        if dd[m] < 0:
            oh[32, m] = 1.0
        else:
            oh[bk[m], m] = 1.0
    n_cmp = 255
    cells = np.arange(n_cmp)[:, None] + np.arange(2)[None, :]
    ov = (cells[:, None, :] // 4 == np.arange(64)[None, :, None]).sum(-1).astype(np.float32)
    ovp = np.zeros((256, 64), np.float32); ovp[:255] = ov
    ovl = ovp.reshape(2, 128, 64).transpose(1, 0, 2).copy()
    qi = np.arange(128)[:, None]; j = np.arange(126)[None, :]
    sp = j - 62
    curp = (qi >= 64).astype(np.int64)
    fm = np.where((sp == curp) | (sp == curp - 1), 1.0e9, np.where(sp > curp, -1.0e9, 0.0)).astype(np.float32)
    return oh, ovl, fm


def make_consts():
    i = np.arange(128)
    same = (i[:, None] // 64) == (i[None, :] // 64)
    U2 = (same & (i[:, None] <= i[None, :])).astype(np.float32)
    BON = same.astype(np.float32)
    SELA = np.repeat((i < 64).astype(np.float32)[:, None], 128, 1)
    SELB = np.repeat((i >= 64).astype(np.float32)[:, None], 128, 1)
    NMS = np.where(same & (i[:, None] > i[None, :]), 0.0, BIG).astype(np.float32)
    NMT = np.where(same & (i[None, :] >= i[:, None]), 0.0, -BIG).astype(np.float32)
    return np.concatenate([U2, BON, SELA, SELB, NMS, NMT], axis=1)


def core_inputs(inp, bi):
    f = np.ascontiguousarray
    d = {
        "x": f(inp["x"][bi]),
        "c_l": f(inp["c"][bi].reshape(8, 128).T),
        "rel_bias": f(inp["rel_bias"]),
        "ada_w": f(inp["ada_w"]),
        "ada_b": f(inp["ada_b"]),
        "norm_g_l": f(inp["norm_g"].reshape(2, 8, 128).transpose(2, 0, 1)),
        "a_in_w": f(inp["a_in_w"][0]),
        "conv_w_l": f(inp["a_conv_w"][0].reshape(4, 24, 128).transpose(2, 1, 0)),
        "a_A_log": f(inp["a_A_log"]),
        "a_dt_bias": f(inp["a_dt_bias"]),
        "a_onorm_g": f(inp["a_onorm_g"]),
        "a_out_w": f(inp["a_out_w"][0]),
        "consts": make_consts(),
        "kvg_l": f(inp["kv_norm_g"].reshape(8, 128).T),
        "final_g": f(inp["final_g"].reshape(1, D)),
        "kv_w": f(inp["kv_w"]),
        "posT": f(np.stack([inp["cmp_pos_k"].T, inp["cmp_pos_v"].T], 0)),
        "cmp_k_w1": f(inp["cmp_k_w1"]), "cmp_v_w1": f(inp["cmp_v_w1"]),
        "cmp_k_w2": f(inp["cmp_k_w2"]), "cmp_v_w2": f(inp["cmp_v_w2"]),
        "b_in_w": f(inp["b_in_w"][0]), "b_out_w": f(inp["b_out_w"][0]),
    }
    oh, ovl, fm = nsa_consts()
    d["oh_const"] = oh; d["ov_const"] = ovl; d["fmw_const"] = fm
    return d


_CACHE = {}
PHASES = ("mod", "A", "B", "C", "KV", "Z", "NSA", "FIN2")


def kernel(**inputs):
    inp = {k: np.asarray(v) for k, v in inputs.items()}
    if "nc" not in _CACHE:
        _CACHE["nc"] = build(phases=PHASES).nc
    nc = _CACHE["nc"]
    in_maps = [core_inputs(inp, bi) for bi in range(8)]
    res = run_bass_kernel_spmd(nc, in_maps, core_ids=list(range(8)))
    return np.stack([np.asarray(r["out"]).reshape(T, D) for r in res.results], 0).astype(np.float32)
```
